# Optimizing a Trainium2 kernel written in Bass

```python
import math
import jax, jax.numpy as jnp
from jax import lax
import numpy as np

D_MODEL = 2048
BATCH = 4
SEQ = 2048
DEPTH = 2

N_EVEN = (DEPTH + 1) // 2
N_ODD = DEPTH // 2
EPS = 1e-6

A_HEADS = 8
A_QK_DIM = 64
A_V_DIM = 2 * A_QK_DIM
A_QK_WIDTH = A_HEADS * 2 * A_QK_DIM
A_WIDTH = A_HEADS * A_V_DIM
ROPE_THETA = 500000.0
ROPE_DIM = A_QK_DIM // 4
Q_BLOCK = 128

B_GROUPS = 8
B_GROUP_DIM = 128
B_WIDTH = B_GROUPS * B_GROUP_DIM
B_CHUNK = 128

EVEN_IN = 2 * A_QK_WIDTH + A_WIDTH + 2 * B_WIDTH
EVEN_OUT = A_WIDTH + B_WIDTH

C_EXPAND = 128
C_HEADS = D_MODEL // C_EXPAND
C_DK = C_EXPAND
C_DV = D_MODEL // C_HEADS
C_WIDTH = C_HEADS * C_DK
C_CHUNK = 64
ODD_IN = 5 * C_WIDTH

FFN_HIDDEN = ((8 * D_MODEL // 3 + 255) // 256) * 256

kernel_name = "hybrid_diffattn_gmlp_hgrn2_encoder"


def rms_norm(x, g):
    xf = x.astype(jnp.float32)
    y = xf * lax.rsqrt(jnp.mean(xf * xf, axis=-1, keepdims=True) + EPS)
    return (y * g.astype(jnp.float32)).astype(x.dtype)


def layer_norm(x, g, b):
    xf = x.astype(jnp.float32)
    mu = jnp.mean(xf, axis=-1, keepdims=True)
    xc = xf - mu
    var = jnp.mean(xc * xc, axis=-1, keepdims=True)
    y = xc * lax.rsqrt(var + EPS) * g.astype(jnp.float32) + b.astype(jnp.float32)
    return y.astype(x.dtype)


def partial_rope(x, pos):
    half = ROPE_DIM // 2
    inv_freq = ROPE_THETA ** (-jnp.arange(half, dtype=jnp.float32) / half)
    ang = pos[:, None] * inv_freq[None, :]
    cos = jnp.cos(ang).astype(x.dtype)
    sin = jnp.sin(ang).astype(x.dtype)
    x1 = x[..., :half]
    x2 = x[..., half:ROPE_DIM]
    rest = x[..., ROPE_DIM:]
    return jnp.concatenate([x1 * cos - x2 * sin, x2 * cos + x1 * sin, rest], axis=-1)


def diff_attention(q, k, v, lam, pos):
    B, S = q.shape[0], q.shape[1]
    q = partial_rope(jnp.einsum('bshcd->bhcsd', q), pos)
    k = partial_rope(jnp.einsum('bshcd->bhcsd', k), pos)
    v = jnp.einsum('bshd->bhsd', v)
    scale = A_QK_DIM ** -0.5
    nb = S // Q_BLOCK
    qb = jnp.moveaxis(q.reshape(B, A_HEADS, 2, nb, Q_BLOCK, A_QK_DIM), 3, 0)

    def block(qi):
        s = jnp.einsum('bhcqd,bhckd->bhcqk', qi, k).astype(jnp.float32) * scale
        p = jax.nn.softmax(s, axis=-1)
        a = p[:, :, 0] - lam * p[:, :, 1]
        return jnp.einsum('bhqk,bhkd->bhqd', a.astype(v.dtype), v)

    o = lax.map(block, qb)
    return jnp.transpose(o, (1, 0, 3, 2, 4)).reshape(B, S, A_HEADS, A_V_DIM)


def even_mixer(h, w_in, w_out, lq1, lk1, lq2, lk2, subln, ln_g, ln_b, w_s, b_s,
               layer_idx, pos):
    B, S, _ = h.shape
    proj = h @ w_in
    s1 = A_QK_WIDTH
    s2 = 2 * A_QK_WIDTH
    s3 = s2 + A_WIDTH
    s4 = s3 + B_WIDTH
    q, k, va, u, vb = jnp.split(proj, [s1, s2, s3, s4], axis=-1)

    q = q.reshape(B, S, A_HEADS, 2, A_QK_DIM)
    k = k.reshape(B, S, A_HEADS, 2, A_QK_DIM)
    va = va.reshape(B, S, A_HEADS, A_V_DIM)
    lam_init = 0.8 - 0.6 * math.exp(-0.3 * layer_idx)
    lam = (jnp.exp(jnp.sum(lq1.astype(jnp.float32) * lk1.astype(jnp.float32)))
           - jnp.exp(jnp.sum(lq2.astype(jnp.float32) * lk2.astype(jnp.float32)))
           + lam_init)
    oa = diff_attention(q, k, va, lam, pos)
    oa = (rms_norm(oa, subln) * (1.0 - lam_init)).reshape(B, S, A_WIDTH)

    u = jax.nn.gelu(u)
    vb = layer_norm(jax.nn.gelu(vb), ln_g, ln_b)
    nc = S // B_CHUNK
    vb = vb.reshape(B, nc, B_CHUNK, B_GROUPS, B_GROUP_DIM)
    sv = jnp.einsum('gpq,bnqgc->bnpgc', w_s, vb) + jnp.transpose(b_s)[None, None, :, :, None]
    ob = u * sv.reshape(B, S, B_WIDTH)

    return jnp.concatenate([oa, ob], axis=-1) @ w_out


def hgrn2_scan(q, k, v, logf):
    D2, B, H, S, dk = q.shape
    dv = v.shape[-1]
    nc = S // C_CHUNK

    def to_chunks(t):
        return jnp.moveaxis(t.reshape(D2, B, H, nc, C_CHUNK, t.shape[-1]), 3, 0)

    xs = (to_chunks(q), to_chunks(k), to_chunks(v), to_chunks(logf))
    mask = jnp.tril(jnp.ones((C_CHUNK, C_CHUNK), dtype=bool))[:, :, None]

    def step(state, inp):
        qi, ki, vi, gi = inp
        b = jnp.cumsum(gi, axis=-2)
        diff = b[..., :, None, :] - b[..., None, :, :]
        decay = jnp.exp(jnp.where(mask, diff, -jnp.inf))
        scores = jnp.einsum('...tk,...tsk->...ts', qi, decay * ki[..., None, :, :])
        o_intra = jnp.einsum('...ts,...sv->...tv', scores, vi)
        o_inter = jnp.einsum('...tk,...kv->...tv', qi * jnp.exp(b), state)
        b_last = b[..., -1:, :]
        k_dec = ki * jnp.exp(b_last - b)
        new_state = (state * jnp.exp(b_last)[..., 0, :, None]
                     + jnp.einsum('...sk,...sv->...kv', k_dec, vi))
        return new_state, o_intra + o_inter

    state0 = jnp.zeros((D2, B, H, dk, dv), jnp.float32)
    _, o = lax.scan(step, state0, xs)
    return jnp.moveaxis(o, 0, 3).reshape(D2, B, H, S, dv)


def odd_mixer(h, w_in, w_out, lower_bounds, g_norm, layer_idx):
    B, S, _ = h.shape
    proj = h @ w_in
    q, f_fwd, f_bwd, i, g = jnp.split(proj, 5, axis=-1)
    lbs = jax.nn.softmax(lower_bounds.astype(jnp.float32), axis=1)
    lb = (jnp.cumsum(lbs, axis=1) - lbs[:, :1])[:, layer_idx]
    lb = lb[:, None, None, :]
    f = lb + (1.0 - lb) * jax.nn.sigmoid(jnp.stack([f_fwd, f_bwd], 0).astype(jnp.float32))
    k = 1.0 - f
    logf = jnp.log(f)
    q = jax.nn.silu(q).astype(jnp.float32)
    i = i.astype(jnp.float32)

    def bidir(fwd, bwd):
        return jnp.stack([fwd, jnp.flip(bwd, axis=1)], axis=0)

    def heads(t, d):
        return jnp.transpose(t.reshape(2, B, S, C_HEADS, d), (0, 1, 3, 2, 4))

    o = hgrn2_scan(heads(bidir(q, q), C_DK), heads(bidir(k[0], k[1]), C_DK),
                   heads(bidir(i, i), C_DV), heads(bidir(logf[0], logf[1]), C_DK))
    o = o[0] + jnp.flip(o[1], axis=2)
    o = jnp.transpose(o, (0, 2, 1, 3)).reshape(B, S, C_HEADS * C_DV).astype(h.dtype)
    o = rms_norm(o, g_norm) * jax.nn.silu(g)
    return o @ w_out


def swiglu(h, w_gate, w_up, w_down):
    return (jax.nn.silu(h @ w_gate) * (h @ w_up)) @ w_down


def setup_inputs(seed: int = 0) -> dict:
    key = jax.random.key(seed)
    ks = jax.random.split(key, 24)
    f32 = jnp.float32

    def nrm(k, shape, scale):
        return jax.random.normal(k, shape, f32) * scale

    def gain(k, shape):
        return 1.0 + 0.02 * jax.random.normal(k, shape, f32)

    return {
        "x": jax.random.normal(ks[0], (BATCH, SEQ, D_MODEL), f32),
        "mix_norm": gain(ks[1], (DEPTH, D_MODEL)),
        "even_w_in": nrm(ks[2], (N_EVEN, D_MODEL, EVEN_IN), D_MODEL ** -0.5),
        "even_w_out": nrm(ks[3], (N_EVEN, EVEN_OUT, D_MODEL), EVEN_OUT ** -0.5),
        "diff_lq1": nrm(ks[4], (N_EVEN, A_QK_DIM), 0.1),
        "diff_lk1": nrm(ks[5], (N_EVEN, A_QK_DIM), 0.1),
        "diff_lq2": nrm(ks[6], (N_EVEN, A_QK_DIM), 0.1),
        "diff_lk2": nrm(ks[7], (N_EVEN, A_QK_DIM), 0.1),
        "diff_subln": gain(ks[8], (N_EVEN, A_V_DIM)),
        "gmlp_ln_g": gain(ks[9], (N_EVEN, B_WIDTH)),
        "gmlp_ln_b": nrm(ks[10], (N_EVEN, B_WIDTH), 0.02),
        "gmlp_w_s": nrm(ks[11], (N_EVEN, B_GROUPS, B_CHUNK, B_CHUNK), B_CHUNK ** -0.5),
        "gmlp_b_s": gain(ks[12], (N_EVEN, B_GROUPS, B_CHUNK)),
        "hgrn_w_in": nrm(ks[13], (N_ODD, D_MODEL, ODD_IN), D_MODEL ** -0.5),
        "hgrn_w_out": nrm(ks[14], (N_ODD, C_WIDTH, D_MODEL), C_WIDTH ** -0.5),
        "hgrn_lower_bounds": nrm(ks[15], (2, DEPTH, C_WIDTH), 0.1),
        "hgrn_g_norm": gain(ks[16], (N_ODD, C_WIDTH)),
        "ffn_norm": gain(ks[17], (DEPTH, D_MODEL)),
        "ffn_w_gate": nrm(ks[18], (DEPTH, D_MODEL, FFN_HIDDEN), D_MODEL ** -0.5),
        "ffn_w_up": nrm(ks[19], (DEPTH, D_MODEL, FFN_HIDDEN), D_MODEL ** -0.5),
        "ffn_w_down": nrm(ks[20], (DEPTH, FFN_HIDDEN, D_MODEL), FFN_HIDDEN ** -0.5),
        "final_norm": gain(ks[21], (D_MODEL,)),
    }


def reference(x, mix_norm, even_w_in, even_w_out, diff_lq1, diff_lk1, diff_lq2,
              diff_lk2, diff_subln, gmlp_ln_g, gmlp_ln_b, gmlp_w_s, gmlp_b_s,
              hgrn_w_in, hgrn_w_out, hgrn_lower_bounds, hgrn_g_norm, ffn_norm,
              ffn_w_gate, ffn_w_up, ffn_w_down, final_norm):
    S = x.shape[1]
    pos = jnp.arange(S, dtype=jnp.float32)
    h = x
    for l in range(DEPTH):
        hn = rms_norm(h, mix_norm[l])
        if l % 2 == 0:
            e = l // 2
            h = h + even_mixer(hn, even_w_in[e], even_w_out[e], diff_lq1[e], diff_lk1[e],
                               diff_lq2[e], diff_lk2[e], diff_subln[e], gmlp_ln_g[e],
                               gmlp_ln_b[e], gmlp_w_s[e], gmlp_b_s[e], l, pos)
        else:
            o = l // 2
            h = h + odd_mixer(hn, hgrn_w_in[o], hgrn_w_out[o], hgrn_lower_bounds,
                              hgrn_g_norm[o], l)
        hn = rms_norm(h, ffn_norm[l])
        h = h + swiglu(hn, ffn_w_gate[l], ffn_w_up[l], ffn_w_down[l])
    return rms_norm(h, final_norm)
```

```python
import contextlib
import numpy as np
import concourse.bass as bass
import concourse.mybir as mybir
from concourse.bass_utils import run_bass_kernel_spmd

F32 = mybir.dt.float32
BF16 = mybir.dt.bfloat16
AF = mybir.ActivationFunctionType
ALU = mybir.AluOpType
AX = mybir.AxisListType

ENGS = ("pe", "act", "dve", "pool", "sp")
DEBUG_TAGS = False
INS_TAGS = {}
D = 2048
KC = 16
T = 1024
TA = 2048
FH = 5632
EPS = 1e-6
NCORES = 8


class Op:
    __slots__ = ("eng", "fn", "deps", "signal", "count", "dma_key", "dma_cum", "dma_inc", "idx", "tag")

    def __init__(self, eng, fn, dma_key, dma_inc):
        self.eng = eng
        self.fn = fn
        self.deps = []
        self.signal = False
        self.count = 0
        self.dma_key = dma_key
        self.dma_cum = 0
        self.dma_inc = dma_inc
        self.idx = 0


class Prog:
    def __init__(self):
        self.ops = []
        self.last_w = {}
        self.readers = {}
        self.dma_cnt = {}
        self.last_on = {}
        self.bar = None
        self.bar_seen = set()

    def op(self, eng, fn, reads=(), writes=(), dma_key=None, dma_inc=16):
        o = Op(eng, fn, dma_key, dma_inc)
        o.idx = len(self.ops)
        if DEBUG_TAGS:
            import sys as _s
            f = _s._getframe(1)
            o.tag = f"{f.f_lineno}<{f.f_back.f_lineno}<{f.f_back.f_back.f_lineno if f.f_back.f_back else 0} w={list(writes)[:3]}"
        deps = set()
        for r in reads:
            w = self.last_w.get(r)
            if w is not None:
                deps.add(w)
            if r.startswith("ps"):
                for rd in self.readers.get(r, ()):
                    if rd.eng != eng:
                        deps.add(rd)
        for wkey in writes:
            w = self.last_w.get(wkey)
            if w is not None:
                deps.add(w)
            for rd in self.readers.get(wkey, ()):
                deps.add(rd)
        if dma_key is not None:
            prev = self.last_on.get("dma:" + dma_key)
            if prev is not None:
                deps.add(prev)
        if self.bar is not None and eng not in self.bar_seen:
            deps.add(self.bar)
            self.bar_seen.add(eng)
        for d in deps:
            if d.dma_key is None and d.eng == "pe" and eng == "pe" and dma_key is None:
                continue
            o.deps.append(d)
            if d.dma_key is None:
                d.signal = True
        for r in reads:
            self.readers.setdefault(r, []).append(o)
        for wkey in writes:
            self.last_w[wkey] = o
            self.readers[wkey] = []
        if dma_key is not None:
            self.dma_cnt[dma_key] = self.dma_cnt.get(dma_key, 0) + dma_inc
            o.dma_cum = self.dma_cnt[dma_key]
            self.last_on["dma:" + dma_key] = o
        else:
            self.last_on[eng] = o
        self.ops.append(o)
        return o

    def barrier(self, nopfn):
        o = Op("dve", nopfn, None, 16)
        o.idx = len(self.ops)
        for k, d in self.last_on.items():
            if d is None:
                continue
            if d.dma_key is None and d.eng == "dve":
                continue
            o.deps.append(d)
            if d.dma_key is None:
                d.signal = True
        self.last_on["dve"] = o
        self.ops.append(o)
        self.bar = o
        self.bar_seen = {"dve"}
        self.last_w = {}
        self.readers = {}
        return o

    def emit(self, nc, final_dma_keys=()):
        cnt = {e: 0 for e in ENGS}
        for o in self.ops:
            if o.dma_key is None and o.signal:
                cnt[o.eng] += 1
                o.count = cnt[o.eng]
        per_eng = {e: [] for e in ENGS}
        for o in self.ops:
            per_eng[o.eng].append(o)
        dma_keys = sorted(self.dma_cnt.keys())
        with contextlib.ExitStack() as st:
            sems = {}
            for e in ENGS:
                sems[e] = st.enter_context(nc.semaphore("s_" + e))
            for k in dma_keys:
                sems["dma_" + k] = st.enter_context(nc.semaphore("d_" + k))
            block = st.enter_context(nc.Block())

            def run(engname, engobj):
                waited = {}
                for o in per_eng[engname]:
                    need = {}
                    for d in o.deps:
                        if d.dma_key is not None:
                            s, v = "dma_" + d.dma_key, d.dma_cum
                        else:
                            s, v = d.eng, d.count
                        if v > need.get(s, 0):
                            need[s] = v
                    for s, v in need.items():
                        if waited.get(s, 0) >= v:
                            continue
                        engobj.wait_ge(sems[s], v)
                        waited[s] = v
                    ins = o.fn(engobj)
                    if DEBUG_TAGS:
                        try:
                            INS_TAGS[ins.ins.name] = o.tag
                        except Exception:
                            pass
                    if o.dma_key is not None:
                        ins.then_inc(sems["dma_" + o.dma_key], o.dma_inc)
                    elif o.signal:
                        ins.then_inc(sems[o.eng], 1)
                if engname == "sp":
                    for k in final_dma_keys:
                        engobj.wait_ge(sems["dma_" + k], self.dma_cnt[k])

            @block.tensor
            def _(e):
                run("pe", e)

            @block.scalar
            def _(e):
                run("act", e)

            @block.vector
            def _(e):
                run("dve", e)

            @block.gpsimd
            def _(e):
                run("pool", e)

            @block.sync
            def _(e):
                run("sp", e)


def slabify(W, ncols):
    K, N = W.shape
    return np.ascontiguousarray(W.reshape(K // 128, 128, N // ncols, ncols).transpose(2, 1, 0, 3))


def fm_vec(v):
    return np.ascontiguousarray(v.reshape(-1, 128).T)


V_MIXG0, V_FFNG0, V_MIXG1, V_FFNG1, V_FING = 0, 16, 32, 48, 64
V_INVF, V_SGN, V_SUBLN, V_SEL0, V_SEL1 = 80, 81, 82, 83, 84
V_LB = 88
V_GNORM = 152
NV = 168
R_LNG, R_LNB, R_BSB, R_LQ = 0, 1024, 2048, 3072
NR = 3072 + 256


class Ctx:
    pass


def build(stage="full"):
    nc = bass.Bass("TRN2", target_bir_lowering=False)
    P = Prog()
    c = Ctx()
    c.nc, c.P = nc, P
    dt_in = lambda name, shape: nc.dram_tensor(name, list(shape), F32, kind="ExternalInput").ap()
    c.xT = dt_in("xT", [128, KC, TA])
    c.pos = dt_in("pos", [1, TA])
    c.vecs_d = dt_in("vecs", [128, NV])
    c.rows_d = dt_in("rows", [1, NR])
    c.consts_d = dt_in("consts", [128, 4, 128])
    c.w_q = dt_in("w_q", [8, 128, KC, 128])
    c.w_k = dt_in("w_k", [8, 128, KC, 128])
    c.w_v = dt_in("w_v", [2, 128, KC, 512])
    c.w_u = dt_in("w_u", [8, 128, KC, 128])
    c.w_vb = dt_in("w_vb", [2, 128, KC, 512])
    c.w_s = dt_in("w_s", [128, 8, 128])
    c.w_eo = dt_in("w_eo", [16, 128, KC, 128])
    if stage in ("full", "l0", "ffn0"):
        c.w_g = dt_in("w_g", [2, 44, 128, KC, 128])
        c.w_up = dt_in("w_up", [2, 44, 128, KC, 128])
        c.w_dn = dt_in("w_dn", [2, 2, 16, 128, 22, 128])
    if stage in ("full", "l1"):
        c.w_hin = dt_in("w_hin", [5, 16, 128, KC, 128])
        c.w_ho = dt_in("w_ho", [16, 128, KC, 128])
    c.yT = nc.dram_tensor("yT", [128, KC, T], F32, kind="ExternalOutput").ap()
    c.cin = nc.dram_tensor("cin", [16 * 128, 128], F32).ap()
    c.cout = nc.dram_tensor("cout", [2 * 16 * 128, 128], F32).ap()

    ARENA_KB = 204
    arena = nc.alloc_sbuf_tensor("arena", [128, ARENA_KB * 512], BF16).ap()

    def sb(off_kb, nbytes, dtype, pattern=None, **kw):
        a = int(round(off_kb * 512))
        n = nbytes // 2
        assert a + n <= ARENA_KB * 512, (off_kb, nbytes)
        v = arena[:, a:a + n]
        if dtype == F32:
            v = v.bitcast(F32)
        if pattern:
            v = v.rearrange(pattern, **kw)
        return v

    c.sb = sb
    c.ps = [nc.alloc_psum_tensor(f"ps{i}", [128, 512], F32).ap() for i in range(8)]
    c.h = sb(0, 64 * 1024, F32, "p (k t) -> p k t", k=KC)
    c.hn = sb(64, 32 * 1024, BF16, "p (k t) -> p k t", k=KC)
    c.vecs = sb(144, NV * 4, F32)
    c.ident = sb(144.75, 256, BF16)
    c.ones = sb(145.0, 256, BF16)
    c.pm = sb(145.25, 256, BF16)
    c.tri = sb(145.5, 256, BF16)
    c.triT = sb(145.75, 256, BF16)
    c.lam = sb(146.0, 16, F32)
    c.epsb = sb(146.0625, 16, F32)
    c.cst32 = sb(200, 4 * 512, F32, "p (a b) -> p a b", a=4)
    c.wslot = [sb(160 + 4 * i, 4096, BF16, "p (k n) -> p k n", k=KC) for i in range(8)]
    c.wbig = [sb(160 + 16 * i, 16384, BF16, "p (k n) -> p k n", k=KC) for i in range(2)]
    c.wcnt = 0
    c.bigcnt = 0

    load_consts(c)
    c.stage = stage
    if stage == "l1":
        for kc in range(KC):
            P.op("sp", lambda e, kc=kc: e.dma_start(out=c.h[:, kc, :], in_=c.xT[:, kc, 0:T]), writes=[f"h{kc}.0", f"h{kc}.1"], dma_key=f"x{kc}")
        layer1_mixer(c)
    elif stage == "ffn0":
        for kc in range(KC):
            P.op("sp", lambda e, kc=kc: e.dma_start(out=c.h[:, kc, :], in_=c.xT[:, kc, 0:T]), writes=[f"h{kc}.0", f"h{kc}.1"], dma_key=f"x{kc}")
        ffn(c, 0)
    elif stage in ("full", "l0") or stage.startswith("l0"):
        layer0_mixer(c)
        if stage in ("full", "l0"):
            ffn(c, 0)
    if stage in ("full",):
        layer1_mixer(c)
        ffn(c, 1)
        final_norm(c)
    else:
        P.barrier(lambda e: e.memset(c.lam[:, 2:3], 0.0))
        for kc in range(KC):
            P.op("sp", lambda e, kc=kc: e.dma_start(out=c.yT[:, kc, :], in_=c.h[:, kc, :]),
                 reads=[f"h{kc}.0", f"h{kc}.1"], writes=[f"y{kc}"], dma_key=f"out{kc % 8}")
    P.emit(nc, final_dma_keys=[f"out{i}" for i in range(8)])
    return nc


def wload(c, dram_slab, big=False, nk=KC):
    P = c.P
    if big:
        i = c.bigcnt % 2
        c.bigcnt += 1
        ap = c.wbig[i]
        keys = [f"ws{4 * i + j}" for j in range(4)]
        dk = f"wb{i}"
    else:
        i = c.wcnt % 8
        c.wcnt += 1
        ap = c.wslot[i]
        keys = [f"ws{i}"]
        dk = f"w{i}"
    dst = ap if nk == KC else ap[:, 0:nk, :]
    P.op("pool", lambda e: e.dma_start(out=dst, in_=dram_slab), writes=keys, dma_key=dk)
    return ap, keys


def load_consts(c):
    P, nc = c.P, c.nc
    P.op("dve", lambda e: e.memset(c.epsb[:, 0:1], EPS), writes=["epsb"])
    P.op("dve", lambda e: e.memset(c.epsb[:, 1:2], EPS / 0.64), reads=["epsb"], writes=["epsb"])
    P.op("sp", lambda e: e.dma_start(out=c.vecs, in_=c.vecs_d), writes=["vecs"], dma_key="c")
    P.op("sp", lambda e: e.dma_start(out=c.cst32, in_=c.consts_d), writes=["cst32"], dma_key="c")
    for i, (ap, nm) in enumerate([(c.ident, "ident"), (c.ones, "ones"), (c.pm, "pm"), (c.tri, "tri")]):
        P.op("dve", lambda e, ap=ap, i=i: e.tensor_copy(ap, c.cst32[:, i, :]), reads=["cst32"], writes=[nm])
    psb = c.ps[7].bitcast(BF16)
    P.op("pe", lambda e: e.transpose(psb[:, 0:128], c.tri, c.ident), reads=["tri", "ident"], writes=["ps7"])
    P.op("dve", lambda e: e.tensor_copy(c.triT, psb[:, 0:128]), reads=["ps7"], writes=["triT"])


def rms_to_hn(c, src_fn, src_keys_fn, gcol, ntc, dst, dst_key_fn, tmp_off):
    P = c.P
    sq = [c.sb(tmp_off + i, 1024, BF16) for i in range(2)]
    rstd = c.sb(tmp_off + 2, 2048, F32)
    for tc in range(ntc):
        ps = c.ps[6 + (tc % 2)]
        psk = f"ps{6 + (tc % 2)}"
        for kc in range(KC):
            s = sq[kc % 2]
            sk = f"sq{kc % 2}"
            P.op("act", lambda e, s=s, kc=kc, tc=tc: e.activation(s, src_fn(kc, tc), AF.Square),
                 reads=src_keys_fn(kc, tc), writes=[sk])
            P.op("pe", lambda e, s=s, kc=kc, ps=ps: e.matmul(ps, c.ones, s, start=(kc == 0), stop=(kc == KC - 1)),
                 reads=[sk, "ones"], writes=[psk])
        P.op("act", lambda e, ps=ps: e.activation(rstd, ps, AF.Sqrt, bias=c.epsb[:, 0:1], scale=1.0 / D), reads=[psk, "epsb"], writes=["rstd"])
        P.op("dve", lambda e: e.reciprocal(rstd, rstd), reads=["rstd"], writes=["rstd"])
        for kc in range(KC):
            P.op("dve", lambda e, kc=kc, tc=tc: e.scalar_tensor_tensor(dst(kc, tc), src_fn(kc, tc), c.vecs[:, gcol + kc:gcol + kc + 1], rstd, ALU.mult, ALU.mult),
                 reads=src_keys_fn(kc, tc) + ["rstd", "vecs"], writes=[dst_key_fn(kc, tc)])


def proj_fm(c, slab, slab_keys, rhs_fn, rhs_keys_fn, ps, psk, nk=KC):
    pairs = [(slab[:, kc, :], rhs_fn(kc)) for kc in range(nk)]
    reads = list(slab_keys)
    for kc in range(nk):
        reads += rhs_keys_fn(kc)

    def fn(e):
        ins = None
        for i, (l, r) in enumerate(pairs):
            ins = e.matmul(ps, l, r, start=(i == 0), stop=(i == nk - 1))
        return ins
    c.P.op("pe", fn, reads=reads, writes=[psk])


def layer0_mixer(c):
    P, nc, sb = c.P, c.nc, c.sb
    hn_oth = sb(96, 32 * 1024, BF16, "p (k t) -> p k t", k=KC)
    cat = hn_oth
    K_all = sb(0, 32 * 1024, BF16, "p (h t) -> p h t", h=8)
    V_all = sb(32, 32 * 1024, BF16, "p (b n) -> p b n", b=16)
    Ctab = sb(128, 8192, F32)
    Stab = sb(136, 8192, F32)
    rows = sb(147, NR * 4, F32)
    tmp_off = 192
    P.op("sp", lambda e: e.dma_start(out=rows, in_=c.rows_d.partition_broadcast(128)), writes=["rows"], dma_key="c")
    posb = sb(160, 8192, F32)
    P.op("sp", lambda e: e.dma_start(out=posb, in_=c.pos.partition_broadcast(128)), writes=["ws0", "ws1"], dma_key="c")
    TWO_PI = 2.0 * np.pi
    C1 = 6.28125
    C2 = TWO_PI - C1
    invf = c.vecs[:, V_INVF:V_INVF + 1]
    ang = sb(168, 8192, F32)
    kf = sb(176, 8192, F32)
    ki = sb(184, 8192, F32).bitcast(mybir.dt.int32)
    PI_IN = 3.1415925

    def make_table(dst, shift, key):
        P.op("dve", lambda e: e.tensor_scalar(ang, posb, invf, shift, ALU.mult, ALU.add), reads=["ws0", "ws1", "vecs"], writes=["ang"])
        P.op("dve", lambda e: e.tensor_scalar(kf, ang, 1.0 / TWO_PI, None, ALU.mult), reads=["ang"], writes=["kf"])
        P.op("dve", lambda e: e.tensor_copy(ki, kf), reads=["kf"], writes=["ki"])
        P.op("dve", lambda e: e.tensor_copy(kf, ki), reads=["ki"], writes=["kf"])
        P.op("dve", lambda e: e.scalar_tensor_tensor(ang, kf, -C1, ang, ALU.mult, ALU.add), reads=["kf", "ang"], writes=["ang"])
        P.op("dve", lambda e: e.scalar_tensor_tensor(ang, kf, -C2, ang, ALU.mult, ALU.add), reads=["kf", "ang"], writes=["ang"])
        P.op("dve", lambda e: e.tensor_scalar(kf, ang, np.pi, -TWO_PI, ALU.is_gt, ALU.mult), reads=["ang"], writes=["kf"])
        P.op("dve", lambda e: e.tensor_tensor(ang, ang, kf, ALU.add), reads=["kf", "ang"], writes=["ang"])
        P.op("dve", lambda e: e.tensor_scalar(kf, ang, -np.pi, TWO_PI, ALU.is_lt, ALU.mult), reads=["ang"], writes=["kf"])
        P.op("dve", lambda e: e.tensor_tensor(ang, ang, kf, ALU.add), reads=["kf", "ang"], writes=["ang"])
        P.op("dve", lambda e: e.tensor_scalar(ang, ang, -PI_IN, PI_IN, ALU.max, ALU.min), reads=["ang"], writes=["ang"])
        P.op("act", lambda e: e.activation(dst, ang, AF.Sin), reads=["ang"], writes=[key])

    make_table(Stab, 0.0, "Stab")
    P.op("dve", lambda e: e.tensor_scalar(Stab, Stab, c.vecs[:, V_SGN:V_SGN + 1], None, ALU.mult), reads=["Stab", "vecs"], writes=["Stab"])
    make_table(Ctab, np.pi / 2, "Ctab")
    lq = rows[:, R_LQ:R_LQ + 256].rearrange("p (a b) -> p a b", a=4)
    lt = sb(tmp_off, 512, F32, "p (a b) -> p a b", a=2)
    P.op("dve", lambda e: e.tensor_tensor(lt[:, 0, :], lq[:, 0, :], lq[:, 1, :], ALU.mult), reads=["rows"], writes=["lt"])
    P.op("dve", lambda e: e.tensor_tensor(lt[:, 1, :], lq[:, 2, :], lq[:, 3, :], ALU.mult), reads=["rows", "lt"], writes=["lt"])
    P.op("dve", lambda e: e.reduce_sum(c.lam[:, 2:4], lt, AX.X), reads=["lt"], writes=["lam"])
    P.op("act", lambda e: e.activation(c.lam[:, 2:4], c.lam[:, 2:4], AF.Exp), reads=["lam"], writes=["lam"])
    P.op("dve", lambda e: e.tensor_tensor(c.lam[:, 0:1], c.lam[:, 2:3], c.lam[:, 3:4], ALU.subtract), reads=["lam"], writes=["lam"])
    P.op("dve", lambda e: e.tensor_scalar(c.lam[:, 1:2], c.lam[:, 0:1], 0.2, -1.0, ALU.add, ALU.mult), reads=["lam"], writes=["lam"])

    stage = c.h
    for half in range(2):
        for kc in range(KC):
            P.op("sp", lambda e, kc=kc, half=half: e.dma_start(out=stage[:, kc, :], in_=c.xT[:, kc, half * T:(half + 1) * T]),
                 writes=[f"h{kc}.0", f"h{kc}.1"], dma_key=f"x{kc}")
        dstT = c.hn if half == 0 else hn_oth
        dkey = "hn" if half == 0 else "ho"
        rms_to_hn(c, lambda kc, tc: stage[:, kc, tc * 512:(tc + 1) * 512], lambda kc, tc: [f"h{kc}.{tc}"], V_MIXG0, 2,
                  lambda kc, tc, dstT=dstT: dstT[:, kc, tc * 512:(tc + 1) * 512], lambda kc, tc, dkey=dkey: f"{dkey}{kc}.{tc}", tmp_off)

    def hn_all(kc, tcc):
        src = c.hn if tcc < 2 else hn_oth
        return src[:, kc, (tcc % 2) * 512:(tcc % 2 + 1) * 512]

    def hn_all_keys(kc, tcc):
        return [f"{'hn' if tcc < 2 else 'ho'}{kc}.{tcc % 2}"]

    if c.stage == "l0A":
        return
    if c.stage == "l0Aw":
        slab, skeys = wload(c, c.w_k[0])
        proj_fm(c, slab, skeys, lambda kc: c.hn[:, kc, 0:512], lambda kc: [f"hn{kc}.0"], c.ps[2], "ps2")
        P.op("dve", lambda e: e.tensor_copy(c.h[:, 0, 0:512], c.ps[2]), reads=["ps2"], writes=["h0.0"])
        return
    P.barrier(lambda e: e.memset(c.lam[:, 2:3], 0.0))
    if c.stage == "l0Abar":
        P.op("act", lambda e: e.copy(c.h[:, 0, 0:512], c.h[:, 1, 0:512]), reads=[], writes=["h0.0"])
        P.op("pe", lambda e: e.matmul(c.ps[2], c.ones, c.hn[:, 0, 0:512], start=True, stop=True), reads=[], writes=["ps2"])
        P.op("dve", lambda e: e.tensor_copy(c.h[:, 2, 0:512], c.ps[2]), reads=["ps2"], writes=["h2.0"])
        return
    evi = [0]
    for cb in range(2 if c.stage not in ("l0B1k", "l0B1kn", "l0B1r1", "l0B1r2") else 0):
        slab, skeys = wload(c, c.w_v[cb], big=True)
        for tb in range(16):
            src = c.hn if tb < 8 else hn_oth
            sk = "hn" if tb < 8 else "ho"
            tcl = (tb % 8) // 4
            bank = tb % 2
            ps, psk = c.ps[bank], f"ps{bank}"
            pairs = [(src[:, kc, (tb % 8) * 128:(tb % 8 + 1) * 128], slab[:, kc, :]) for kc in range(KC)]

            def fn(e, pairs=pairs, ps=ps):
                ins = None
                for i, (l, r) in enumerate(pairs):
                    ins = e.matmul(ps, l, r, start=(i == 0), stop=(i == KC - 1))
                return ins
            P.op("pe", fn, reads=skeys + [f"{sk}{kc}.{tcl}" for kc in range(KC)], writes=[psk])
            dst = V_all[:, tb, cb * 512:(cb + 1) * 512]
            if tb % 2 == 0:
                P.op("act", lambda e, dst=dst, ps=ps: e.copy(dst, ps), reads=[psk], writes=[f"V{tb}.{cb}"])
            else:
                P.op("dve", lambda e, dst=dst, ps=ps: e.tensor_copy(dst, ps), reads=[psk], writes=[f"V{tb}.{cb}"])

    if c.stage == "l0B1v":
        return
    t1 = sb(tmp_off + 4, 2048, F32)
    t2 = sb(tmp_off + 6, 2048, F32)
    q16_b1 = [sb(tmp_off + 8 + i, 1024, BF16) for i in range(2)]
    q16_c = [sb(154 + i, 1024, BF16) for i in range(2)]

    def rope_block(ps_a, psk_a, ps_b, psk_b, tcc, dst, dst_key, i, q16):
        qb = q16[i % 2]
        qk = f"q16{i % 2}"
        if c.stage == "l0B1r1":
            P.op("act", lambda e: e.copy(qb, ps_a), reads=[psk_a], writes=[qk])
            P.op("pe", lambda e: e.matmul(ps_b, c.pm, qb, start=True, stop=True), reads=[qk, "pm"], writes=[psk_b])
            P.op("dve", lambda e: e.tensor_copy(dst, ps_b), reads=[psk_b], writes=[dst_key])
            return
        if c.stage == "l0B1r2":
            P.op("dve", lambda e: e.tensor_tensor(t1, ps_a, Ctab[:, tcc * 512:(tcc + 1) * 512], ALU.mult), reads=[psk_a, "Ctab"], writes=["t1"])
            P.op("dve", lambda e: e.tensor_tensor(t2, ps_a, Stab[:, tcc * 512:(tcc + 1) * 512], ALU.mult), reads=[psk_a, "Stab"], writes=["t2"])
            P.op("dve", lambda e: e.tensor_tensor(dst, t1, t2, ALU.add), reads=["t1", "t2"], writes=[dst_key])
            return
        P.op("act", lambda e: e.copy(qb, ps_a), reads=[psk_a], writes=[qk])
        P.op("pe", lambda e: e.matmul(ps_b, c.pm, qb, start=True, stop=True), reads=[qk, "pm"], writes=[psk_b])
        P.op("dve", lambda e: e.tensor_tensor(t1, ps_a, Ctab[:, tcc * 512:(tcc + 1) * 512], ALU.mult), reads=[psk_a, "Ctab", qk], writes=["t1"])
        P.op("dve", lambda e: e.tensor_tensor(t2, ps_b, Stab[:, tcc * 512:(tcc + 1) * 512], ALU.mult), reads=[psk_b, "Stab"], writes=["t2"])
        P.op("dve", lambda e: e.tensor_tensor(dst, t1, t2, ALU.add), reads=["t1", "t2"], writes=[dst_key])

    ri = 0
    for hd in range(8):
        slab, skeys = wload(c, c.w_k[hd])
        for tcc in range(4):
            ba, bb = 2 + (ri % 2) * 2, 3 + (ri % 2) * 2
            proj_fm(c, slab, skeys, lambda kc, tcc=tcc: hn_all(kc, tcc), lambda kc, tcc=tcc: hn_all_keys(kc, tcc), c.ps[ba], f"ps{ba}")
            if c.stage == "l0B1kn":
                P.op("act", lambda e, ba=ba, hd=hd, tcc=tcc: e.copy(K_all[:, hd, tcc * 512:(tcc + 1) * 512], c.ps[ba]), reads=[f"ps{ba}"], writes=[f"K{hd}.{tcc}"])
            else:
                rope_block(c.ps[ba], f"ps{ba}", c.ps[bb], f"ps{bb}", tcc, K_all[:, hd, tcc * 512:(tcc + 1) * 512], f"K{hd}.{tcc}", ri, q16_b1)
            ri += 1

    if c.stage.startswith("l0B1"):
        return
    P.barrier(lambda e: e.memset(c.lam[:, 2:3], 0.0))
    wsT = sb(tmp_off + 10, 2048, BF16, "p (g n) -> p g n", g=8)
    P.op("pool", lambda e: e.dma_start(out=wsT, in_=c.w_s), writes=["wsT"], dma_key="c2")
    for g in range(8):
        slab, skeys = wload(c, c.w_u[g])
        for tc in range(2):
            bank = 2 + (g * 2 + tc) % 2
            proj_fm(c, slab, skeys, lambda kc, tc=tc: c.hn[:, kc, tc * 512:(tc + 1) * 512], lambda kc, tc=tc: [f"hn{kc}.{tc}"], c.ps[bank], f"ps{bank}")
            P.op("act", lambda e, g=g, tc=tc, bank=bank: e.activation(cat[:, 8 + g, tc * 512:(tc + 1) * 512], c.ps[bank], AF.Gelu),
                 reads=[f"ps{bank}"], writes=[f"cat{8 + g}.{tc}"])
    vbg = sb(96, 16 * 1024, F32, "p (b n) -> p b n", b=4)
    vbn = sb(tmp_off + 1, 2048, BF16)
    sqj = sb(tmp_off + 4, 4096, F32)
    stats = sb(tmp_off, 64, F32)
    lng = rows[:, R_LNG:R_LNG + 1024]
    lnb = rows[:, R_LNB:R_LNB + 1024]
    bsb = rows[:, R_BSB:R_BSB + 1024].rearrange("p (g n) -> p g n", g=8)
    for th in range(2):
        for cb in range(2):
            slab, skeys = wload(c, c.w_vb[cb], big=True)
            for tbl in range(4):
                tb = th * 4 + tbl
                bank = tb % 2
                ps, psk = c.ps[bank], f"ps{bank}"
                pairs = [(c.hn[:, kc, tb * 128:(tb + 1) * 128], slab[:, kc, :]) for kc in range(KC)]

                def fn(e, pairs=pairs, ps=ps):
                    ins = None
                    for i, (l, r) in enumerate(pairs):
                        ins = e.matmul(ps, l, r, start=(i == 0), stop=(i == KC - 1))
                    return ins
                P.op("pe", fn, reads=skeys + [f"hn{kc}.{tb // 4}" for kc in range(KC)], writes=[psk])
                P.op("act", lambda e, tbl=tbl, cb=cb, ps=ps: e.activation(vbg[:, tbl, cb * 512:(cb + 1) * 512], ps, AF.Gelu),
                     reads=[psk], writes=[f"vbg{tbl}.{cb}"])
        for tbl in range(4):
            tb = th * 4 + tbl
            xv = vbg[:, tbl, :]
            rk = [f"vbg{tbl}.0", f"vbg{tbl}.1"]
            P.op("dve", lambda e, xv=xv: e.reduce_sum(stats[:, 0:1], xv, AX.X), reads=rk, writes=["stats"])
            P.op("dve", lambda e: e.tensor_scalar(stats[:, 1:2], stats[:, 0:1], -1.0 / 1024, None, ALU.mult), reads=["stats"], writes=["stats"])
            P.op("dve", lambda e, xv=xv: e.tensor_scalar(xv, xv, stats[:, 1:2], None, ALU.add), reads=rk + ["stats"], writes=rk)
            P.op("dve", lambda e, xv=xv: e.tensor_tensor(sqj, xv, xv, ALU.mult), reads=rk, writes=["sqj"])
            P.op("dve", lambda e: e.reduce_sum(stats[:, 2:3], sqj, AX.X), reads=["sqj"], writes=["stats"])
            P.op("act", lambda e: e.activation(stats[:, 3:4], stats[:, 2:3], AF.Sqrt, bias=c.epsb[:, 0:1], scale=1.0 / 1024), reads=["stats", "epsb"], writes=["stats"])
            P.op("dve", lambda e: e.reciprocal(stats[:, 3:4], stats[:, 3:4]), reads=["stats"], writes=["stats"])
            P.op("dve", lambda e, xv=xv: e.scalar_tensor_tensor(xv, xv, stats[:, 3:4], lng, ALU.mult, ALU.mult), reads=rk + ["stats", "rows"], writes=rk)
            P.op("dve", lambda e, xv=xv: e.tensor_tensor(vbn, xv, lnb, ALU.add), reads=rk + ["rows"], writes=["vbn"])
            for gh in range(2):
                bank = 2 + gh
                ps, psk = c.ps[bank], f"ps{bank}"

                def fn(e, gh=gh, ps=ps):
                    ins = None
                    for gl in range(4):
                        g = gh * 4 + gl
                        ins = e.matmul(ps[:, gl * 128:(gl + 1) * 128], vbn[:, g * 128:(g + 1) * 128], wsT[:, g, :], start=True, stop=True)
                    return ins
                P.op("pe", fn, reads=["vbn", "wsT"], writes=[psk])
                svt = sb(tmp_off + 8, 2048, F32, "p (g n) -> p g n", g=4)
                P.op("dve", lambda e, ps=ps, gh=gh: e.tensor_tensor(svt, ps.rearrange("p (g n) -> p g n", g=4), bsb[:, gh * 4:(gh + 1) * 4, :], ALU.add),
                     reads=[psk, "rows"], writes=["svt"])
                cv = cat[:, 8 + gh * 4:8 + gh * 4 + 4, tb * 128:(tb + 1) * 128]
                ck = [f"cat{8 + gh * 4 + gl}.{tb // 4}" for gl in range(4)]
                P.op("dve", lambda e, cv=cv: e.tensor_tensor(cv, cv, svt, ALU.mult), reads=["svt"] + ck, writes=ck)

    if c.stage == "l0B2":
        return
    P.barrier(lambda e: e.memset(c.lam[:, 2:3], 0.0))
    Et = [[sb(tmp_off + 8 + 2 * m + b, 1024, BF16) for b in range(2)] for m in range(2)]
    qrot = sb(tmp_off + 1, 2048, BF16)
    r0 = sb(tmp_off + 4, 2048, F32)
    r1 = sb(tmp_off + 6, 2048, F32)
    oT = sb(147, 2048, F32)
    a0 = sb(149, 2048, F32)
    sqb = sb(151, 1024, BF16)
    rs2 = sb(152, 2048, F32)
    scale = 64 ** -0.5
    ri = 0
    for hd in range(8):
        slab, skeys = wload(c, c.w_q[hd])
        for tc in range(2):
            proj_fm(c, slab, skeys, lambda kc, tc=tc: c.hn[:, kc, tc * 512:(tc + 1) * 512], lambda kc, tc=tc: [f"hn{kc}.{tc}"], c.ps[6], "ps6")
            rope_block(c.ps[6], "ps6", c.ps[7], "ps7", tc, qrot[:, tc * 512:(tc + 1) * 512], f"qrot{tc}", ri, q16_c)
            ri += 1
        for tc in range(2):
            for j in range(16):
                for m in range(2):
                    ps, psk = c.ps[m], f"ps{m}"
                    P.op("pe", lambda e, m=m, j=j, tc=tc, ps=ps, hd=hd: e.matmul(ps, K_all[64 * m:64 * m + 64, hd, j * 128:(j + 1) * 128],
                                                                       qrot[64 * m:64 * m + 64, tc * 512:(tc + 1) * 512], start=True, stop=True),
                         reads=[f"K{hd}.{j // 4}", f"qrot{tc}"], writes=[psk])
                    E = Et[m][j % 2]
                    ek = f"E{m}.{j % 2}"
                    P.op("act", lambda e, E=E, ps=ps: e.activation(E, ps, AF.Exp, scale=scale), reads=[psk], writes=[ek])
                for m in range(2):
                    E = Et[m][j % 2]
                    ek = f"E{m}.{j % 2}"
                    P.op("pe", lambda e, m=m, j=j, E=E, hd=hd: e.matmul(c.ps[2 + 2 * m], V_all[:, j, hd * 128:(hd + 1) * 128], E, start=(j == 0), stop=(j == 15)),
                         reads=[ek, f"V{j}.{hd // 4}"], writes=[f"ps{2 + 2 * m}"])
                    P.op("pe", lambda e, m=m, j=j, E=E: e.matmul(c.ps[3 + 2 * m], c.ones, E, start=(j == 0), stop=(j == 15)),
                         reads=[ek, "ones"], writes=[f"ps{3 + 2 * m}"])
            P.op("dve", lambda e: e.reciprocal(r0, c.ps[3]), reads=["ps3"], writes=["t1"])
            P.op("dve", lambda e: e.reciprocal(r1, c.ps[5]), reads=["ps5"], writes=["t2"])
            P.op("dve", lambda e: e.tensor_tensor(a0, c.ps[2], r0, ALU.mult), reads=["ps2", "t1"], writes=["a0"])
            P.op("dve", lambda e: e.tensor_tensor(r1, c.ps[4], r1, ALU.mult), reads=["ps4", "t2"], writes=["t2"])
            P.op("dve", lambda e: e.scalar_tensor_tensor(oT, r1, c.lam[:, 1:2], a0, ALU.mult, ALU.add), reads=["t2", "a0", "lam"], writes=["oT"])
            P.op("act", lambda e: e.activation(sqb, oT, AF.Square), reads=["oT"], writes=["sqb"])
            P.op("pe", lambda e: e.matmul(c.ps[6], c.ones, sqb, start=True, stop=True), reads=["sqb", "ones"], writes=["ps6"])
            P.op("act", lambda e: e.activation(rs2, c.ps[6], AF.Sqrt, bias=c.epsb[:, 1:2], scale=1.0 / (128 * 0.64)), reads=["ps6", "epsb"], writes=["rs2"])
            P.op("dve", lambda e: e.reciprocal(rs2, rs2), reads=["rs2"], writes=["rs2"])
            P.op("dve", lambda e, tc=tc, hd=hd: e.scalar_tensor_tensor(cat[:, hd, tc * 512:(tc + 1) * 512], oT, c.vecs[:, V_SUBLN:V_SUBLN + 1], rs2, ALU.mult, ALU.mult),
                 reads=["oT", "rs2", "vecs"], writes=[f"cat{hd}.{tc}"])

    if c.stage == "l0C":
        P.barrier(lambda e: e.memset(c.lam[:, 2:3], 0.0))
        for kc in range(KC):
            P.op("dve", lambda e, kc=kc: e.tensor_copy(c.h[:, kc, :], cat[:, kc, :]), writes=[f"h{kc}.0", f"h{kc}.1"])
        return
    P.barrier(lambda e: e.memset(c.lam[:, 2:3], 0.0))
    for db in range(16):
        slab, skeys = wload(c, c.w_eo[db])
        P.op("sp", lambda e, db=db: e.dma_start(out=c.h[:, db, :], in_=c.xT[:, db, 0:T]), writes=[f"h{db}.0", f"h{db}.1"], dma_key=f"x{db}")
        for tc in range(2):
            bank = (db * 2 + tc) % 4
            proj_fm(c, slab, skeys, lambda kc, tc=tc: cat[:, kc, tc * 512:(tc + 1) * 512], lambda kc, tc=tc: [f"cat{kc}.{tc}"], c.ps[bank], f"ps{bank}")
            hv = c.h[:, db, tc * 512:(tc + 1) * 512]
            P.op("dve", lambda e, hv=hv, bank=bank: e.tensor_tensor(hv, hv, c.ps[bank], ALU.add), reads=[f"ps{bank}", f"h{db}.{tc}"], writes=[f"h{db}.{tc}"])


def ffn(c, l):
    P, sb = c.P, c.sb
    P.barrier(lambda e: e.memset(c.lam[:, 2:3], 0.0))
    tmp_off = 192
    gcol = V_FFNG0 if l == 0 else V_FFNG1
    rms_to_hn(c, lambda kc, tc: c.h[:, kc, tc * 512:(tc + 1) * 512], lambda kc, tc: [f"h{kc}.{tc}"], gcol, 2,
              lambda kc, tc: c.hn[:, kc, tc * 512:(tc + 1) * 512], lambda kc, tc: f"hn{kc}.{tc}", tmp_off)
    act = sb(96, 44 * 1024, BF16, "p (j t) -> p j t", j=22)
    sg = [sb(tmp_off + 4 + 2 * i, 2048, F32) for i in range(4)]
    gi = 0
    for half in range(2):
        for hb in range(22):
            sl_g, kg = wload(c, c.w_g[l, half * 22 + hb])
            sl_u, ku = wload(c, c.w_up[l, half * 22 + hb])
            for tc in range(2):
                bg = (gi % 2) * 4 + tc * 2
                bu = bg + 1
                rf = lambda kc, tc=tc: c.hn[:, kc, tc * 512:(tc + 1) * 512]
                rk = lambda kc, tc=tc: [f"hn{kc}.{tc}"]
                proj_fm(c, sl_g, kg, rf, rk, c.ps[bg], f"ps{bg}")
                proj_fm(c, sl_u, ku, rf, rk, c.ps[bu], f"ps{bu}")
                s = sg[(gi * 2 + tc) % 4]
                sk = f"sg{(gi * 2 + tc) % 4}"
                P.op("act", lambda e, s=s, bg=bg: e.activation(s, c.ps[bg], AF.Silu), reads=[f"ps{bg}"], writes=[sk])
                P.op("dve", lambda e, s=s, bu=bu, hb=hb, tc=tc: e.tensor_tensor(act[:, hb, tc * 512:(tc + 1) * 512], s, c.ps[bu], ALU.mult),
                     reads=[sk, f"ps{bu}"], writes=[f"act{hb}.{tc}"])
            gi += 1
        for db in range(16):
            wl = []
            for part in range(2):
                slab, sk_ = wload(c, c.w_dn[l, half, db, :, part * 11:(part + 1) * 11, :], nk=11)
                wl.append((slab, sk_))
            for tc in range(2):
                bank = (db * 2 + tc) % 4 if (gi % 2 == 0) else 4 + (db * 2 + tc) % 4
                ps, psk = c.ps[bank], f"ps{bank}"
                pairs = []
                reads = []
                for part in range(2):
                    slab, sk_ = wl[part]
                    reads += sk_
                    for jj in range(11):
                        j = part * 11 + jj
                        pairs.append((slab[:, jj, :], act[:, j, tc * 512:(tc + 1) * 512]))
                        reads.append(f"act{j}.{tc}")

                def fn(e, pairs=pairs, ps=ps):
                    ins = None
                    n = len(pairs)
                    for i, (lh, r) in enumerate(pairs):
                        ins = e.matmul(ps, lh, r, start=(i == 0), stop=(i == n - 1))
                    return ins
                P.op("pe", fn, reads=reads, writes=[psk])
                hv = c.h[:, db, tc * 512:(tc + 1) * 512]
                P.op("dve", lambda e, hv=hv, ps=ps: e.tensor_tensor(hv, hv, ps, ALU.add), reads=[psk, f"h{db}.{tc}"], writes=[f"h{db}.{tc}"])


def layer1_mixer(c):
    P, nc, sb = c.P, c.nc, c.sb
    P.barrier(lambda e: e.memset(c.lam[:, 2:3], 0.0))
    rms_to_hn(c, lambda kc, tc: c.h[:, kc, tc * 512:(tc + 1) * 512], lambda kc, tc: [f"h{kc}.{tc}"], V_MIXG1, 2,
              lambda kc, tc: c.hn[:, kc, tc * 512:(tc + 1) * 512], lambda kc, tc: f"hn{kc}.{tc}", 192)
    P.barrier(lambda e: e.memset(c.lam[:, 2:3], 0.0))
    y = sb(96, 32 * 1024, BF16, "p (k t) -> p k t", k=KC)
    A = [sb(128 + 4 * i, 4096, F32) for i in range(4)] + [sb(147 + 4 * i, 4096, F32) for i in range(3)]
    QB = sb(192, 2048, BF16)
    KB = sb(194, 2048, BF16)
    KBT = sb(196, 4096, BF16, "p (c k) -> p c k", c=16)
    VT = sb(200, 4096, BF16, "p (c k) -> p c k", c=16)
    SCT = sb(155, 2048, BF16, "p (c t) -> p c t", c=16)
    msk = sb(157, 1024, BF16)
    S32 = sb(158, 512, F32)
    Stmp = sb(158.5, 512, F32)
    SBF = sb(146.25, 256, BF16)
    EBL = sb(146.5, 64, F32)
    LBV = sb(146.5625, 4 * 64, F32, "p (a h) -> p a h", a=4)
    IT = A[5]
    ITb = sb(147 + 4 * 1, 2048, BF16)
    P.op("dve", lambda e: e.memset(msk, 1.0), writes=["msk"])
    P.op("dve", lambda e: e.memset(msk.rearrange("p (c t) -> p c t", t=64)[:, :, 0:1], 0.0), reads=["msk"], writes=["msk"])
    for ld in range(2):
        r0c = V_LB + (ld * 2 + 0) * 16
        r1c = V_LB + (ld * 2 + 1) * 16
        P.op("dve", lambda e, ld=ld, r0c=r0c, r1c=r1c: e.tensor_tensor(LBV[:, 2 * ld, :], c.vecs[:, r1c:r1c + 16], c.vecs[:, r0c:r0c + 16], ALU.subtract),
             reads=["vecs", "LBV"], writes=["LBV"])
        P.op("act", lambda e, ld=ld: e.activation(LBV[:, 2 * ld, :], LBV[:, 2 * ld, :], AF.Sigmoid), reads=["LBV"], writes=["LBV"])
        P.op("dve", lambda e, ld=ld: e.tensor_scalar(LBV[:, 2 * ld + 1, :], LBV[:, 2 * ld, :], -1.0, 1.0, ALU.mult, ALU.add), reads=["LBV"], writes=["LBV"])

    def proj2(wslab, dst_fn, func, dkey, scale=None):
        slab, skeys = wload(c, wslab)
        for tc in range(2):
            bank = tc
            proj_fm(c, slab, skeys, lambda kc, tc=tc: c.hn[:, kc, tc * 512:(tc + 1) * 512], lambda kc, tc=tc: [f"hn{kc}.{tc}"], c.ps[bank], f"ps{bank}")
            P.op("act", lambda e, tc=tc, bank=bank: e.activation(dst_fn(tc), c.ps[bank], func), reads=[f"ps{bank}"], writes=[dkey])

    def head_pass(hd, ld):
        lb = LBV[:, 2 * ld, hd:hd + 1]
        oml = LBV[:, 2 * ld + 1, hd:hd + 1]
        sq, fv, kk, lf, bb = A[0], A[1], A[2], A[3], A[4]
        proj2(c.w_hin[0, hd], lambda tc: sq[:, tc * 512:(tc + 1) * 512], AF.Silu, "A0")
        proj2(c.w_hin[1 + ld, hd], lambda tc: fv[:, tc * 512:(tc + 1) * 512], AF.Sigmoid, "A1")
        proj2(c.w_hin[3, hd], lambda tc: ITb[:, tc * 512:(tc + 1) * 512], AF.Copy, "A5")
        P.op("dve", lambda e: e.tensor_scalar(fv, fv, oml, lb, ALU.mult, ALU.add), reads=["A1", "LBV"], writes=["A1"])
        P.op("dve", lambda e: e.tensor_scalar(kk, fv, -1.0, 1.0, ALU.mult, ALU.add), reads=["A1"], writes=["A2"])
        P.op("act", lambda e: e.activation(lf, fv, AF.Ln), reads=["A1"], writes=["A3"])
        for tc in range(2):
            P.op("dve", lambda e, tc=tc: e.tensor_tensor_scan(bb[:, tc * 512:(tc + 1) * 512], msk, lf[:, tc * 512:(tc + 1) * 512], 0.0, ALU.mult, ALU.add),
                 reads=["A3", "msk"], writes=["A4"])
        b3 = bb.rearrange("p (c t) -> p c t", t=64)
        if ld == 1:
            P.op("dve", lambda e: e.tensor_tensor(lf, lf, bb, ALU.subtract), reads=["A3", "A4"], writes=["A3"])
            P.op("dve", lambda e: e.tensor_tensor(b3, lf.rearrange("p (c t) -> p c t", t=64), b3[:, :, 63:64].to_broadcast([128, 16, 64]), ALU.add),
                 reads=["A3", "A4"], writes=["A4"])
        eb, enb = A[1], A[3]
        P.op("act", lambda e: e.activation(eb, bb, AF.Exp), reads=["A4", "A1"], writes=["A1"])
        P.op("act", lambda e: e.activation(enb, bb, AF.Exp, scale=-1.0), reads=["A4", "A3"], writes=["A3"])
        P.op("dve", lambda e: e.tensor_tensor(QB, sq, eb, ALU.mult), reads=["A0", "A1"], writes=["QB"])
        P.op("dve", lambda e: e.tensor_tensor(KB, kk, enb, ALU.mult), reads=["A2", "A3"], writes=["KB"])
        e3 = eb.rearrange("p (c t) -> p c t", t=64)
        edge = 63 if ld == 0 else 0
        P.op("dve", lambda e: e.tensor_copy(EBL, e3[:, :, edge]), reads=["A1"], writes=["EBL"])
        for src, skey, dst, dkey in ((ITb, "A5", VT, "VT"), (KB, "KB", KBT, "KBT")):
            for half in range(2):
                bank = 4 + half
                pst = c.ps[bank].bitcast(BF16).rearrange("p (c k) -> p c k", k=128)

                def fn(e, src=src, half=half, pst=pst):
                    ins = None
                    for cl in range(8):
                        ch = half * 8 + cl
                        ins = e.transpose(pst[0:64, cl, :], src[:, ch * 64:(ch + 1) * 64], c.ident)
                    return ins
                P.op("pe", fn, reads=[skey, "ident"], writes=[f"ps{bank}"])
                P.op("act" if half == 0 else "dve",
                     (lambda e, dst=dst, half=half, pst=pst: e.copy(dst[0:64, half * 8:(half + 1) * 8, :], pst[0:64, :, :])) if half == 0 else
                     (lambda e, dst=dst, half=half, pst=pst: e.tensor_copy(dst[0:64, half * 8:(half + 1) * 8, :], pst[0:64, :, :])),
                     reads=[f"ps{bank}"], writes=[dkey])
        msk2 = (c.tri if ld == 0 else c.triT)[0:64, 0:64]
        for half in range(2):
            bank = 6 + half
            psv = c.ps[bank].rearrange("p (c t) -> p c t", t=64)

            def fn(e, half=half, psv=psv):
                ins = None
                for cl in range(8):
                    ch = half * 8 + cl
                    ins = e.matmul(psv[0:64, cl, :], KB[:, ch * 64:(ch + 1) * 64], QB[:, ch * 64:(ch + 1) * 64], start=True, stop=True)
                return ins
            P.op("pe", fn, reads=["KB", "QB"], writes=[f"ps{bank}"])
            P.op("dve", lambda e, half=half, psv=psv: e.tensor_tensor(SCT[0:64, half * 8:(half + 1) * 8, :], psv[0:64, :, :],
                                                                     msk2.unsqueeze(1).to_broadcast([64, 8, 64]), ALU.mult),
                 reads=[f"ps{bank}", "tri", "triT"], writes=["SCT"])
        if ld == 0:
            P.op("dve", lambda e: e.memset(S32, 0.0), writes=["S32"])
        else:
            P.op("sp", lambda e: e.dma_start(out=S32, in_=c.cout[hd * 128:(hd + 1) * 128, :]), reads=["cout"], writes=["S32"], dma_key="st0")
            P.op("sp", lambda e: e.dma_start(out=Stmp, in_=c.cout[2048 + hd * 128:2048 + (hd + 1) * 128, :]), reads=["cout"], writes=["Stmp"], dma_key="st1")
            P.op("dve", lambda e: e.tensor_scalar(S32, S32, c.vecs[:, V_SEL0:V_SEL0 + 1], None, ALU.mult), reads=["S32", "vecs"], writes=["S32"])
            P.op("dve", lambda e: e.scalar_tensor_tensor(S32, Stmp, c.vecs[:, V_SEL1:V_SEL1 + 1], S32, ALU.mult, ALU.add), reads=["S32", "Stmp", "vecs"], writes=["S32"])
        P.op("act", lambda e: e.copy(SBF, S32), reads=["S32"], writes=["SBF"])
        order = list(range(16)) if ld == 0 else list(range(15, -1, -1))
        for n, ch in enumerate(order):
            obank = 2 + (ch // 8)
            pso = c.ps[obank].rearrange("p (c t) -> p c t", t=64)[:, ch % 8, :]

            def fo(e, ch=ch, pso=pso):
                e.matmul(pso, VT[0:64, ch, :], SCT[0:64, ch, :], start=True, stop=False)
                return e.matmul(pso, SBF, QB[:, ch * 64:(ch + 1) * 64], start=False, stop=True)
            P.op("pe", fo, reads=["VT", "SCT", "SBF", "QB"], writes=[f"ps{obank}"])
            ubank = 4 + (n % 2)
            psu = c.ps[ubank][:, 0:128]
            P.op("pe", lambda e, ch=ch, psu=psu: e.matmul(psu, KBT[0:64, ch, :], VT[0:64, ch, :], start=True, stop=True),
                 reads=["KBT", "VT"], writes=[f"ps{ubank}"])
            ecol = EBL[:, ch:ch + 1]
            P.op("dve", lambda e, ecol=ecol: e.tensor_scalar(Stmp, S32, ecol, None, ALU.mult), reads=["S32", "EBL"], writes=["Stmp"])
            P.op("dve", lambda e, ecol=ecol, psu=psu: e.scalar_tensor_tensor(S32, psu, ecol, Stmp, ALU.mult, ALU.add),
                 reads=[f"ps{ubank}", "Stmp", "EBL"], writes=["S32"])
            if n < 15:
                P.op("act", lambda e: e.copy(SBF, S32), reads=["S32"], writes=["SBF"])
            if (n % 8) == 7:
                t0 = (ch // 8) * 512
                yv = y[:, hd, t0:t0 + 512]
                if ld == 0:
                    P.op("act", lambda e, yv=yv, obank=obank: e.copy(yv, c.ps[obank]), reads=[f"ps{obank}"], writes=[f"y{hd}.{ch // 8}"])
                else:
                    P.op("dve", lambda e, yv=yv, obank=obank: e.tensor_tensor(yv, yv, c.ps[obank], ALU.add), reads=[f"ps{obank}", f"y{hd}.{ch // 8}"], writes=[f"y{hd}.{ch // 8}"])
        if ld == 0:
            P.op("sp", lambda e: e.dma_start(out=c.cin[hd * 128:(hd + 1) * 128, :], in_=S32), reads=["S32"], writes=["cin"], dma_key=f"ci{hd % 4}")

    for hd in range(16):
        head_pass(hd, 0)
    P.op("pool", lambda e: e.collective_compute("AllGather", ALU.bypass, [[0, 1], [2, 3], [4, 5], [6, 7]], [c.cin.opt()], [c.cout.opt()]),
         reads=["cin"], writes=["cout"], dma_key="cc", dma_inc=1)
    for hd in range(16):
        head_pass(hd, 1)

    P.barrier(lambda e: e.memset(c.lam[:, 2:3], 0.0))
    sqy = [sb(192 + i, 1024, BF16) for i in range(2)]
    rstd2 = A[0]
    for tc in range(2):
        ps, psk = c.ps[6 + tc], f"ps{6 + tc}"
        for hd in range(16):
            s_ = sqy[hd % 2]
            sk = f"sqy{hd % 2}"
            P.op("act", lambda e, s_=s_, hd=hd, tc=tc: e.activation(s_, y[:, hd, tc * 512:(tc + 1) * 512], AF.Square), reads=[f"y{hd}.{tc}"], writes=[sk])
            P.op("pe", lambda e, s_=s_, hd=hd, ps=ps: e.matmul(ps, c.ones, s_, start=(hd == 0), stop=(hd == 15)), reads=[sk, "ones"], writes=[psk])
        rv = rstd2[:, tc * 512:(tc + 1) * 512]
        P.op("act", lambda e, ps=ps, rv=rv: e.activation(rv, ps, AF.Sqrt, bias=c.epsb[:, 0:1], scale=1.0 / D), reads=[psk, "epsb"], writes=["A0"])
        P.op("dve", lambda e, rv=rv: e.reciprocal(rv, rv), reads=["A0"], writes=["A0"])
    sgt = [sb(128 + 4 + 2 * i, 2048, F32) for i in range(2)]
    ytmp = sb(128 + 8, 2048, F32)
    for hd in range(16):
        slab, skeys = wload(c, c.w_hin[4, hd])
        for tc in range(2):
            bank = tc
            proj_fm(c, slab, skeys, lambda kc, tc=tc: c.hn[:, kc, tc * 512:(tc + 1) * 512], lambda kc, tc=tc: [f"hn{kc}.{tc}"], c.ps[bank], f"ps{bank}")
            sg_ = sgt[tc]
            P.op("act", lambda e, sg_=sg_, bank=bank: e.activation(sg_, c.ps[bank], AF.Silu), reads=[f"ps{bank}"], writes=[f"sgt{tc}"])
            yv = y[:, hd, tc * 512:(tc + 1) * 512]
            P.op("dve", lambda e, yv=yv, hd=hd, tc=tc: e.scalar_tensor_tensor(ytmp, yv, c.vecs[:, V_GNORM + hd:V_GNORM + hd + 1], rstd2[:, tc * 512:(tc + 1) * 512], ALU.mult, ALU.mult),
                 reads=[f"y{hd}.{tc}", "A0", "vecs"], writes=["ytmp"])
            P.op("dve", lambda e, yv=yv, sg_=sg_: e.tensor_tensor(yv, ytmp, sg_, ALU.mult), reads=["ytmp", f"sgt{tc}"], writes=[f"y{hd}.{tc}"])
    for db in range(16):
        slab, skeys = wload(c, c.w_ho[db])
        for tc in range(2):
            bank = 2 + (db * 2 + tc) % 4
            proj_fm(c, slab, skeys, lambda kc, tc=tc: y[:, kc, tc * 512:(tc + 1) * 512], lambda kc, tc=tc: [f"y{kc}.{tc}"], c.ps[bank], f"ps{bank}")
            hv = c.h[:, db, tc * 512:(tc + 1) * 512]
            P.op("dve", lambda e, hv=hv, bank=bank: e.tensor_tensor(hv, hv, c.ps[bank], ALU.add), reads=[f"ps{bank}", f"h{db}.{tc}"], writes=[f"h{db}.{tc}"])


def final_norm(c):
    P, sb = c.P, c.sb
    P.barrier(lambda e: e.memset(c.lam[:, 2:3], 0.0))
    rms_to_hn(c, lambda kc, tc: c.h[:, kc, tc * 512:(tc + 1) * 512], lambda kc, tc: [f"h{kc}.{tc}"], V_FING, 2,
              lambda kc, tc: c.h[:, kc, tc * 512:(tc + 1) * 512], lambda kc, tc: f"h{kc}.{tc}", 192)
    for kc in range(KC):
        P.op("sp", lambda e, kc=kc: e.dma_start(out=c.yT[:, kc, :], in_=c.h[:, kc, :]),
             reads=[f"h{kc}.0", f"h{kc}.1"], writes=[f"yo{kc}"], dma_key=f"out{kc % 8}")


def host_prep(inp):
    f32 = np.float32
    x = inp["x"]
    shared = {}
    W = inp["even_w_in"][0]
    shared["w_q"] = slabify(W[:, 0:1024], 128)
    shared["w_k"] = slabify(W[:, 1024:2048], 128)
    shared["w_v"] = slabify(W[:, 2048:3072], 512)
    shared["w_u"] = slabify(W[:, 3072:4096], 128)
    shared["w_vb"] = slabify(W[:, 4096:5120], 512)
    shared["w_eo"] = slabify(inp["even_w_out"][0], 128)
    shared["w_g"] = np.stack([slabify(inp["ffn_w_gate"][l], 128) for l in range(2)])
    shared["w_up"] = np.stack([slabify(inp["ffn_w_up"][l], 128) for l in range(2)])
    wd = inp["ffn_w_down"]
    shared["w_dn"] = np.ascontiguousarray(wd.reshape(2, 2, 22, 128, 16, 128).transpose(0, 1, 4, 3, 2, 5))
    shared["w_ho"] = slabify(inp["hgrn_w_out"][0], 128)
    Wh = inp["hgrn_w_in"][0]
    hin = [slabify(Wh[:, i * 2048:(i + 1) * 2048], 128) for i in range(5)]
    hin_even = np.stack([hin[0], hin[1], hin[2], hin[3], hin[4]])
    hin_odd = np.stack([hin[0], hin[2], hin[1], hin[3], hin[4]])
    consts = np.zeros((128, 4, 128), f32)
    consts[:, 0, :] = np.eye(128, dtype=f32)
    consts[:, 1, :] = 1.0
    pm = np.zeros((128, 128), f32)
    for base in (0, 64):
        for d in range(8):
            pm[base + d + 8, base + d] = 1.0
            pm[base + d, base + d + 8] = 1.0
    consts[:, 2, :] = pm
    consts[:, 3, :] = np.triu(np.ones((128, 128), f32))
    half = 8
    invf_vals = (500000.0 ** (-np.arange(half, dtype=np.float64) / half)).astype(f32)
    invf = np.zeros(128, f32)
    sgn = np.zeros(128, f32)
    for base in (0, 64):
        invf[base:base + 8] = invf_vals
        invf[base + 8:base + 16] = invf_vals
        sgn[base:base + 8] = -1.0
        sgn[base + 8:base + 16] = 1.0
    lbraw = inp["hgrn_lower_bounds"]
    in_maps = []
    for core in range(NCORES):
        b, hf = core // 2, core % 2
        own = np.arange(T) if hf == 0 else (TA - 1 - np.arange(T))
        oth = (TA - 1 - np.arange(T)) if hf == 0 else np.arange(T)
        tok = np.concatenate([own, oth])
        xT = np.ascontiguousarray(x[b][tok, :].T.reshape(KC, 128, TA).transpose(1, 0, 2))
        vecs = np.zeros((128, NV), f32)
        vecs[:, V_MIXG0:V_MIXG0 + 16] = fm_vec(inp["mix_norm"][0])
        vecs[:, V_FFNG0:V_FFNG0 + 16] = fm_vec(inp["ffn_norm"][0])
        vecs[:, V_MIXG1:V_MIXG1 + 16] = fm_vec(inp["mix_norm"][1])
        vecs[:, V_FFNG1:V_FFNG1 + 16] = fm_vec(inp["ffn_norm"][1])
        vecs[:, V_FING:V_FING + 16] = fm_vec(inp["final_norm"])
        vecs[:, V_INVF] = invf
        vecs[:, V_SGN] = sgn
        vecs[:, V_SUBLN] = inp["diff_subln"][0]
        vecs[:, V_SEL0] = 1.0 if hf == 1 else 0.0
        vecs[:, V_SEL1] = 1.0 if hf == 0 else 0.0
        dirs = (0, 1) if hf == 0 else (1, 0)
        for ld in range(2):
            for layer in range(2):
                vecs[:, V_LB + (ld * 2 + layer) * 16:V_LB + (ld * 2 + layer + 1) * 16] = fm_vec(lbraw[dirs[ld], layer])
        vecs[:, V_GNORM:V_GNORM + 16] = fm_vec(inp["hgrn_g_norm"][0])
        rows = np.zeros((1, NR), f32)
        rows[0, R_LNG:R_LNG + 1024] = inp["gmlp_ln_g"][0]
        rows[0, R_LNB:R_LNB + 1024] = inp["gmlp_ln_b"][0]
        ws = inp["gmlp_w_s"][0]
        bs = inp["gmlp_b_s"][0]
        if hf == 1:
            ws = ws[:, ::-1, ::-1]
            bs = bs[:, ::-1]
        rows[0, R_BSB:R_BSB + 1024] = bs.reshape(-1)
        rows[0, R_LQ:R_LQ + 256] = np.concatenate([inp["diff_lq1"][0], inp["diff_lk1"][0], inp["diff_lq2"][0], inp["diff_lk2"][0]])
        m = dict(shared)
        m["xT"] = xT
        m["pos"] = tok.astype(f32).reshape(1, TA)
        m["vecs"] = vecs
        m["rows"] = rows
        m["consts"] = consts
        m["w_s"] = np.ascontiguousarray(ws.transpose(2, 0, 1))
        m["w_hin"] = hin_even if hf == 0 else hin_odd
        in_maps.append(m)
    return in_maps


def assemble(results):
    out = np.zeros((4, TA, D), np.float32)
    for core in range(NCORES):
        b, hf = core // 2, core % 2
        yT = results[core]["yT"]
        y = yT.transpose(2, 1, 0).reshape(T, D)
        if hf == 0:
            out[b, 0:T] = y
        else:
            out[b, T:TA] = y[::-1]
    return out


def kernel(**inputs):
    inp = {k: np.asarray(v) for k, v in inputs.items()}
    in_maps = host_prep(inp)
    nc = build("full")
    res = run_bass_kernel_spmd(nc, in_maps, core_ids=list(range(NCORES)))
    return assemble(res.results)
```

```python
import contextlib
import numpy as np
import concourse.bass as bass
import concourse.mybir as mybir
from concourse.bass_utils import run_bass_kernel_spmd

F32 = mybir.dt.float32
BF16 = mybir.dt.bfloat16
AF = mybir.ActivationFunctionType
ALU = mybir.AluOpType
AX = mybir.AxisListType

ENGS = ("pe", "act", "dve", "pool", "sp")
DEBUG_TAGS = False
INS_TAGS = {}
D = 2048
KC = 16
T = 1024
TA = 2048
FH = 5632
EPS = 1e-6
NCORES = 8


class Op:
    __slots__ = ("eng", "fn", "deps", "signal", "count", "dma_key", "dma_cum", "dma_inc", "idx", "tag")

    def __init__(self, eng, fn, dma_key, dma_inc):
        self.eng = eng
        self.fn = fn
        self.deps = []
        self.signal = False
        self.count = 0
        self.dma_key = dma_key
        self.dma_cum = 0
        self.dma_inc = dma_inc
        self.idx = 0


class Prog:
    def __init__(self):
        self.ops = []
        self.last_w = {}
        self.readers = {}
        self.dma_cnt = {}
        self.last_on = {}
        self.bar = None
        self.bar_seen = set()

    def op(self, eng, fn, reads=(), writes=(), dma_key=None, dma_inc=16):
        o = Op(eng, fn, dma_key, dma_inc)
        o.idx = len(self.ops)
        if DEBUG_TAGS:
            import sys as _s
            f = _s._getframe(1)
            o.tag = f"{f.f_lineno}<{f.f_back.f_lineno}<{f.f_back.f_back.f_lineno if f.f_back.f_back else 0} w={list(writes)[:3]}"
        deps = set()
        for r in reads:
            w = self.last_w.get(r)
            if w is not None:
                deps.add(w)
            if r.startswith("ps"):
                for rd in self.readers.get(r, ()):
                    if rd.eng != eng:
                        deps.add(rd)
        for wkey in writes:
            w = self.last_w.get(wkey)
            if w is not None:
                deps.add(w)
            for rd in self.readers.get(wkey, ()):
                deps.add(rd)
        if dma_key is not None:
            prev = self.last_on.get("dma:" + dma_key)
            if prev is not None:
                deps.add(prev)
        if self.bar is not None and eng not in self.bar_seen:
            deps.add(self.bar)
            self.bar_seen.add(eng)
        for d in deps:
            if d.dma_key is None and d.eng == "pe" and eng == "pe" and dma_key is None:
                continue
            o.deps.append(d)
            if d.dma_key is None:
                d.signal = True
        for r in reads:
            self.readers.setdefault(r, []).append(o)
        for wkey in writes:
            self.last_w[wkey] = o
            self.readers[wkey] = []
        if dma_key is not None:
            self.dma_cnt[dma_key] = self.dma_cnt.get(dma_key, 0) + dma_inc
            o.dma_cum = self.dma_cnt[dma_key]
            self.last_on["dma:" + dma_key] = o
        else:
            self.last_on[eng] = o
        self.ops.append(o)
        return o

    def barrier(self, nopfn):
        o = Op("dve", nopfn, None, 16)
        o.idx = len(self.ops)
        for k, d in self.last_on.items():
            if d is None:
                continue
            if d.dma_key is None and d.eng == "dve":
                continue
            o.deps.append(d)
            if d.dma_key is None:
                d.signal = True
        self.last_on["dve"] = o
        self.ops.append(o)
        self.bar = o
        self.bar_seen = {"dve"}
        self.last_w = {}
        self.readers = {}
        return o

    def emit(self, nc, final_dma_keys=()):
        cnt = {e: 0 for e in ENGS}
        for o in self.ops:
            if o.dma_key is None and o.signal:
                cnt[o.eng] += 1
                o.count = cnt[o.eng]
        per_eng = {e: [] for e in ENGS}
        for o in self.ops:
            per_eng[o.eng].append(o)
        dma_keys = sorted(self.dma_cnt.keys())
        with contextlib.ExitStack() as st:
            sems = {}
            for e in ENGS:
                sems[e] = st.enter_context(nc.semaphore("s_" + e))
            for k in dma_keys:
                sems["dma_" + k] = st.enter_context(nc.semaphore("d_" + k))
            block = st.enter_context(nc.Block())

            def run(engname, engobj):
                waited = {}
                for o in per_eng[engname]:
                    need = {}
                    for d in o.deps:
                        if d.dma_key is not None:
                            s, v = "dma_" + d.dma_key, d.dma_cum
                        else:
                            s, v = d.eng, d.count
                        if v > need.get(s, 0):
                            need[s] = v
                    for s, v in need.items():
                        if waited.get(s, 0) >= v:
                            continue
                        engobj.wait_ge(sems[s], v)
                        waited[s] = v
                    ins = o.fn(engobj)
                    if DEBUG_TAGS:
                        try:
                            INS_TAGS[ins.ins.name] = o.tag
                        except Exception:
                            pass
                    if o.dma_key is not None:
                        ins.then_inc(sems["dma_" + o.dma_key], o.dma_inc)
                    elif o.signal:
                        ins.then_inc(sems[o.eng], 1)
                if engname == "sp":
                    for k in final_dma_keys:
                        engobj.wait_ge(sems["dma_" + k], self.dma_cnt[k])

            @block.tensor
            def _(e):
                run("pe", e)

            @block.scalar
            def _(e):
                run("act", e)

            @block.vector
            def _(e):
                run("dve", e)

            @block.gpsimd
            def _(e):
                run("pool", e)

            @block.sync
            def _(e):
                run("sp", e)


def slabify(W, ncols):
    K, N = W.shape
    return np.ascontiguousarray(W.reshape(K // 128, 128, N // ncols, ncols).transpose(2, 1, 0, 3))


def fm_vec(v):
    return np.ascontiguousarray(v.reshape(-1, 128).T)


V_MIXG0, V_FFNG0, V_MIXG1, V_FFNG1, V_FING = 0, 16, 32, 48, 64
V_INVF, V_SGN, V_SUBLN, V_SEL0, V_SEL1 = 80, 81, 82, 83, 84
V_LB = 88
V_GNORM = 152
NV = 168
R_LNG, R_LNB, R_BSB, R_LQ = 0, 1024, 2048, 3072
NR = 3072 + 256


class Ctx:
    pass


def build(stage="full"):
    nc = bass.Bass("TRN2", target_bir_lowering=False)
    P = Prog()
    c = Ctx()
    c.nc, c.P = nc, P
    dt_in = lambda name, shape: nc.dram_tensor(name, list(shape), F32, kind="ExternalInput").ap()
    c.xT = dt_in("xT", [128, KC, TA])
    c.pos = dt_in("pos", [1, TA])
    c.vecs_d = dt_in("vecs", [128, NV])
    c.rows_d = dt_in("rows", [1, NR])
    c.consts_d = dt_in("consts", [128, 4, 128])
    c.w_q = dt_in("w_q", [8, 128, KC, 128])
    c.w_k = dt_in("w_k", [8, 128, KC, 128])
    c.w_v = dt_in("w_v", [2, 128, KC, 512])
    c.w_u = dt_in("w_u", [8, 128, KC, 128])
    c.w_vb = dt_in("w_vb", [2, 128, KC, 512])
    c.w_s = dt_in("w_s", [128, 8, 128])
    c.w_eo = dt_in("w_eo", [16, 128, KC, 128])
    if stage in ("full", "l0", "ffn0"):
        c.w_g = dt_in("w_g", [2, 44, 128, KC, 128])
        c.w_up = dt_in("w_up", [2, 44, 128, KC, 128])
        c.w_dn = dt_in("w_dn", [2, 2, 16, 128, 22, 128])
    if stage in ("full", "l1"):
        c.w_hin = dt_in("w_hin", [5, 16, 128, KC, 128])
        c.w_ho = dt_in("w_ho", [16, 128, KC, 128])
    c.yT = nc.dram_tensor("yT", [128, KC, T], F32, kind="ExternalOutput").ap()
    c.cin = nc.dram_tensor("cin", [16 * 128, 128], F32).ap()
    c.cout = nc.dram_tensor("cout", [2 * 16 * 128, 128], F32).ap()

    ARENA_KB = 204
    arena = nc.alloc_sbuf_tensor("arena", [128, ARENA_KB * 512], BF16).ap()

    def sb(off_kb, nbytes, dtype, pattern=None, **kw):
        a = int(round(off_kb * 512))
        n = nbytes // 2
        assert a + n <= ARENA_KB * 512, (off_kb, nbytes)
        v = arena[:, a:a + n]
        if dtype == F32:
            v = v.bitcast(F32)
        if pattern:
            v = v.rearrange(pattern, **kw)
        return v

    c.sb = sb
    c.ps = [nc.alloc_psum_tensor(f"ps{i}", [128, 512], F32).ap() for i in range(8)]
    c.h = sb(0, 64 * 1024, F32, "p (k t) -> p k t", k=KC)
    c.hn = sb(64, 32 * 1024, BF16, "p (k t) -> p k t", k=KC)
    c.vecs = sb(144, NV * 4, F32)
    c.ident = sb(144.75, 256, BF16)
    c.ones = sb(145.0, 256, BF16)
    c.pm = sb(145.25, 256, BF16)
    c.tri = sb(145.5, 256, BF16)
    c.triT = sb(145.75, 256, BF16)
    c.lam = sb(146.0, 16, F32)
    c.epsb = sb(146.0625, 16, F32)
    c.cst32 = sb(200, 4 * 512, F32, "p (a b) -> p a b", a=4)
    c.wslot = [sb(160 + 4 * i, 4096, BF16, "p (k n) -> p k n", k=KC) for i in range(8)]
    c.wbig = [sb(160 + 16 * i, 16384, BF16, "p (k n) -> p k n", k=KC) for i in range(2)]
    c.wcnt = 0
    c.bigcnt = 0

    load_consts(c)
    c.stage = stage
    if stage == "l1":
        for kc in range(KC):
            P.op("sp", lambda e, kc=kc: e.dma_start(out=c.h[:, kc, :], in_=c.xT[:, kc, 0:T]), writes=[f"h{kc}.0", f"h{kc}.1"], dma_key=f"x{kc}")
        layer1_mixer(c)
    elif stage == "ffn0":
        for kc in range(KC):
            P.op("sp", lambda e, kc=kc: e.dma_start(out=c.h[:, kc, :], in_=c.xT[:, kc, 0:T]), writes=[f"h{kc}.0", f"h{kc}.1"], dma_key=f"x{kc}")
        ffn(c, 0)
    elif stage in ("full", "l0") or stage.startswith("l0"):
        layer0_mixer(c)
        if stage in ("full", "l0"):
            ffn(c, 0)
    if stage in ("full",):
        layer1_mixer(c)
        ffn(c, 1)
        final_norm(c)
    else:
        P.barrier(lambda e: e.memset(c.lam[:, 2:3], 0.0))
        for kc in range(KC):
            P.op("sp", lambda e, kc=kc: e.dma_start(out=c.yT[:, kc, :], in_=c.h[:, kc, :]),
                 reads=[f"h{kc}.0", f"h{kc}.1"], writes=[f"y{kc}"], dma_key=f"out{kc % 8}")
    P.emit(nc, final_dma_keys=[f"out{i}" for i in range(8)])
    return nc


def wload(c, dram_slab, big=False, nk=KC):
    P = c.P
    if big:
        i = c.bigcnt % 2
        c.bigcnt += 1
        ap = c.wbig[i]
        keys = [f"ws{4 * i + j}" for j in range(4)]
        dk = f"wb{i}"
    else:
        i = c.wcnt % 8
        c.wcnt += 1
        ap = c.wslot[i]
        keys = [f"ws{i}"]
        dk = f"w{i}"
    dst = ap if nk == KC else ap[:, 0:nk, :]
    P.op("pool", lambda e: e.dma_start(out=dst, in_=dram_slab), writes=keys, dma_key=dk)
    return ap, keys


def load_consts(c):
    P, nc = c.P, c.nc
    P.op("dve", lambda e: e.memset(c.epsb[:, 0:1], EPS), writes=["epsb"])
    P.op("dve", lambda e: e.memset(c.epsb[:, 1:2], EPS / 0.64), reads=["epsb"], writes=["epsb"])
    P.op("sp", lambda e: e.dma_start(out=c.vecs, in_=c.vecs_d), writes=["vecs"], dma_key="c")
    P.op("sp", lambda e: e.dma_start(out=c.cst32, in_=c.consts_d), writes=["cst32"], dma_key="c")
    for i, (ap, nm) in enumerate([(c.ident, "ident"), (c.ones, "ones"), (c.pm, "pm"), (c.tri, "tri")]):
        P.op("dve", lambda e, ap=ap, i=i: e.tensor_copy(ap, c.cst32[:, i, :]), reads=["cst32"], writes=[nm])
    psb = c.ps[7].bitcast(BF16)
    P.op("pe", lambda e: e.transpose(psb[:, 0:128], c.tri, c.ident), reads=["tri", "ident"], writes=["ps7"])
    P.op("dve", lambda e: e.tensor_copy(c.triT, psb[:, 0:128]), reads=["ps7"], writes=["triT"])


def rms_to_hn(c, src_fn, src_keys_fn, gcol, ntc, dst, dst_key_fn, tmp_off):
    P = c.P
    sq = [c.sb(tmp_off + i, 1024, BF16) for i in range(2)]
    rstd = c.sb(tmp_off + 2, 2048, F32)
    for tc in range(ntc):
        ps = c.ps[6 + (tc % 2)]
        psk = f"ps{6 + (tc % 2)}"
        for kc in range(KC):
            s = sq[kc % 2]
            sk = f"sq{kc % 2}"
            P.op("act", lambda e, s=s, kc=kc, tc=tc: e.activation(s, src_fn(kc, tc), AF.Square),
                 reads=src_keys_fn(kc, tc), writes=[sk])
            P.op("pe", lambda e, s=s, kc=kc, ps=ps: e.matmul(ps, c.ones, s, start=(kc == 0), stop=(kc == KC - 1)),
                 reads=[sk, "ones"], writes=[psk])
        P.op("act", lambda e, ps=ps: e.activation(rstd, ps, AF.Sqrt, bias=c.epsb[:, 0:1], scale=1.0 / D), reads=[psk, "epsb"], writes=["rstd"])
        P.op("dve", lambda e: e.reciprocal(rstd, rstd), reads=["rstd"], writes=["rstd"])
        for kc in range(KC):
            P.op("dve", lambda e, kc=kc, tc=tc: e.scalar_tensor_tensor(dst(kc, tc), src_fn(kc, tc), c.vecs[:, gcol + kc:gcol + kc + 1], rstd, ALU.mult, ALU.mult),
                 reads=src_keys_fn(kc, tc) + ["rstd", "vecs"], writes=[dst_key_fn(kc, tc)])


def proj_fm(c, slab, slab_keys, rhs_fn, rhs_keys_fn, ps, psk, nk=KC):
    pairs = [(slab[:, kc, :], rhs_fn(kc)) for kc in range(nk)]
    reads = list(slab_keys)
    for kc in range(nk):
        reads += rhs_keys_fn(kc)

    def fn(e):
        ins = None
        for i, (l, r) in enumerate(pairs):
            ins = e.matmul(ps, l, r, start=(i == 0), stop=(i == nk - 1))
        return ins
    c.P.op("pe", fn, reads=reads, writes=[psk])


def layer0_mixer(c):
    P, nc, sb = c.P, c.nc, c.sb
    hn_oth = sb(96, 32 * 1024, BF16, "p (k t) -> p k t", k=KC)
    cat = hn_oth
    K_all = sb(0, 32 * 1024, BF16, "p (h t) -> p h t", h=8)
    V_all = sb(32, 32 * 1024, BF16, "p (b n) -> p b n", b=16)
    Ctab = sb(128, 8192, F32)
    Stab = sb(136, 8192, F32)
    rows = sb(147, NR * 4, F32)
    tmp_off = 192
    P.op("sp", lambda e: e.dma_start(out=rows, in_=c.rows_d.partition_broadcast(128)), writes=["rows"], dma_key="c")
    posb = sb(160, 8192, F32)
    P.op("sp", lambda e: e.dma_start(out=posb, in_=c.pos.partition_broadcast(128)), writes=["ws0", "ws1"], dma_key="c")
    TWO_PI = 2.0 * np.pi
    C1 = 6.28125
    C2 = TWO_PI - C1
    invf = c.vecs[:, V_INVF:V_INVF + 1]
    ang = sb(168, 8192, F32)
    kf = sb(176, 8192, F32)
    ki = sb(184, 8192, F32).bitcast(mybir.dt.int32)
    PI_IN = 3.1415925

    def make_table(dst, shift, key):
        P.op("dve", lambda e: e.tensor_scalar(ang, posb, invf, shift, ALU.mult, ALU.add), reads=["ws0", "ws1", "vecs"], writes=["ang"])
        P.op("dve", lambda e: e.tensor_scalar(kf, ang, 1.0 / TWO_PI, None, ALU.mult), reads=["ang"], writes=["kf"])
        P.op("dve", lambda e: e.tensor_copy(ki, kf), reads=["kf"], writes=["ki"])
        P.op("dve", lambda e: e.tensor_copy(kf, ki), reads=["ki"], writes=["kf"])
        P.op("dve", lambda e: e.scalar_tensor_tensor(ang, kf, -C1, ang, ALU.mult, ALU.add), reads=["kf", "ang"], writes=["ang"])
        P.op("dve", lambda e: e.scalar_tensor_tensor(ang, kf, -C2, ang, ALU.mult, ALU.add), reads=["kf", "ang"], writes=["ang"])
        P.op("dve", lambda e: e.tensor_scalar(kf, ang, np.pi, -TWO_PI, ALU.is_gt, ALU.mult), reads=["ang"], writes=["kf"])
        P.op("dve", lambda e: e.tensor_tensor(ang, ang, kf, ALU.add), reads=["kf", "ang"], writes=["ang"])
        P.op("dve", lambda e: e.tensor_scalar(kf, ang, -np.pi, TWO_PI, ALU.is_lt, ALU.mult), reads=["ang"], writes=["kf"])
        P.op("dve", lambda e: e.tensor_tensor(ang, ang, kf, ALU.add), reads=["kf", "ang"], writes=["ang"])
        P.op("dve", lambda e: e.tensor_scalar(ang, ang, -PI_IN, PI_IN, ALU.max, ALU.min), reads=["ang"], writes=["ang"])
        P.op("act", lambda e: e.activation(dst, ang, AF.Sin), reads=["ang"], writes=[key])

    make_table(Stab, 0.0, "Stab")
    P.op("dve", lambda e: e.tensor_scalar(Stab, Stab, c.vecs[:, V_SGN:V_SGN + 1], None, ALU.mult), reads=["Stab", "vecs"], writes=["Stab"])
    make_table(Ctab, np.pi / 2, "Ctab")
    lq = rows[:, R_LQ:R_LQ + 256].rearrange("p (a b) -> p a b", a=4)
    lt = sb(tmp_off, 512, F32, "p (a b) -> p a b", a=2)
    P.op("dve", lambda e: e.tensor_tensor(lt[:, 0, :], lq[:, 0, :], lq[:, 1, :], ALU.mult), reads=["rows"], writes=["lt"])
    P.op("dve", lambda e: e.tensor_tensor(lt[:, 1, :], lq[:, 2, :], lq[:, 3, :], ALU.mult), reads=["rows", "lt"], writes=["lt"])
    P.op("dve", lambda e: e.reduce_sum(c.lam[:, 2:4], lt, AX.X), reads=["lt"], writes=["lam"])
    P.op("act", lambda e: e.activation(c.lam[:, 2:4], c.lam[:, 2:4], AF.Exp), reads=["lam"], writes=["lam"])
    P.op("dve", lambda e: e.tensor_tensor(c.lam[:, 0:1], c.lam[:, 2:3], c.lam[:, 3:4], ALU.subtract), reads=["lam"], writes=["lam"])
    P.op("dve", lambda e: e.tensor_scalar(c.lam[:, 1:2], c.lam[:, 0:1], 0.2, -1.0, ALU.add, ALU.mult), reads=["lam"], writes=["lam"])

    stage = c.h
    for half in range(2):
        for kc in range(KC):
            P.op("sp", lambda e, kc=kc, half=half: e.dma_start(out=stage[:, kc, :], in_=c.xT[:, kc, half * T:(half + 1) * T]),
                 writes=[f"h{kc}.0", f"h{kc}.1"], dma_key=f"x{kc}")
        dstT = c.hn if half == 0 else hn_oth
        dkey = "hn" if half == 0 else "ho"
        rms_to_hn(c, lambda kc, tc: stage[:, kc, tc * 512:(tc + 1) * 512], lambda kc, tc: [f"h{kc}.{tc}"], V_MIXG0, 2,
                  lambda kc, tc, dstT=dstT: dstT[:, kc, tc * 512:(tc + 1) * 512], lambda kc, tc, dkey=dkey: f"{dkey}{kc}.{tc}", tmp_off)

    def hn_all(kc, tcc):
        src = c.hn if tcc < 2 else hn_oth
        return src[:, kc, (tcc % 2) * 512:(tcc % 2 + 1) * 512]

    def hn_all_keys(kc, tcc):
        return [f"{'hn' if tcc < 2 else 'ho'}{kc}.{tcc % 2}"]

    if c.stage == "l0A":
        return
    if c.stage == "l0Aw":
        slab, skeys = wload(c, c.w_k[0])
        proj_fm(c, slab, skeys, lambda kc: c.hn[:, kc, 0:512], lambda kc: [f"hn{kc}.0"], c.ps[2], "ps2")
        P.op("dve", lambda e: e.tensor_copy(c.h[:, 0, 0:512], c.ps[2]), reads=["ps2"], writes=["h0.0"])
        return
    P.barrier(lambda e: e.memset(c.lam[:, 2:3], 0.0))
    if c.stage == "l0Abar":
        P.op("act", lambda e: e.copy(c.h[:, 0, 0:512], c.h[:, 1, 0:512]), reads=[], writes=["h0.0"])
        P.op("pe", lambda e: e.matmul(c.ps[2], c.ones, c.hn[:, 0, 0:512], start=True, stop=True), reads=[], writes=["ps2"])
        P.op("dve", lambda e: e.tensor_copy(c.h[:, 2, 0:512], c.ps[2]), reads=["ps2"], writes=["h2.0"])
        return
    evi = [0]
    for cb in range(2 if c.stage not in ("l0B1k", "l0B1kn", "l0B1r1", "l0B1r2") else 0):
        slab, skeys = wload(c, c.w_v[cb], big=True)
        for tb in range(16):
            src = c.hn if tb < 8 else hn_oth
            sk = "hn" if tb < 8 else "ho"
            tcl = (tb % 8) // 4
            bank = tb % 2
            ps, psk = c.ps[bank], f"ps{bank}"
            pairs = [(src[:, kc, (tb % 8) * 128:(tb % 8 + 1) * 128], slab[:, kc, :]) for kc in range(KC)]

            def fn(e, pairs=pairs, ps=ps):
                ins = None
                for i, (l, r) in enumerate(pairs):
                    ins = e.matmul(ps, l, r, start=(i == 0), stop=(i == KC - 1))
                return ins
            P.op("pe", fn, reads=skeys + [f"{sk}{kc}.{tcl}" for kc in range(KC)], writes=[psk])
            dst = V_all[:, tb, cb * 512:(cb + 1) * 512]
            if tb % 2 == 0:
                P.op("act", lambda e, dst=dst, ps=ps: e.copy(dst, ps), reads=[psk], writes=[f"V{tb}.{cb}"])
            else:
                P.op("dve", lambda e, dst=dst, ps=ps: e.tensor_copy(dst, ps), reads=[psk], writes=[f"V{tb}.{cb}"])

    if c.stage == "l0B1v":
        return
    t1 = sb(tmp_off + 4, 2048, F32)
    t2 = sb(tmp_off + 6, 2048, F32)
    q16_b1 = [sb(tmp_off + 8 + i, 1024, BF16) for i in range(2)]
    q16_c = [sb(154 + i, 1024, BF16) for i in range(2)]

    def rope_block(ps_a, psk_a, ps_b, psk_b, tcc, dst, dst_key, i, q16):
        qb = q16[i % 2]
        qk = f"q16{i % 2}"
        if c.stage == "l0B1r1":
            P.op("act", lambda e: e.copy(qb, ps_a), reads=[psk_a], writes=[qk])
            P.op("pe", lambda e: e.matmul(ps_b, c.pm, qb, start=True, stop=True), reads=[qk, "pm"], writes=[psk_b])
            P.op("dve", lambda e: e.tensor_copy(dst, ps_b), reads=[psk_b], writes=[dst_key])
            return
        if c.stage == "l0B1r2":
            P.op("dve", lambda e: e.tensor_tensor(t1, ps_a, Ctab[:, tcc * 512:(tcc + 1) * 512], ALU.mult), reads=[psk_a, "Ctab"], writes=["t1"])
            P.op("dve", lambda e: e.tensor_tensor(t2, ps_a, Stab[:, tcc * 512:(tcc + 1) * 512], ALU.mult), reads=[psk_a, "Stab"], writes=["t2"])
            P.op("dve", lambda e: e.tensor_tensor(dst, t1, t2, ALU.add), reads=["t1", "t2"], writes=[dst_key])
            return
        P.op("act", lambda e: e.copy(qb, ps_a), reads=[psk_a], writes=[qk])
        P.op("pe", lambda e: e.matmul(ps_b, c.pm, qb, start=True, stop=True), reads=[qk, "pm"], writes=[psk_b])
        P.op("dve", lambda e: e.tensor_tensor(t1, ps_a, Ctab[:, tcc * 512:(tcc + 1) * 512], ALU.mult), reads=[psk_a, "Ctab", qk], writes=["t1"])
        P.op("dve", lambda e: e.tensor_tensor(t2, ps_b, Stab[:, tcc * 512:(tcc + 1) * 512], ALU.mult), reads=[psk_b, "Stab"], writes=["t2"])
        P.op("dve", lambda e: e.tensor_tensor(dst, t1, t2, ALU.add), reads=["t1", "t2"], writes=[dst_key])

    ri = 0
    for hd in range(8):
        slab, skeys = wload(c, c.w_k[hd])
        for tcc in range(4):
            ba, bb = 2 + (ri % 2) * 2, 3 + (ri % 2) * 2
            proj_fm(c, slab, skeys, lambda kc, tcc=tcc: hn_all(kc, tcc), lambda kc, tcc=tcc: hn_all_keys(kc, tcc), c.ps[ba], f"ps{ba}")
            if c.stage == "l0B1kn":
                P.op("act", lambda e, ba=ba, hd=hd, tcc=tcc: e.copy(K_all[:, hd, tcc * 512:(tcc + 1) * 512], c.ps[ba]), reads=[f"ps{ba}"], writes=[f"K{hd}.{tcc}"])
            else:
                rope_block(c.ps[ba], f"ps{ba}", c.ps[bb], f"ps{bb}", tcc, K_all[:, hd, tcc * 512:(tcc + 1) * 512], f"K{hd}.{tcc}", ri, q16_b1)
            ri += 1

    if c.stage.startswith("l0B1"):
        return
    P.barrier(lambda e: e.memset(c.lam[:, 2:3], 0.0))
    wsT = sb(tmp_off + 10, 2048, BF16, "p (g n) -> p g n", g=8)
    P.op("pool", lambda e: e.dma_start(out=wsT, in_=c.w_s), writes=["wsT"], dma_key="c2")
    for g in range(8):
        slab, skeys = wload(c, c.w_u[g])
        for tc in range(2):
            bank = 2 + (g * 2 + tc) % 2
            proj_fm(c, slab, skeys, lambda kc, tc=tc: c.hn[:, kc, tc * 512:(tc + 1) * 512], lambda kc, tc=tc: [f"hn{kc}.{tc}"], c.ps[bank], f"ps{bank}")
            P.op("act", lambda e, g=g, tc=tc, bank=bank: e.activation(cat[:, 8 + g, tc * 512:(tc + 1) * 512], c.ps[bank], AF.Gelu),
                 reads=[f"ps{bank}"], writes=[f"cat{8 + g}.{tc}"])
    vbg = sb(96, 16 * 1024, F32, "p (b n) -> p b n", b=4)
    vbn = sb(tmp_off + 1, 2048, BF16)
    sqj = sb(tmp_off + 4, 4096, F32)
    stats = sb(tmp_off, 64, F32)
    lng = rows[:, R_LNG:R_LNG + 1024]
    lnb = rows[:, R_LNB:R_LNB + 1024]
    bsb = rows[:, R_BSB:R_BSB + 1024].rearrange("p (g n) -> p g n", g=8)
    for th in range(2):
        for cb in range(2):
            slab, skeys = wload(c, c.w_vb[cb], big=True)
            for tbl in range(4):
                tb = th * 4 + tbl
                bank = tb % 2
                ps, psk = c.ps[bank], f"ps{bank}"
                pairs = [(c.hn[:, kc, tb * 128:(tb + 1) * 128], slab[:, kc, :]) for kc in range(KC)]

                def fn(e, pairs=pairs, ps=ps):
                    ins = None
                    for i, (l, r) in enumerate(pairs):
                        ins = e.matmul(ps, l, r, start=(i == 0), stop=(i == KC - 1))
                    return ins
                P.op("pe", fn, reads=skeys + [f"hn{kc}.{tb // 4}" for kc in range(KC)], writes=[psk])
                P.op("act", lambda e, tbl=tbl, cb=cb, ps=ps: e.activation(vbg[:, tbl, cb * 512:(cb + 1) * 512], ps, AF.Gelu),
                     reads=[psk], writes=[f"vbg{tbl}.{cb}"])
        for tbl in range(4):
            tb = th * 4 + tbl
            xv = vbg[:, tbl, :]
            rk = [f"vbg{tbl}.0", f"vbg{tbl}.1"]
            P.op("dve", lambda e, xv=xv: e.reduce_sum(stats[:, 0:1], xv, AX.X), reads=rk, writes=["stats"])
            P.op("dve", lambda e: e.tensor_scalar(stats[:, 1:2], stats[:, 0:1], -1.0 / 1024, None, ALU.mult), reads=["stats"], writes=["stats"])
            P.op("dve", lambda e, xv=xv: e.tensor_scalar(xv, xv, stats[:, 1:2], None, ALU.add), reads=rk + ["stats"], writes=rk)
            P.op("dve", lambda e, xv=xv: e.tensor_tensor(sqj, xv, xv, ALU.mult), reads=rk, writes=["sqj"])
            P.op("dve", lambda e: e.reduce_sum(stats[:, 2:3], sqj, AX.X), reads=["sqj"], writes=["stats"])
            P.op("act", lambda e: e.activation(stats[:, 3:4], stats[:, 2:3], AF.Sqrt, bias=c.epsb[:, 0:1], scale=1.0 / 1024), reads=["stats", "epsb"], writes=["stats"])
            P.op("dve", lambda e: e.reciprocal(stats[:, 3:4], stats[:, 3:4]), reads=["stats"], writes=["stats"])
            P.op("dve", lambda e, xv=xv: e.scalar_tensor_tensor(xv, xv, stats[:, 3:4], lng, ALU.mult, ALU.mult), reads=rk + ["stats", "rows"], writes=rk)
            P.op("dve", lambda e, xv=xv: e.tensor_tensor(vbn, xv, lnb, ALU.add), reads=rk + ["rows"], writes=["vbn"])
            for gh in range(2):
                bank = 2 + gh
                ps, psk = c.ps[bank], f"ps{bank}"

                def fn(e, gh=gh, ps=ps):
                    ins = None
                    for gl in range(4):
                        g = gh * 4 + gl
                        ins = e.matmul(ps[:, gl * 128:(gl + 1) * 128], vbn[:, g * 128:(g + 1) * 128], wsT[:, g, :], start=True, stop=True)
                    return ins
                P.op("pe", fn, reads=["vbn", "wsT"], writes=[psk])
                svt = sb(tmp_off + 8, 2048, F32, "p (g n) -> p g n", g=4)
                P.op("dve", lambda e, ps=ps, gh=gh: e.tensor_tensor(svt, ps.rearrange("p (g n) -> p g n", g=4), bsb[:, gh * 4:(gh + 1) * 4, :], ALU.add),
                     reads=[psk, "rows"], writes=["svt"])
                cv = cat[:, 8 + gh * 4:8 + gh * 4 + 4, tb * 128:(tb + 1) * 128]
                ck = [f"cat{8 + gh * 4 + gl}.{tb // 4}" for gl in range(4)]
                P.op("dve", lambda e, cv=cv: e.tensor_tensor(cv, cv, svt, ALU.mult), reads=["svt"] + ck, writes=ck)

    if c.stage == "l0B2":
        return
    P.barrier(lambda e: e.memset(c.lam[:, 2:3], 0.0))
    Et = [[sb(tmp_off + 8 + 2 * m + b, 1024, BF16) for b in range(2)] for m in range(2)]
    qrot = sb(tmp_off + 1, 2048, BF16)
    r0 = sb(tmp_off + 4, 2048, F32)
    r1 = sb(tmp_off + 6, 2048, F32)
    oT = sb(147, 2048, F32)
    a0 = sb(149, 2048, F32)
    sqb = sb(151, 1024, BF16)
    rs2 = sb(152, 2048, F32)
    scale = 64 ** -0.5
    ri = 0
    for hd in range(8):
        slab, skeys = wload(c, c.w_q[hd])
        for tc in range(2):
            proj_fm(c, slab, skeys, lambda kc, tc=tc: c.hn[:, kc, tc * 512:(tc + 1) * 512], lambda kc, tc=tc: [f"hn{kc}.{tc}"], c.ps[6], "ps6")
            rope_block(c.ps[6], "ps6", c.ps[7], "ps7", tc, qrot[:, tc * 512:(tc + 1) * 512], f"qrot{tc}", ri, q16_c)
            ri += 1
        for tc in range(2):
            SB = [0, 1, 6, 7]

            def emit_scores(j, tc=tc, hd=hd):
                for m in range(2):
                    b = SB[(j % 2) * 2 + m]
                    P.op("pe", lambda e, m=m, j=j, tc=tc, b=b, hd=hd: e.matmul(c.ps[b], K_all[64 * m:64 * m + 64, hd, j * 128:(j + 1) * 128],
                                                                            qrot[64 * m:64 * m + 64, tc * 512:(tc + 1) * 512], start=True, stop=True),
                         reads=[f"K{hd}.{j // 4}", f"qrot{tc}"], writes=[f"ps{b}"])

            emit_scores(0)
            for j in range(16):
                for m in range(2):
                    b = SB[(j % 2) * 2 + m]
                    E = Et[m][j % 2]
                    ek = f"E{m}.{j % 2}"
                    P.op("act", lambda e, E=E, b=b: e.activation(E, c.ps[b], AF.Exp, scale=scale), reads=[f"ps{b}"], writes=[ek])
                if j < 15:
                    emit_scores(j + 1)
                for m in range(2):
                    E = Et[m][j % 2]
                    ek = f"E{m}.{j % 2}"
                    P.op("pe", lambda e, m=m, j=j, E=E, hd=hd: e.matmul(c.ps[2 + 2 * m], V_all[:, j, hd * 128:(hd + 1) * 128], E, start=(j == 0), stop=(j == 15)),
                         reads=[ek, f"V{j}.{hd // 4}"], writes=[f"ps{2 + 2 * m}"])
                    P.op("pe", lambda e, m=m, j=j, E=E: e.matmul(c.ps[3 + 2 * m], c.ones, E, start=(j == 0), stop=(j == 15)),
                         reads=[ek, "ones"], writes=[f"ps{3 + 2 * m}"])
            P.op("dve", lambda e: e.reciprocal(r0, c.ps[3]), reads=["ps3"], writes=["t1"])
            P.op("dve", lambda e: e.reciprocal(r1, c.ps[5]), reads=["ps5"], writes=["t2"])
            P.op("dve", lambda e: e.tensor_tensor(a0, c.ps[2], r0, ALU.mult), reads=["ps2", "t1"], writes=["a0"])
            P.op("dve", lambda e: e.tensor_tensor(r1, c.ps[4], r1, ALU.mult), reads=["ps4", "t2"], writes=["t2"])
            P.op("dve", lambda e: e.scalar_tensor_tensor(oT, r1, c.lam[:, 1:2], a0, ALU.mult, ALU.add), reads=["t2", "a0", "lam"], writes=["oT"])
            P.op("act", lambda e: e.activation(sqb, oT, AF.Square), reads=["oT"], writes=["sqb"])
            P.op("pe", lambda e: e.matmul(c.ps[6], c.ones, sqb, start=True, stop=True), reads=["sqb", "ones"], writes=["ps6"])
            P.op("act", lambda e: e.activation(rs2, c.ps[6], AF.Sqrt, bias=c.epsb[:, 1:2], scale=1.0 / (128 * 0.64)), reads=["ps6", "epsb"], writes=["rs2"])
            P.op("dve", lambda e: e.reciprocal(rs2, rs2), reads=["rs2"], writes=["rs2"])
            P.op("dve", lambda e, tc=tc, hd=hd: e.scalar_tensor_tensor(cat[:, hd, tc * 512:(tc + 1) * 512], oT, c.vecs[:, V_SUBLN:V_SUBLN + 1], rs2, ALU.mult, ALU.mult),
                 reads=["oT", "rs2", "vecs"], writes=[f"cat{hd}.{tc}"])

    if c.stage == "l0C":
        P.barrier(lambda e: e.memset(c.lam[:, 2:3], 0.0))
        for kc in range(KC):
            P.op("dve", lambda e, kc=kc: e.tensor_copy(c.h[:, kc, :], cat[:, kc, :]), writes=[f"h{kc}.0", f"h{kc}.1"])
        return
    P.barrier(lambda e: e.memset(c.lam[:, 2:3], 0.0))
    for db in range(16):
        slab, skeys = wload(c, c.w_eo[db])
        P.op("sp", lambda e, db=db: e.dma_start(out=c.h[:, db, :], in_=c.xT[:, db, 0:T]), writes=[f"h{db}.0", f"h{db}.1"], dma_key=f"x{db}")
        for tc in range(2):
            bank = (db * 2 + tc) % 4
            proj_fm(c, slab, skeys, lambda kc, tc=tc: cat[:, kc, tc * 512:(tc + 1) * 512], lambda kc, tc=tc: [f"cat{kc}.{tc}"], c.ps[bank], f"ps{bank}")
            hv = c.h[:, db, tc * 512:(tc + 1) * 512]
            P.op("dve", lambda e, hv=hv, bank=bank: e.tensor_tensor(hv, hv, c.ps[bank], ALU.add), reads=[f"ps{bank}", f"h{db}.{tc}"], writes=[f"h{db}.{tc}"])


def ffn(c, l):
    P, sb = c.P, c.sb
    P.barrier(lambda e: e.memset(c.lam[:, 2:3], 0.0))
    tmp_off = 192
    gcol = V_FFNG0 if l == 0 else V_FFNG1
    rms_to_hn(c, lambda kc, tc: c.h[:, kc, tc * 512:(tc + 1) * 512], lambda kc, tc: [f"h{kc}.{tc}"], gcol, 2,
              lambda kc, tc: c.hn[:, kc, tc * 512:(tc + 1) * 512], lambda kc, tc: f"hn{kc}.{tc}", tmp_off)
    act = sb(96, 44 * 1024, BF16, "p (j t) -> p j t", j=22)
    sg = [sb(tmp_off + 4 + 2 * i, 2048, F32) for i in range(4)]
    gi = 0
    for half in range(2):
        for hb in range(22):
            sl_g, kg = wload(c, c.w_g[l, half * 22 + hb])
            sl_u, ku = wload(c, c.w_up[l, half * 22 + hb])
            for tc in range(2):
                bg = (gi % 2) * 4 + tc * 2
                bu = bg + 1
                rf = lambda kc, tc=tc: c.hn[:, kc, tc * 512:(tc + 1) * 512]
                rk = lambda kc, tc=tc: [f"hn{kc}.{tc}"]
                proj_fm(c, sl_g, kg, rf, rk, c.ps[bg], f"ps{bg}")
                proj_fm(c, sl_u, ku, rf, rk, c.ps[bu], f"ps{bu}")
                s = sg[(gi * 2 + tc) % 4]
                sk = f"sg{(gi * 2 + tc) % 4}"
                P.op("act", lambda e, s=s, bg=bg: e.activation(s, c.ps[bg], AF.Silu), reads=[f"ps{bg}"], writes=[sk])
                P.op("dve", lambda e, s=s, bu=bu, hb=hb, tc=tc: e.tensor_tensor(act[:, hb, tc * 512:(tc + 1) * 512], s, c.ps[bu], ALU.mult),
                     reads=[sk, f"ps{bu}"], writes=[f"act{hb}.{tc}"])
            gi += 1
        for db in range(16):
            wl = []
            for part in range(2):
                slab, sk_ = wload(c, c.w_dn[l, half, db, :, part * 11:(part + 1) * 11, :], nk=11)
                wl.append((slab, sk_))
            for tc in range(2):
                bank = (db * 2 + tc) % 4 if (gi % 2 == 0) else 4 + (db * 2 + tc) % 4
                ps, psk = c.ps[bank], f"ps{bank}"
                pairs = []
                reads = []
                for part in range(2):
                    slab, sk_ = wl[part]
                    reads += sk_
                    for jj in range(11):
                        j = part * 11 + jj
                        pairs.append((slab[:, jj, :], act[:, j, tc * 512:(tc + 1) * 512]))
                        reads.append(f"act{j}.{tc}")

                def fn(e, pairs=pairs, ps=ps):
                    ins = None
                    n = len(pairs)
                    for i, (lh, r) in enumerate(pairs):
                        ins = e.matmul(ps, lh, r, start=(i == 0), stop=(i == n - 1))
                    return ins
                P.op("pe", fn, reads=reads, writes=[psk])
                hv = c.h[:, db, tc * 512:(tc + 1) * 512]
                P.op("dve", lambda e, hv=hv, ps=ps: e.tensor_tensor(hv, hv, ps, ALU.add), reads=[psk, f"h{db}.{tc}"], writes=[f"h{db}.{tc}"])


def layer1_mixer(c):
    P, nc, sb = c.P, c.nc, c.sb
    P.barrier(lambda e: e.memset(c.lam[:, 2:3], 0.0))
    rms_to_hn(c, lambda kc, tc: c.h[:, kc, tc * 512:(tc + 1) * 512], lambda kc, tc: [f"h{kc}.{tc}"], V_MIXG1, 2,
              lambda kc, tc: c.hn[:, kc, tc * 512:(tc + 1) * 512], lambda kc, tc: f"hn{kc}.{tc}", 192)
    P.barrier(lambda e: e.memset(c.lam[:, 2:3], 0.0))
    y = sb(96, 32 * 1024, BF16, "p (k t) -> p k t", k=KC)
    A = [sb(128 + 4 * i, 4096, F32) for i in range(4)] + [sb(147 + 4 * i, 4096, F32) for i in range(3)]
    QB = sb(192, 2048, BF16)
    KB = sb(194, 2048, BF16)
    KBT = sb(196, 4096, BF16, "p (c k) -> p c k", c=16)
    VT = sb(200, 4096, BF16, "p (c k) -> p c k", c=16)
    SCT = sb(155, 2048, BF16, "p (c t) -> p c t", c=16)
    msk = sb(157, 1024, BF16)
    TB = [sb(158 + 0.5 * i, 512, F32) for i in range(4)]
    SBFR = [sb(153.5 + 0.25 * i, 256, BF16) for i in range(4)]
    S32 = sb(153, 512, F32)
    EBL = sb(146.5, 64, F32)
    LBV = sb(146.5625, 4 * 64, F32, "p (a h) -> p a h", a=4)
    IT = A[5]
    ITb = sb(147 + 4 * 1, 2048, BF16)
    P.op("dve", lambda e: e.memset(msk, 1.0), writes=["msk"])
    P.op("dve", lambda e: e.memset(msk.rearrange("p (c t) -> p c t", t=64)[:, :, 0:1], 0.0), reads=["msk"], writes=["msk"])
    for ld in range(2):
        r0c = V_LB + (ld * 2 + 0) * 16
        r1c = V_LB + (ld * 2 + 1) * 16
        P.op("dve", lambda e, ld=ld, r0c=r0c, r1c=r1c: e.tensor_tensor(LBV[:, 2 * ld, :], c.vecs[:, r1c:r1c + 16], c.vecs[:, r0c:r0c + 16], ALU.subtract),
             reads=["vecs", "LBV"], writes=["LBV"])
        P.op("act", lambda e, ld=ld: e.activation(LBV[:, 2 * ld, :], LBV[:, 2 * ld, :], AF.Sigmoid), reads=["LBV"], writes=["LBV"])
        P.op("dve", lambda e, ld=ld: e.tensor_scalar(LBV[:, 2 * ld + 1, :], LBV[:, 2 * ld, :], -1.0, 1.0, ALU.mult, ALU.add), reads=["LBV"], writes=["LBV"])

    def proj2(wslab, dst_fn, func, dkey, scale=None):
        slab, skeys = wload(c, wslab)
        for tc in range(2):
            bank = tc
            proj_fm(c, slab, skeys, lambda kc, tc=tc: c.hn[:, kc, tc * 512:(tc + 1) * 512], lambda kc, tc=tc: [f"hn{kc}.{tc}"], c.ps[bank], f"ps{bank}")
            P.op("act", lambda e, tc=tc, bank=bank: e.activation(dst_fn(tc), c.ps[bank], func), reads=[f"ps{bank}"], writes=[dkey])

    def head_pass(hd, ld):
        lb = LBV[:, 2 * ld, hd:hd + 1]
        oml = LBV[:, 2 * ld + 1, hd:hd + 1]
        sq, fv, kk, lf, bb = A[0], A[1], A[2], A[3], A[4]
        proj2(c.w_hin[0, hd], lambda tc: sq[:, tc * 512:(tc + 1) * 512], AF.Silu, "A0")
        proj2(c.w_hin[1 + ld, hd], lambda tc: fv[:, tc * 512:(tc + 1) * 512], AF.Sigmoid, "A1")
        proj2(c.w_hin[3, hd], lambda tc: ITb[:, tc * 512:(tc + 1) * 512], AF.Copy, "A5")
        P.op("dve", lambda e: e.tensor_scalar(fv, fv, oml, lb, ALU.mult, ALU.add), reads=["A1", "LBV"], writes=["A1"])
        P.op("dve", lambda e: e.tensor_scalar(kk, fv, -1.0, 1.0, ALU.mult, ALU.add), reads=["A1"], writes=["A2"])
        P.op("act", lambda e: e.activation(lf, fv, AF.Ln), reads=["A1"], writes=["A3"])
        for tc in range(2):
            P.op("dve", lambda e, tc=tc: e.tensor_tensor_scan(bb[:, tc * 512:(tc + 1) * 512], msk, lf[:, tc * 512:(tc + 1) * 512], 0.0, ALU.mult, ALU.add),
                 reads=["A3", "msk"], writes=["A4"])
        b3 = bb.rearrange("p (c t) -> p c t", t=64)
        if ld == 1:
            P.op("dve", lambda e: e.tensor_tensor(lf, lf, bb, ALU.subtract), reads=["A3", "A4"], writes=["A3"])
            P.op("dve", lambda e: e.tensor_tensor(b3, lf.rearrange("p (c t) -> p c t", t=64), b3[:, :, 63:64].to_broadcast([128, 16, 64]), ALU.add),
                 reads=["A3", "A4"], writes=["A4"])
        eb, enb = A[1], A[3]
        P.op("act", lambda e: e.activation(eb, bb, AF.Exp), reads=["A4", "A1"], writes=["A1"])
        P.op("act", lambda e: e.activation(enb, bb, AF.Exp, scale=-1.0), reads=["A4", "A3"], writes=["A3"])
        P.op("dve", lambda e: e.tensor_tensor(QB, sq, eb, ALU.mult), reads=["A0", "A1"], writes=["QB"])
        P.op("dve", lambda e: e.tensor_tensor(KB, kk, enb, ALU.mult), reads=["A2", "A3"], writes=["KB"])
        e3 = eb.rearrange("p (c t) -> p c t", t=64)
        edge = 63 if ld == 0 else 0
        P.op("dve", lambda e: e.tensor_copy(EBL, e3[:, :, edge]), reads=["A1"], writes=["EBL"])
        for src, skey, dst, dkey in ((ITb, "A5", VT, "VT"), (KB, "KB", KBT, "KBT")):
            for half in range(2):
                bank = 4 + half
                pst = c.ps[bank].bitcast(BF16).rearrange("p (c k) -> p c k", k=128)

                def fn(e, src=src, half=half, pst=pst):
                    ins = None
                    for cl in range(8):
                        ch = half * 8 + cl
                        ins = e.transpose(pst[0:64, cl, :], src[:, ch * 64:(ch + 1) * 64], c.ident)
                    return ins
                P.op("pe", fn, reads=[skey, "ident"], writes=[f"ps{bank}"])
                P.op("act" if half == 0 else "dve",
                     (lambda e, dst=dst, half=half, pst=pst: e.copy(dst[0:64, half * 8:(half + 1) * 8, :], pst[0:64, :, :])) if half == 0 else
                     (lambda e, dst=dst, half=half, pst=pst: e.tensor_copy(dst[0:64, half * 8:(half + 1) * 8, :], pst[0:64, :, :])),
                     reads=[f"ps{bank}"], writes=[dkey])
        msk2 = (c.tri if ld == 0 else c.triT)[0:64, 0:64]
        for half in range(2):
            bank = 6 + half
            psv = c.ps[bank].rearrange("p (c t) -> p c t", t=64)

            def fn(e, half=half, psv=psv):
                ins = None
                for cl in range(8):
                    ch = half * 8 + cl
                    ins = e.matmul(psv[0:64, cl, :], KB[:, ch * 64:(ch + 1) * 64], QB[:, ch * 64:(ch + 1) * 64], start=True, stop=True)
                return ins
            P.op("pe", fn, reads=["KB", "QB"], writes=[f"ps{bank}"])
            P.op("dve", lambda e, half=half, psv=psv: e.tensor_tensor(SCT[0:64, half * 8:(half + 1) * 8, :], psv[0:64, :, :],
                                                                     msk2.unsqueeze(1).to_broadcast([64, 8, 64]), ALU.mult),
                 reads=[f"ps{bank}", "tri", "triT"], writes=["SCT"])
        order = list(range(16)) if ld == 0 else list(range(15, -1, -1))
        for q4 in range(4):
            ubank = 4 + q4

            def fu(e, q4=q4, ubank=ubank):
                ins = None
                for i4 in range(4):
                    ch = order[q4 * 4 + i4]
                    ins = e.matmul(c.ps[ubank][:, i4 * 128:(i4 + 1) * 128], KBT[0:64, ch, :], VT[0:64, ch, :], start=True, stop=True)
                return ins
            P.op("pe", fu, reads=["KBT", "VT"], writes=[f"ps{ubank}"])
        if ld == 0:
            P.op("dve", lambda e: e.memset(TB[3], 0.0), writes=["TB3"])
        else:
            P.op("sp", lambda e: e.dma_start(out=TB[3], in_=c.cout[hd * 128:(hd + 1) * 128, :]), reads=["cout"], writes=["TB3"], dma_key="st0")
            P.op("sp", lambda e: e.dma_start(out=TB[2], in_=c.cout[2048 + hd * 128:2048 + (hd + 1) * 128, :]), reads=["cout"], writes=["TB2"], dma_key="st1")
            P.op("dve", lambda e: e.tensor_scalar(TB[3], TB[3], c.vecs[:, V_SEL0:V_SEL0 + 1], None, ALU.mult), reads=["TB3", "vecs"], writes=["TB3"])
            P.op("dve", lambda e: e.scalar_tensor_tensor(TB[3], TB[2], c.vecs[:, V_SEL1:V_SEL1 + 1], TB[3], ALU.mult, ALU.add), reads=["TB3", "TB2", "vecs"], writes=["TB3"])
        P.op("act", lambda e: e.copy(SBFR[3], TB[3]), reads=["TB3"], writes=["SBF3"])
        for n, ch in enumerate(order):
            un = c.ps[4 + n // 4][:, (n % 4) * 128:(n % 4 + 1) * 128]
            prev = (n - 1) % 4
            cur = n % 4
            if n == 0:
                P.op("dve", lambda e, un=un, prev=prev, cur=cur: e.tensor_tensor(TB[cur], TB[prev], un, ALU.add),
                     reads=[f"TB{prev}", f"ps{4 + n // 4}"], writes=[f"TB{cur}"])
            else:
                ep = EBL[:, order[n - 1]:order[n - 1] + 1]
                P.op("dve", lambda e, un=un, prev=prev, cur=cur, ep=ep: e.scalar_tensor_tensor(TB[cur], TB[prev], ep, un, ALU.mult, ALU.add),
                     reads=[f"TB{prev}", f"ps{4 + n // 4}", "EBL"], writes=[f"TB{cur}"])
            obank = 2 + (ch // 8)
            pso = c.ps[obank].rearrange("p (c t) -> p c t", t=64)[:, ch % 8, :]

            def fo(e, ch=ch, pso=pso, prev=prev):
                e.matmul(pso, VT[0:64, ch, :], SCT[0:64, ch, :], start=True, stop=False)
                return e.matmul(pso, SBFR[prev], QB[:, ch * 64:(ch + 1) * 64], start=False, stop=True)
            P.op("pe", fo, reads=["VT", "SCT", f"SBF{prev}", "QB"], writes=[f"ps{obank}"])
            ec = EBL[:, ch:ch + 1]
            if n < 15:
                P.op("act", lambda e, cur=cur, ec=ec: e.activation(SBFR[cur], TB[cur], AF.Copy, scale=ec), reads=[f"TB{cur}", "EBL"], writes=[f"SBF{cur}"])
            else:
                P.op("act", lambda e, cur=cur, ec=ec: e.activation(S32, TB[cur], AF.Copy, scale=ec), reads=[f"TB{cur}", "EBL"], writes=["S32"])
            if (n % 8) == 7:
                t0 = (ch // 8) * 512
                yv = y[:, hd, t0:t0 + 512]
                if ld == 0:
                    P.op("act", lambda e, yv=yv, obank=obank: e.copy(yv, c.ps[obank]), reads=[f"ps{obank}"], writes=[f"y{hd}.{ch // 8}"])
                else:
                    P.op("dve", lambda e, yv=yv, obank=obank: e.tensor_tensor(yv, yv, c.ps[obank], ALU.add), reads=[f"ps{obank}", f"y{hd}.{ch // 8}"], writes=[f"y{hd}.{ch // 8}"])
        if ld == 0:
            P.op("sp", lambda e: e.dma_start(out=c.cin[hd * 128:(hd + 1) * 128, :], in_=S32), reads=["S32"], writes=["cin"], dma_key=f"ci{hd % 4}")

    for hd in range(16):
        head_pass(hd, 0)
    P.op("pool", lambda e: e.collective_compute("AllGather", ALU.bypass, [[0, 1], [2, 3], [4, 5], [6, 7]], [c.cin.opt()], [c.cout.opt()]),
         reads=["cin"], writes=["cout"], dma_key="cc", dma_inc=1)
    for hd in range(16):
        head_pass(hd, 1)

    P.barrier(lambda e: e.memset(c.lam[:, 2:3], 0.0))
    sqy = [sb(192 + i, 1024, BF16) for i in range(2)]
    rstd2 = A[0]
    for tc in range(2):
        ps, psk = c.ps[6 + tc], f"ps{6 + tc}"
        for hd in range(16):
            s_ = sqy[hd % 2]
            sk = f"sqy{hd % 2}"
            P.op("act", lambda e, s_=s_, hd=hd, tc=tc: e.activation(s_, y[:, hd, tc * 512:(tc + 1) * 512], AF.Square), reads=[f"y{hd}.{tc}"], writes=[sk])
            P.op("pe", lambda e, s_=s_, hd=hd, ps=ps: e.matmul(ps, c.ones, s_, start=(hd == 0), stop=(hd == 15)), reads=[sk, "ones"], writes=[psk])
        rv = rstd2[:, tc * 512:(tc + 1) * 512]
        P.op("act", lambda e, ps=ps, rv=rv: e.activation(rv, ps, AF.Sqrt, bias=c.epsb[:, 0:1], scale=1.0 / D), reads=[psk, "epsb"], writes=["A0"])
        P.op("dve", lambda e, rv=rv: e.reciprocal(rv, rv), reads=["A0"], writes=["A0"])
    sgt = [sb(128 + 4 + 2 * i, 2048, F32) for i in range(2)]
    ytmp = sb(128 + 8, 2048, F32)
    for hd in range(16):
        slab, skeys = wload(c, c.w_hin[4, hd])
        for tc in range(2):
            bank = tc
            proj_fm(c, slab, skeys, lambda kc, tc=tc: c.hn[:, kc, tc * 512:(tc + 1) * 512], lambda kc, tc=tc: [f"hn{kc}.{tc}"], c.ps[bank], f"ps{bank}")
            sg_ = sgt[tc]
            P.op("act", lambda e, sg_=sg_, bank=bank: e.activation(sg_, c.ps[bank], AF.Silu), reads=[f"ps{bank}"], writes=[f"sgt{tc}"])
            yv = y[:, hd, tc * 512:(tc + 1) * 512]
            P.op("dve", lambda e, yv=yv, hd=hd, tc=tc: e.scalar_tensor_tensor(ytmp, yv, c.vecs[:, V_GNORM + hd:V_GNORM + hd + 1], rstd2[:, tc * 512:(tc + 1) * 512], ALU.mult, ALU.mult),
                 reads=[f"y{hd}.{tc}", "A0", "vecs"], writes=["ytmp"])
            P.op("dve", lambda e, yv=yv, sg_=sg_: e.tensor_tensor(yv, ytmp, sg_, ALU.mult), reads=["ytmp", f"sgt{tc}"], writes=[f"y{hd}.{tc}"])
    for db in range(16):
        slab, skeys = wload(c, c.w_ho[db])
        for tc in range(2):
            bank = 2 + (db * 2 + tc) % 4
            proj_fm(c, slab, skeys, lambda kc, tc=tc: y[:, kc, tc * 512:(tc + 1) * 512], lambda kc, tc=tc: [f"y{kc}.{tc}"], c.ps[bank], f"ps{bank}")
            hv = c.h[:, db, tc * 512:(tc + 1) * 512]
            P.op("dve", lambda e, hv=hv, bank=bank: e.tensor_tensor(hv, hv, c.ps[bank], ALU.add), reads=[f"ps{bank}", f"h{db}.{tc}"], writes=[f"h{db}.{tc}"])


def final_norm(c):
    P, sb = c.P, c.sb
    P.barrier(lambda e: e.memset(c.lam[:, 2:3], 0.0))
    rms_to_hn(c, lambda kc, tc: c.h[:, kc, tc * 512:(tc + 1) * 512], lambda kc, tc: [f"h{kc}.{tc}"], V_FING, 2,
              lambda kc, tc: c.h[:, kc, tc * 512:(tc + 1) * 512], lambda kc, tc: f"h{kc}.{tc}", 192)
    for kc in range(KC):
        P.op("sp", lambda e, kc=kc: e.dma_start(out=c.yT[:, kc, :], in_=c.h[:, kc, :]),
             reads=[f"h{kc}.0", f"h{kc}.1"], writes=[f"yo{kc}"], dma_key=f"out{kc % 8}")


def host_prep(inp):
    f32 = np.float32
    x = inp["x"]
    shared = {}
    W = inp["even_w_in"][0]
    shared["w_q"] = slabify(W[:, 0:1024], 128)
    shared["w_k"] = slabify(W[:, 1024:2048], 128)
    shared["w_v"] = slabify(W[:, 2048:3072], 512)
    shared["w_u"] = slabify(W[:, 3072:4096], 128)
    shared["w_vb"] = slabify(W[:, 4096:5120], 512)
    shared["w_eo"] = slabify(inp["even_w_out"][0], 128)
    shared["w_g"] = np.stack([slabify(inp["ffn_w_gate"][l], 128) for l in range(2)])
    shared["w_up"] = np.stack([slabify(inp["ffn_w_up"][l], 128) for l in range(2)])
    wd = inp["ffn_w_down"]
    shared["w_dn"] = np.ascontiguousarray(wd.reshape(2, 2, 22, 128, 16, 128).transpose(0, 1, 4, 3, 2, 5))
    shared["w_ho"] = slabify(inp["hgrn_w_out"][0], 128)
    Wh = inp["hgrn_w_in"][0]
    hin = [slabify(Wh[:, i * 2048:(i + 1) * 2048], 128) for i in range(5)]
    hin_even = np.stack([hin[0], hin[1], hin[2], hin[3], hin[4]])
    hin_odd = np.stack([hin[0], hin[2], hin[1], hin[3], hin[4]])
    consts = np.zeros((128, 4, 128), f32)
    consts[:, 0, :] = np.eye(128, dtype=f32)
    consts[:, 1, :] = 1.0
    pm = np.zeros((128, 128), f32)
    for base in (0, 64):
        for d in range(8):
            pm[base + d + 8, base + d] = 1.0
            pm[base + d, base + d + 8] = 1.0
    consts[:, 2, :] = pm
    consts[:, 3, :] = np.triu(np.ones((128, 128), f32))
    half = 8
    invf_vals = (500000.0 ** (-np.arange(half, dtype=np.float64) / half)).astype(f32)
    invf = np.zeros(128, f32)
    sgn = np.zeros(128, f32)
    for base in (0, 64):
        invf[base:base + 8] = invf_vals
        invf[base + 8:base + 16] = invf_vals
        sgn[base:base + 8] = -1.0
        sgn[base + 8:base + 16] = 1.0
    lbraw = inp["hgrn_lower_bounds"]
    in_maps = []
    for core in range(NCORES):
        b, hf = core // 2, core % 2
        own = np.arange(T) if hf == 0 else (TA - 1 - np.arange(T))
        oth = (TA - 1 - np.arange(T)) if hf == 0 else np.arange(T)
        tok = np.concatenate([own, oth])
        xT = np.ascontiguousarray(x[b][tok, :].T.reshape(KC, 128, TA).transpose(1, 0, 2))
        vecs = np.zeros((128, NV), f32)
        vecs[:, V_MIXG0:V_MIXG0 + 16] = fm_vec(inp["mix_norm"][0])
        vecs[:, V_FFNG0:V_FFNG0 + 16] = fm_vec(inp["ffn_norm"][0])
        vecs[:, V_MIXG1:V_MIXG1 + 16] = fm_vec(inp["mix_norm"][1])
        vecs[:, V_FFNG1:V_FFNG1 + 16] = fm_vec(inp["ffn_norm"][1])
        vecs[:, V_FING:V_FING + 16] = fm_vec(inp["final_norm"])
        vecs[:, V_INVF] = invf
        vecs[:, V_SGN] = sgn
        vecs[:, V_SUBLN] = inp["diff_subln"][0]
        vecs[:, V_SEL0] = 1.0 if hf == 1 else 0.0
        vecs[:, V_SEL1] = 1.0 if hf == 0 else 0.0
        dirs = (0, 1) if hf == 0 else (1, 0)
        for ld in range(2):
            for layer in range(2):
                vecs[:, V_LB + (ld * 2 + layer) * 16:V_LB + (ld * 2 + layer + 1) * 16] = fm_vec(lbraw[dirs[ld], layer])
        vecs[:, V_GNORM:V_GNORM + 16] = fm_vec(inp["hgrn_g_norm"][0])
        rows = np.zeros((1, NR), f32)
        rows[0, R_LNG:R_LNG + 1024] = inp["gmlp_ln_g"][0]
        rows[0, R_LNB:R_LNB + 1024] = inp["gmlp_ln_b"][0]
        ws = inp["gmlp_w_s"][0]
        bs = inp["gmlp_b_s"][0]
        if hf == 1:
            ws = ws[:, ::-1, ::-1]
            bs = bs[:, ::-1]
        rows[0, R_BSB:R_BSB + 1024] = bs.reshape(-1)
        rows[0, R_LQ:R_LQ + 256] = np.concatenate([inp["diff_lq1"][0], inp["diff_lk1"][0], inp["diff_lq2"][0], inp["diff_lk2"][0]])
        m = dict(shared)
        m["xT"] = xT
        m["pos"] = tok.astype(f32).reshape(1, TA)
        m["vecs"] = vecs
        m["rows"] = rows
        m["consts"] = consts
        m["w_s"] = np.ascontiguousarray(ws.transpose(2, 0, 1))
        m["w_hin"] = hin_even if hf == 0 else hin_odd
        in_maps.append(m)
    return in_maps


def assemble(results):
    out = np.zeros((4, TA, D), np.float32)
    for core in range(NCORES):
        b, hf = core // 2, core % 2
        yT = results[core]["yT"]
        y = yT.transpose(2, 1, 0).reshape(T, D)
        if hf == 0:
            out[b, 0:T] = y
        else:
            out[b, T:TA] = y[::-1]
    return out


def kernel(**inputs):
    inp = {k: np.asarray(v) for k, v in inputs.items()}
    in_maps = host_prep(inp)
    nc = build("full")
    res = run_bass_kernel_spmd(nc, in_maps, core_ids=list(range(NCORES)))
    return assemble(res.results)
```

```python
import contextlib
import numpy as np
import concourse.bass as bass
import concourse.mybir as mybir
from concourse.bass_utils import run_bass_kernel_spmd

F32 = mybir.dt.float32
BF16 = mybir.dt.bfloat16
AF = mybir.ActivationFunctionType
ALU = mybir.AluOpType
AX = mybir.AxisListType

ENGS = ("pe", "act", "dve", "pool", "sp")
DEBUG_TAGS = False
INS_TAGS = {}
D = 2048
KC = 16
T = 1024
TA = 2048
FH = 5632
EPS = 1e-6
NCORES = 8


class Op:
    __slots__ = ("eng", "fn", "deps", "signal", "count", "dma_key", "dma_cum", "dma_inc", "idx", "tag")

    def __init__(self, eng, fn, dma_key, dma_inc):
        self.eng = eng
        self.fn = fn
        self.deps = []
        self.signal = False
        self.count = 0
        self.dma_key = dma_key
        self.dma_cum = 0
        self.dma_inc = dma_inc
        self.idx = 0


class Prog:
    def __init__(self):
        self.ops = []
        self.last_w = {}
        self.readers = {}
        self.dma_cnt = {}
        self.last_on = {}
        self.bar = None
        self.bar_seen = set()
        self.hook = None
        self.in_hook = False

    def op(self, eng, fn, reads=(), writes=(), dma_key=None, dma_inc=16):
        o = Op(eng, fn, dma_key, dma_inc)
        o.idx = len(self.ops)
        if DEBUG_TAGS:
            import sys as _s
            f = _s._getframe(1)
            o.tag = f"{f.f_lineno}<{f.f_back.f_lineno}<{f.f_back.f_back.f_lineno if f.f_back.f_back else 0} w={list(writes)[:3]}"
        deps = set()
        for r in reads:
            w = self.last_w.get(r)
            if w is not None:
                deps.add(w)
            if r.startswith("ps"):
                for rd in self.readers.get(r, ()):
                    if rd.eng != eng:
                        deps.add(rd)
        for wkey in writes:
            w = self.last_w.get(wkey)
            if w is not None:
                deps.add(w)
            for rd in self.readers.get(wkey, ()):
                deps.add(rd)
        if dma_key is not None:
            prev = self.last_on.get("dma:" + dma_key)
            if prev is not None:
                deps.add(prev)
        if self.bar is not None and eng not in self.bar_seen:
            deps.add(self.bar)
            self.bar_seen.add(eng)
        for d in deps:
            if d.dma_key is None and d.eng == "pe" and eng == "pe" and dma_key is None:
                continue
            o.deps.append(d)
            if d.dma_key is None:
                d.signal = True
        for r in reads:
            self.readers.setdefault(r, []).append(o)
        for wkey in writes:
            self.last_w[wkey] = o
            self.readers[wkey] = []
        if dma_key is not None:
            self.dma_cnt[dma_key] = self.dma_cnt.get(dma_key, 0) + dma_inc
            o.dma_cum = self.dma_cnt[dma_key]
            self.last_on["dma:" + dma_key] = o
        else:
            self.last_on[eng] = o
        self.ops.append(o)
        if self.hook is not None and not self.in_hook:
            self.hook()
        return o

    def barrier(self, nopfn):
        o = Op("dve", nopfn, None, 16)
        o.idx = len(self.ops)
        for k, d in self.last_on.items():
            if d is None:
                continue
            if d.dma_key is None and d.eng == "dve":
                continue
            o.deps.append(d)
            if d.dma_key is None:
                d.signal = True
        self.last_on["dve"] = o
        self.ops.append(o)
        self.bar = o
        self.bar_seen = {"dve"}
        self.last_w = {}
        self.readers = {}
        return o

    def emit(self, nc, final_dma_keys=()):
        cnt = {e: 0 for e in ENGS}
        for o in self.ops:
            if o.dma_key is None and o.signal:
                cnt[o.eng] += 1
                o.count = cnt[o.eng]
        per_eng = {e: [] for e in ENGS}
        for o in self.ops:
            per_eng[o.eng].append(o)
        dma_keys = sorted(self.dma_cnt.keys())
        with contextlib.ExitStack() as st:
            sems = {}
            for e in ENGS:
                sems[e] = st.enter_context(nc.semaphore("s_" + e))
            for k in dma_keys:
                sems["dma_" + k] = st.enter_context(nc.semaphore("d_" + k))
            block = st.enter_context(nc.Block())

            def run(engname, engobj):
                waited = {}
                for o in per_eng[engname]:
                    need = {}
                    for d in o.deps:
                        if d.dma_key is not None:
                            s, v = "dma_" + d.dma_key, d.dma_cum
                        else:
                            s, v = d.eng, d.count
                        if v > need.get(s, 0):
                            need[s] = v
                    for s, v in need.items():
                        if waited.get(s, 0) >= v:
                            continue
                        engobj.wait_ge(sems[s], v)
                        waited[s] = v
                    ins = o.fn(engobj)
                    if DEBUG_TAGS:
                        try:
                            INS_TAGS[ins.ins.name] = o.tag
                        except Exception:
                            pass
                    if o.dma_key is not None:
                        ins.then_inc(sems["dma_" + o.dma_key], o.dma_inc)
                    elif o.signal:
                        ins.then_inc(sems[o.eng], 1)
                if engname == "sp":
                    for k in final_dma_keys:
                        engobj.wait_ge(sems["dma_" + k], self.dma_cnt[k])

            @block.tensor
            def _(e):
                run("pe", e)

            @block.scalar
            def _(e):
                run("act", e)

            @block.vector
            def _(e):
                run("dve", e)

            @block.gpsimd
            def _(e):
                run("pool", e)

            @block.sync
            def _(e):
                run("sp", e)


def slabify(W, ncols):
    K, N = W.shape
    return np.ascontiguousarray(W.reshape(K // 128, 128, N // ncols, ncols).transpose(2, 1, 0, 3))


def fm_vec(v):
    return np.ascontiguousarray(v.reshape(-1, 128).T)


V_MIXG0, V_FFNG0, V_MIXG1, V_FFNG1, V_FING = 0, 16, 32, 48, 64
V_INVF, V_SGN, V_SUBLN, V_SEL0, V_SEL1 = 80, 81, 82, 83, 84
V_LB = 88
V_GNORM = 152
NV = 168
R_LNG, R_LNB, R_BSB, R_LQ = 0, 1024, 2048, 3072
NR = 3072 + 256


class Ctx:
    pass


def build(stage="full"):
    nc = bass.Bass("TRN2", target_bir_lowering=False)
    P = Prog()
    c = Ctx()
    c.nc, c.P = nc, P
    dt_in = lambda name, shape: nc.dram_tensor(name, list(shape), F32, kind="ExternalInput").ap()
    c.xT = dt_in("xT", [128, KC, TA])
    c.pos = dt_in("pos", [1, TA])
    c.vecs_d = dt_in("vecs", [128, NV])
    c.rows_d = dt_in("rows", [1, NR])
    c.consts_d = dt_in("consts", [128, 4, 128])
    c.w_q = dt_in("w_q", [8, 128, KC, 128])
    c.w_k = dt_in("w_k", [8, 128, KC, 128])
    c.w_v = dt_in("w_v", [2, 128, KC, 512])
    c.w_u = dt_in("w_u", [8, 128, KC, 128])
    c.w_vb = dt_in("w_vb", [2, 128, KC, 512])
    c.w_s = dt_in("w_s", [128, 8, 128])
    c.w_eo = dt_in("w_eo", [16, 128, KC, 128])
    if stage in ("full", "l0", "ffn0"):
        c.w_g = dt_in("w_g", [2, 44, 128, KC, 128])
        c.w_up = dt_in("w_up", [2, 44, 128, KC, 128])
        c.w_dn = dt_in("w_dn", [2, 2, 16, 128, 22, 128])
    if stage in ("full", "l1"):
        c.w_hin = dt_in("w_hin", [5, 16, 128, KC, 128])
        c.w_ho = dt_in("w_ho", [16, 128, KC, 128])
    c.yT = nc.dram_tensor("yT", [128, KC, T], F32, kind="ExternalOutput").ap()
    c.cin = nc.dram_tensor("cin", [16 * 128, 128], F32).ap()
    c.cout = nc.dram_tensor("cout", [2 * 16 * 128, 128], F32).ap()

    ARENA_KB = 204
    arena = nc.alloc_sbuf_tensor("arena", [128, ARENA_KB * 512], BF16).ap()

    def sb(off_kb, nbytes, dtype, pattern=None, **kw):
        a = int(round(off_kb * 512))
        n = nbytes // 2
        assert a + n <= ARENA_KB * 512, (off_kb, nbytes)
        v = arena[:, a:a + n]
        if dtype == F32:
            v = v.bitcast(F32)
        if pattern:
            v = v.rearrange(pattern, **kw)
        return v

    c.sb = sb
    c.ps = [nc.alloc_psum_tensor(f"ps{i}", [128, 512], F32).ap() for i in range(8)]
    c.h = sb(0, 64 * 1024, F32, "p (k t) -> p k t", k=KC)
    c.hn = sb(64, 32 * 1024, BF16, "p (k t) -> p k t", k=KC)
    c.vecs = sb(144, NV * 4, F32)
    c.ident = sb(144.75, 256, BF16)
    c.ones = sb(145.0, 256, BF16)
    c.pm = sb(145.25, 256, BF16)
    c.tri = sb(145.5, 256, BF16)
    c.triT = sb(145.75, 256, BF16)
    c.lam = sb(146.0, 16, F32)
    c.epsb = sb(146.0625, 16, F32)
    c.cst32 = sb(200, 4 * 512, F32, "p (a b) -> p a b", a=4)
    c.wslot = [sb(160 + 4 * i, 4096, BF16, "p (k n) -> p k n", k=KC) for i in range(8)]
    c.wbig = [sb(160 + 16 * i, 16384, BF16, "p (k n) -> p k n", k=KC) for i in range(2)]
    c.wcnt = 0
    c.bigcnt = 0
    c.nring = 8

    load_consts(c)
    c.stage = stage
    if stage == "l1":
        for kc in range(KC):
            P.op("sp", lambda e, kc=kc: e.dma_start(out=c.h[:, kc, :], in_=c.xT[:, kc, 0:T]), writes=[f"h{kc}.0", f"h{kc}.1"], dma_key=f"x{kc}")
        layer1_mixer(c)
    elif stage == "ffn0":
        for kc in range(KC):
            P.op("sp", lambda e, kc=kc: e.dma_start(out=c.h[:, kc, :], in_=c.xT[:, kc, 0:T]), writes=[f"h{kc}.0", f"h{kc}.1"], dma_key=f"x{kc}")
        ffn(c, 0)
    elif stage in ("full", "l0") or stage.startswith("l0"):
        layer0_mixer(c)
        if stage in ("full", "l0"):
            ffn(c, 0)
    if stage in ("full",):
        layer1_mixer(c)
        ffn(c, 1)
        final_norm(c)
    else:
        P.barrier(lambda e: e.memset(c.lam[:, 2:3], 0.0))
        for kc in range(KC):
            P.op("sp", lambda e, kc=kc: e.dma_start(out=c.yT[:, kc, :], in_=c.h[:, kc, :]),
                 reads=[f"h{kc}.0", f"h{kc}.1"], writes=[f"y{kc}"], dma_key=f"out{kc % 8}")
    P.emit(nc, final_dma_keys=[f"out{i}" for i in range(8)])
    return nc


def wload(c, dram_slab, big=False, nk=KC):
    P = c.P
    if big:
        i = c.bigcnt % 2
        c.bigcnt += 1
        ap = c.wbig[i]
        keys = [f"ws{4 * i + j}" for j in range(4)]
        dk = f"wb{i}"
    else:
        i = c.wcnt % c.nring
        c.wcnt += 1
        ap = c.wslot[i]
        keys = [f"ws{i}"]
        dk = f"w{i}"
    dst = ap if nk == KC else ap[:, 0:nk, :]
    P.op("pool", lambda e: e.dma_start(out=dst, in_=dram_slab), writes=keys, dma_key=dk)
    return ap, keys


def load_consts(c):
    P, nc = c.P, c.nc
    P.op("dve", lambda e: e.memset(c.epsb[:, 0:1], EPS), writes=["epsb"])
    P.op("dve", lambda e: e.memset(c.epsb[:, 1:2], EPS / 0.64), reads=["epsb"], writes=["epsb"])
    P.op("sp", lambda e: e.dma_start(out=c.vecs, in_=c.vecs_d), writes=["vecs"], dma_key="c")
    P.op("sp", lambda e: e.dma_start(out=c.cst32, in_=c.consts_d), writes=["cst32"], dma_key="c")
    for i, (ap, nm) in enumerate([(c.ident, "ident"), (c.ones, "ones"), (c.pm, "pm"), (c.tri, "tri")]):
        P.op("dve", lambda e, ap=ap, i=i: e.tensor_copy(ap, c.cst32[:, i, :]), reads=["cst32"], writes=[nm])
    psb = c.ps[7].bitcast(BF16)
    P.op("pe", lambda e: e.transpose(psb[:, 0:128], c.tri, c.ident), reads=["tri", "ident"], writes=["ps7"])
    P.op("dve", lambda e: e.tensor_copy(c.triT, psb[:, 0:128]), reads=["ps7"], writes=["triT"])


def rms_to_hn(c, src_fn, src_keys_fn, gcol, ntc, dst, dst_key_fn, tmp_off):
    P = c.P
    sq = [c.sb(tmp_off + i, 1024, BF16) for i in range(2)]
    rstd = c.sb(tmp_off + 2, 2048, F32)
    for tc in range(ntc):
        ps = c.ps[6 + (tc % 2)]
        psk = f"ps{6 + (tc % 2)}"
        for kc in range(KC):
            s = sq[kc % 2]
            sk = f"sq{kc % 2}"
            P.op("act", lambda e, s=s, kc=kc, tc=tc: e.activation(s, src_fn(kc, tc), AF.Square),
                 reads=src_keys_fn(kc, tc), writes=[sk])
            P.op("pe", lambda e, s=s, kc=kc, ps=ps: e.matmul(ps, c.ones, s, start=(kc == 0), stop=(kc == KC - 1)),
                 reads=[sk, "ones"], writes=[psk])
        P.op("act", lambda e, ps=ps: e.activation(rstd, ps, AF.Sqrt, bias=c.epsb[:, 0:1], scale=1.0 / D), reads=[psk, "epsb"], writes=["rstd"])
        P.op("dve", lambda e: e.reciprocal(rstd, rstd), reads=["rstd"], writes=["rstd"])
        for kc in range(KC):
            P.op("dve", lambda e, kc=kc, tc=tc: e.scalar_tensor_tensor(dst(kc, tc), src_fn(kc, tc), c.vecs[:, gcol + kc:gcol + kc + 1], rstd, ALU.mult, ALU.mult),
                 reads=src_keys_fn(kc, tc) + ["rstd", "vecs"], writes=[dst_key_fn(kc, tc)])


def proj_fm(c, slab, slab_keys, rhs_fn, rhs_keys_fn, ps, psk, nk=KC):
    pairs = [(slab[:, kc, :], rhs_fn(kc)) for kc in range(nk)]
    reads = list(slab_keys)
    for kc in range(nk):
        reads += rhs_keys_fn(kc)

    def fn(e):
        ins = None
        for i, (l, r) in enumerate(pairs):
            ins = e.matmul(ps, l, r, start=(i == 0), stop=(i == nk - 1))
        return ins
    c.P.op("pe", fn, reads=reads, writes=[psk])


def layer0_mixer(c):
    P, nc, sb = c.P, c.nc, c.sb
    hn_oth = sb(96, 32 * 1024, BF16, "p (k t) -> p k t", k=KC)
    cat = hn_oth
    K_all = sb(0, 32 * 1024, BF16, "p (h t) -> p h t", h=8)
    V_all = sb(32, 32 * 1024, BF16, "p (b n) -> p b n", b=16)
    Ctab = sb(128, 8192, F32)
    Stab = sb(136, 8192, F32)
    rows = sb(147, NR * 4, F32)
    tmp_off = 192
    P.op("sp", lambda e: e.dma_start(out=rows, in_=c.rows_d.partition_broadcast(128)), writes=["rows"], dma_key="c")
    posb = sb(160, 8192, F32)
    P.op("sp", lambda e: e.dma_start(out=posb, in_=c.pos.partition_broadcast(128)), writes=["ws0", "ws1"], dma_key="c")
    TWO_PI = 2.0 * np.pi
    C1 = 6.28125
    C2 = TWO_PI - C1
    invf = c.vecs[:, V_INVF:V_INVF + 1]
    ang = sb(168, 8192, F32)
    kf = sb(176, 8192, F32)
    ki = sb(184, 8192, F32).bitcast(mybir.dt.int32)
    PI_IN = 3.1415925

    def make_table(dst, shift, key):
        P.op("dve", lambda e: e.tensor_scalar(ang, posb, invf, shift, ALU.mult, ALU.add), reads=["ws0", "ws1", "vecs"], writes=["ang"])
        P.op("dve", lambda e: e.tensor_scalar(kf, ang, 1.0 / TWO_PI, None, ALU.mult), reads=["ang"], writes=["kf"])
        P.op("dve", lambda e: e.tensor_copy(ki, kf), reads=["kf"], writes=["ki"])
        P.op("dve", lambda e: e.tensor_copy(kf, ki), reads=["ki"], writes=["kf"])
        P.op("dve", lambda e: e.scalar_tensor_tensor(ang, kf, -C1, ang, ALU.mult, ALU.add), reads=["kf", "ang"], writes=["ang"])
        P.op("dve", lambda e: e.scalar_tensor_tensor(ang, kf, -C2, ang, ALU.mult, ALU.add), reads=["kf", "ang"], writes=["ang"])
        P.op("dve", lambda e: e.tensor_scalar(kf, ang, np.pi, -TWO_PI, ALU.is_gt, ALU.mult), reads=["ang"], writes=["kf"])
        P.op("dve", lambda e: e.tensor_tensor(ang, ang, kf, ALU.add), reads=["kf", "ang"], writes=["ang"])
        P.op("dve", lambda e: e.tensor_scalar(kf, ang, -np.pi, TWO_PI, ALU.is_lt, ALU.mult), reads=["ang"], writes=["kf"])
        P.op("dve", lambda e: e.tensor_tensor(ang, ang, kf, ALU.add), reads=["kf", "ang"], writes=["ang"])
        P.op("dve", lambda e: e.tensor_scalar(ang, ang, -PI_IN, PI_IN, ALU.max, ALU.min), reads=["ang"], writes=["ang"])
        P.op("act", lambda e: e.activation(dst, ang, AF.Sin), reads=["ang"], writes=[key])

    make_table(Stab, 0.0, "Stab")
    P.op("dve", lambda e: e.tensor_scalar(Stab, Stab, c.vecs[:, V_SGN:V_SGN + 1], None, ALU.mult), reads=["Stab", "vecs"], writes=["Stab"])
    make_table(Ctab, np.pi / 2, "Ctab")
    lq = rows[:, R_LQ:R_LQ + 256].rearrange("p (a b) -> p a b", a=4)
    lt = sb(tmp_off, 512, F32, "p (a b) -> p a b", a=2)
    P.op("dve", lambda e: e.tensor_tensor(lt[:, 0, :], lq[:, 0, :], lq[:, 1, :], ALU.mult), reads=["rows"], writes=["lt"])
    P.op("dve", lambda e: e.tensor_tensor(lt[:, 1, :], lq[:, 2, :], lq[:, 3, :], ALU.mult), reads=["rows", "lt"], writes=["lt"])
    P.op("dve", lambda e: e.reduce_sum(c.lam[:, 2:4], lt, AX.X), reads=["lt"], writes=["lam"])
    P.op("act", lambda e: e.activation(c.lam[:, 2:4], c.lam[:, 2:4], AF.Exp), reads=["lam"], writes=["lam"])
    P.op("dve", lambda e: e.tensor_tensor(c.lam[:, 0:1], c.lam[:, 2:3], c.lam[:, 3:4], ALU.subtract), reads=["lam"], writes=["lam"])
    P.op("dve", lambda e: e.tensor_scalar(c.lam[:, 1:2], c.lam[:, 0:1], 0.2, -1.0, ALU.add, ALU.mult), reads=["lam"], writes=["lam"])

    stage = c.h
    for half in range(2):
        for kc in range(KC):
            P.op("sp", lambda e, kc=kc, half=half: e.dma_start(out=stage[:, kc, :], in_=c.xT[:, kc, half * T:(half + 1) * T]),
                 writes=[f"h{kc}.0", f"h{kc}.1"], dma_key=f"x{kc}")
        dstT = c.hn if half == 0 else hn_oth
        dkey = "hn" if half == 0 else "ho"
        rms_to_hn(c, lambda kc, tc: stage[:, kc, tc * 512:(tc + 1) * 512], lambda kc, tc: [f"h{kc}.{tc}"], V_MIXG0, 2,
                  lambda kc, tc, dstT=dstT: dstT[:, kc, tc * 512:(tc + 1) * 512], lambda kc, tc, dkey=dkey: f"{dkey}{kc}.{tc}", tmp_off)

    def hn_all(kc, tcc):
        src = c.hn if tcc < 2 else hn_oth
        return src[:, kc, (tcc % 2) * 512:(tcc % 2 + 1) * 512]

    def hn_all_keys(kc, tcc):
        return [f"{'hn' if tcc < 2 else 'ho'}{kc}.{tcc % 2}"]

    if c.stage == "l0A":
        return
    if c.stage == "l0Aw":
        slab, skeys = wload(c, c.w_k[0])
        proj_fm(c, slab, skeys, lambda kc: c.hn[:, kc, 0:512], lambda kc: [f"hn{kc}.0"], c.ps[2], "ps2")
        P.op("dve", lambda e: e.tensor_copy(c.h[:, 0, 0:512], c.ps[2]), reads=["ps2"], writes=["h0.0"])
        return
    P.barrier(lambda e: e.memset(c.lam[:, 2:3], 0.0))
    if c.stage == "l0Abar":
        P.op("act", lambda e: e.copy(c.h[:, 0, 0:512], c.h[:, 1, 0:512]), reads=[], writes=["h0.0"])
        P.op("pe", lambda e: e.matmul(c.ps[2], c.ones, c.hn[:, 0, 0:512], start=True, stop=True), reads=[], writes=["ps2"])
        P.op("dve", lambda e: e.tensor_copy(c.h[:, 2, 0:512], c.ps[2]), reads=["ps2"], writes=["h2.0"])
        return
    evi = [0]
    for cb in range(2 if c.stage not in ("l0B1k", "l0B1kn", "l0B1r1", "l0B1r2") else 0):
        slab, skeys = wload(c, c.w_v[cb], big=True)
        for tb in range(16):
            src = c.hn if tb < 8 else hn_oth
            sk = "hn" if tb < 8 else "ho"
            tcl = (tb % 8) // 4
            bank = tb % 2
            ps, psk = c.ps[bank], f"ps{bank}"
            pairs = [(src[:, kc, (tb % 8) * 128:(tb % 8 + 1) * 128], slab[:, kc, :]) for kc in range(KC)]

            def fn(e, pairs=pairs, ps=ps):
                ins = None
                for i, (l, r) in enumerate(pairs):
                    ins = e.matmul(ps, l, r, start=(i == 0), stop=(i == KC - 1))
                return ins
            P.op("pe", fn, reads=skeys + [f"{sk}{kc}.{tcl}" for kc in range(KC)], writes=[psk])
            dst = V_all[:, tb, cb * 512:(cb + 1) * 512]
            if tb % 2 == 0:
                P.op("act", lambda e, dst=dst, ps=ps: e.copy(dst, ps), reads=[psk], writes=[f"V{tb}.{cb}"])
            else:
                P.op("dve", lambda e, dst=dst, ps=ps: e.tensor_copy(dst, ps), reads=[psk], writes=[f"V{tb}.{cb}"])

    if c.stage == "l0B1v":
        return
    t1 = sb(tmp_off + 4, 2048, F32)
    t2 = sb(tmp_off + 6, 2048, F32)
    q16_b1 = [sb(tmp_off + 8 + i, 1024, BF16) for i in range(2)]
    q16_c = [sb(154 + i, 1024, BF16) for i in range(2)]

    def rope_block(ps_a, psk_a, ps_b, psk_b, tcc, dst, dst_key, i, q16):
        qb = q16[i % 2]
        qk = f"q16{i % 2}"
        if c.stage == "l0B1r1":
            P.op("act", lambda e: e.copy(qb, ps_a), reads=[psk_a], writes=[qk])
            P.op("pe", lambda e: e.matmul(ps_b, c.pm, qb, start=True, stop=True), reads=[qk, "pm"], writes=[psk_b])
            P.op("dve", lambda e: e.tensor_copy(dst, ps_b), reads=[psk_b], writes=[dst_key])
            return
        if c.stage == "l0B1r2":
            P.op("dve", lambda e: e.tensor_tensor(t1, ps_a, Ctab[:, tcc * 512:(tcc + 1) * 512], ALU.mult), reads=[psk_a, "Ctab"], writes=["t1"])
            P.op("dve", lambda e: e.tensor_tensor(t2, ps_a, Stab[:, tcc * 512:(tcc + 1) * 512], ALU.mult), reads=[psk_a, "Stab"], writes=["t2"])
            P.op("dve", lambda e: e.tensor_tensor(dst, t1, t2, ALU.add), reads=["t1", "t2"], writes=[dst_key])
            return
        P.op("act", lambda e: e.copy(qb, ps_a), reads=[psk_a], writes=[qk])
        P.op("pe", lambda e: e.matmul(ps_b, c.pm, qb, start=True, stop=True), reads=[qk, "pm"], writes=[psk_b])
        P.op("dve", lambda e: e.tensor_tensor(t1, ps_a, Ctab[:, tcc * 512:(tcc + 1) * 512], ALU.mult), reads=[psk_a, "Ctab", qk], writes=["t1"])
        P.op("dve", lambda e: e.tensor_tensor(t2, ps_b, Stab[:, tcc * 512:(tcc + 1) * 512], ALU.mult), reads=[psk_b, "Stab"], writes=["t2"])
        P.op("dve", lambda e: e.tensor_tensor(dst, t1, t2, ALU.add), reads=["t1", "t2"], writes=[dst_key])

    ri = 0
    for hd in range(8):
        slab, skeys = wload(c, c.w_k[hd])
        for tcc in range(4):
            ba, bb = 2 + (ri % 2) * 2, 3 + (ri % 2) * 2
            proj_fm(c, slab, skeys, lambda kc, tcc=tcc: hn_all(kc, tcc), lambda kc, tcc=tcc: hn_all_keys(kc, tcc), c.ps[ba], f"ps{ba}")
            if c.stage == "l0B1kn":
                P.op("act", lambda e, ba=ba, hd=hd, tcc=tcc: e.copy(K_all[:, hd, tcc * 512:(tcc + 1) * 512], c.ps[ba]), reads=[f"ps{ba}"], writes=[f"K{hd}.{tcc}"])
            else:
                rope_block(c.ps[ba], f"ps{ba}", c.ps[bb], f"ps{bb}", tcc, K_all[:, hd, tcc * 512:(tcc + 1) * 512], f"K{hd}.{tcc}", ri, q16_b1)
            ri += 1

    if c.stage.startswith("l0B1"):
        return
    P.barrier(lambda e: e.memset(c.lam[:, 2:3], 0.0))
    wsT = sb(tmp_off + 10, 2048, BF16, "p (g n) -> p g n", g=8)
    P.op("pool", lambda e: e.dma_start(out=wsT, in_=c.w_s), writes=["wsT"], dma_key="c2")
    for g in range(8):
        slab, skeys = wload(c, c.w_u[g])
        for tc in range(2):
            bank = 2 + (g * 2 + tc) % 2
            proj_fm(c, slab, skeys, lambda kc, tc=tc: c.hn[:, kc, tc * 512:(tc + 1) * 512], lambda kc, tc=tc: [f"hn{kc}.{tc}"], c.ps[bank], f"ps{bank}")
            P.op("act", lambda e, g=g, tc=tc, bank=bank: e.activation(cat[:, 8 + g, tc * 512:(tc + 1) * 512], c.ps[bank], AF.Gelu),
                 reads=[f"ps{bank}"], writes=[f"cat{8 + g}.{tc}"])
    vbg = sb(96, 16 * 1024, F32, "p (b n) -> p b n", b=4)
    vbn = sb(tmp_off + 1, 2048, BF16)
    sqj = sb(tmp_off + 4, 4096, F32)
    stats = sb(tmp_off, 64, F32)
    lng = rows[:, R_LNG:R_LNG + 1024]
    lnb = rows[:, R_LNB:R_LNB + 1024]
    bsb = rows[:, R_BSB:R_BSB + 1024].rearrange("p (g n) -> p g n", g=8)
    for th in range(2):
        for cb in range(2):
            slab, skeys = wload(c, c.w_vb[cb], big=True)
            for tbl in range(4):
                tb = th * 4 + tbl
                bank = tb % 2
                ps, psk = c.ps[bank], f"ps{bank}"
                pairs = [(c.hn[:, kc, tb * 128:(tb + 1) * 128], slab[:, kc, :]) for kc in range(KC)]

                def fn(e, pairs=pairs, ps=ps):
                    ins = None
                    for i, (l, r) in enumerate(pairs):
                        ins = e.matmul(ps, l, r, start=(i == 0), stop=(i == KC - 1))
                    return ins
                P.op("pe", fn, reads=skeys + [f"hn{kc}.{tb // 4}" for kc in range(KC)], writes=[psk])
                P.op("act", lambda e, tbl=tbl, cb=cb, ps=ps: e.activation(vbg[:, tbl, cb * 512:(cb + 1) * 512], ps, AF.Gelu),
                     reads=[psk], writes=[f"vbg{tbl}.{cb}"])
        for tbl in range(4):
            tb = th * 4 + tbl
            xv = vbg[:, tbl, :]
            rk = [f"vbg{tbl}.0", f"vbg{tbl}.1"]
            P.op("dve", lambda e, xv=xv: e.reduce_sum(stats[:, 0:1], xv, AX.X), reads=rk, writes=["stats"])
            P.op("dve", lambda e: e.tensor_scalar(stats[:, 1:2], stats[:, 0:1], -1.0 / 1024, None, ALU.mult), reads=["stats"], writes=["stats"])
            P.op("dve", lambda e, xv=xv: e.tensor_scalar(xv, xv, stats[:, 1:2], None, ALU.add), reads=rk + ["stats"], writes=rk)
            P.op("dve", lambda e, xv=xv: e.tensor_tensor(sqj, xv, xv, ALU.mult), reads=rk, writes=["sqj"])
            P.op("dve", lambda e: e.reduce_sum(stats[:, 2:3], sqj, AX.X), reads=["sqj"], writes=["stats"])
            P.op("act", lambda e: e.activation(stats[:, 3:4], stats[:, 2:3], AF.Sqrt, bias=c.epsb[:, 0:1], scale=1.0 / 1024), reads=["stats", "epsb"], writes=["stats"])
            P.op("dve", lambda e: e.reciprocal(stats[:, 3:4], stats[:, 3:4]), reads=["stats"], writes=["stats"])
            P.op("dve", lambda e, xv=xv: e.scalar_tensor_tensor(xv, xv, stats[:, 3:4], lng, ALU.mult, ALU.mult), reads=rk + ["stats", "rows"], writes=rk)
            P.op("dve", lambda e, xv=xv: e.tensor_tensor(vbn, xv, lnb, ALU.add), reads=rk + ["rows"], writes=["vbn"])
            for gh in range(2):
                bank = 2 + gh
                ps, psk = c.ps[bank], f"ps{bank}"

                def fn(e, gh=gh, ps=ps):
                    ins = None
                    for gl in range(4):
                        g = gh * 4 + gl
                        ins = e.matmul(ps[:, gl * 128:(gl + 1) * 128], vbn[:, g * 128:(g + 1) * 128], wsT[:, g, :], start=True, stop=True)
                    return ins
                P.op("pe", fn, reads=["vbn", "wsT"], writes=[psk])
                svt = sb(tmp_off + 8, 2048, F32, "p (g n) -> p g n", g=4)
                P.op("dve", lambda e, ps=ps, gh=gh: e.tensor_tensor(svt, ps.rearrange("p (g n) -> p g n", g=4), bsb[:, gh * 4:(gh + 1) * 4, :], ALU.add),
                     reads=[psk, "rows"], writes=["svt"])
                cv = cat[:, 8 + gh * 4:8 + gh * 4 + 4, tb * 128:(tb + 1) * 128]
                ck = [f"cat{8 + gh * 4 + gl}.{tb // 4}" for gl in range(4)]
                P.op("dve", lambda e, cv=cv: e.tensor_tensor(cv, cv, svt, ALU.mult), reads=["svt"] + ck, writes=ck)

    if c.stage == "l0B2":
        return
    P.barrier(lambda e: e.memset(c.lam[:, 2:3], 0.0))
    Et = [[sb(tmp_off + 8 + 2 * m + b, 1024, BF16) for b in range(2)] for m in range(2)]
    qrot = sb(tmp_off + 1, 2048, BF16)
    r0 = sb(tmp_off + 4, 2048, F32)
    r1 = sb(tmp_off + 6, 2048, F32)
    oT = sb(147, 2048, F32)
    a0 = sb(149, 2048, F32)
    sqb = sb(151, 1024, BF16)
    rs2 = sb(152, 2048, F32)
    scale = 64 ** -0.5
    ri = 0
    for hd in range(8):
        slab, skeys = wload(c, c.w_q[hd])
        for tc in range(2):
            proj_fm(c, slab, skeys, lambda kc, tc=tc: c.hn[:, kc, tc * 512:(tc + 1) * 512], lambda kc, tc=tc: [f"hn{kc}.{tc}"], c.ps[6], "ps6")
            rope_block(c.ps[6], "ps6", c.ps[7], "ps7", tc, qrot[:, tc * 512:(tc + 1) * 512], f"qrot{tc}", ri, q16_c)
            ri += 1
        for tc in range(2):
            SB = [0, 1, 6, 7]

            def emit_scores(j, tc=tc, hd=hd):
                for m in range(2):
                    b = SB[(j % 2) * 2 + m]
                    P.op("pe", lambda e, m=m, j=j, tc=tc, b=b, hd=hd: e.matmul(c.ps[b], K_all[64 * m:64 * m + 64, hd, j * 128:(j + 1) * 128],
                                                                            qrot[64 * m:64 * m + 64, tc * 512:(tc + 1) * 512], start=True, stop=True),
                         reads=[f"K{hd}.{j // 4}", f"qrot{tc}"], writes=[f"ps{b}"])

            emit_scores(0)
            for j in range(16):
                for m in range(2):
                    b = SB[(j % 2) * 2 + m]
                    E = Et[m][j % 2]
                    ek = f"E{m}.{j % 2}"
                    P.op("act", lambda e, E=E, b=b: e.activation(E, c.ps[b], AF.Exp, scale=scale), reads=[f"ps{b}"], writes=[ek])
                if j < 15:
                    emit_scores(j + 1)
                for m in range(2):
                    E = Et[m][j % 2]
                    ek = f"E{m}.{j % 2}"
                    P.op("pe", lambda e, m=m, j=j, E=E, hd=hd: e.matmul(c.ps[2 + 2 * m], V_all[:, j, hd * 128:(hd + 1) * 128], E, start=(j == 0), stop=(j == 15)),
                         reads=[ek, f"V{j}.{hd // 4}"], writes=[f"ps{2 + 2 * m}"])
                    P.op("pe", lambda e, m=m, j=j, E=E: e.matmul(c.ps[3 + 2 * m], c.ones, E, start=(j == 0), stop=(j == 15)),
                         reads=[ek, "ones"], writes=[f"ps{3 + 2 * m}"])
            P.op("dve", lambda e: e.reciprocal(r0, c.ps[3]), reads=["ps3"], writes=["t1"])
            P.op("dve", lambda e: e.reciprocal(r1, c.ps[5]), reads=["ps5"], writes=["t2"])
            P.op("dve", lambda e: e.tensor_tensor(a0, c.ps[2], r0, ALU.mult), reads=["ps2", "t1"], writes=["a0"])
            P.op("dve", lambda e: e.tensor_tensor(r1, c.ps[4], r1, ALU.mult), reads=["ps4", "t2"], writes=["t2"])
            P.op("dve", lambda e: e.scalar_tensor_tensor(oT, r1, c.lam[:, 1:2], a0, ALU.mult, ALU.add), reads=["t2", "a0", "lam"], writes=["oT"])
            P.op("act", lambda e: e.activation(sqb, oT, AF.Square), reads=["oT"], writes=["sqb"])
            P.op("pe", lambda e: e.matmul(c.ps[6], c.ones, sqb, start=True, stop=True), reads=["sqb", "ones"], writes=["ps6"])
            P.op("act", lambda e: e.activation(rs2, c.ps[6], AF.Sqrt, bias=c.epsb[:, 1:2], scale=1.0 / (128 * 0.64)), reads=["ps6", "epsb"], writes=["rs2"])
            P.op("dve", lambda e: e.reciprocal(rs2, rs2), reads=["rs2"], writes=["rs2"])
            P.op("dve", lambda e, tc=tc, hd=hd: e.scalar_tensor_tensor(cat[:, hd, tc * 512:(tc + 1) * 512], oT, c.vecs[:, V_SUBLN:V_SUBLN + 1], rs2, ALU.mult, ALU.mult),
                 reads=["oT", "rs2", "vecs"], writes=[f"cat{hd}.{tc}"])

    if c.stage == "l0C":
        P.barrier(lambda e: e.memset(c.lam[:, 2:3], 0.0))
        for kc in range(KC):
            P.op("dve", lambda e, kc=kc: e.tensor_copy(c.h[:, kc, :], cat[:, kc, :]), writes=[f"h{kc}.0", f"h{kc}.1"])
        return
    P.barrier(lambda e: e.memset(c.lam[:, 2:3], 0.0))
    for db in range(16):
        slab, skeys = wload(c, c.w_eo[db])
        P.op("sp", lambda e, db=db: e.dma_start(out=c.h[:, db, :], in_=c.xT[:, db, 0:T]), writes=[f"h{db}.0", f"h{db}.1"], dma_key=f"x{db}")
        for tc in range(2):
            bank = (db * 2 + tc) % 4
            proj_fm(c, slab, skeys, lambda kc, tc=tc: cat[:, kc, tc * 512:(tc + 1) * 512], lambda kc, tc=tc: [f"cat{kc}.{tc}"], c.ps[bank], f"ps{bank}")
            hv = c.h[:, db, tc * 512:(tc + 1) * 512]
            P.op("dve", lambda e, hv=hv, bank=bank: e.tensor_tensor(hv, hv, c.ps[bank], ALU.add), reads=[f"ps{bank}", f"h{db}.{tc}"], writes=[f"h{db}.{tc}"])


def ffn(c, l):
    P, sb = c.P, c.sb
    P.barrier(lambda e: e.memset(c.lam[:, 2:3], 0.0))
    tmp_off = 192
    gcol = V_FFNG0 if l == 0 else V_FFNG1
    rms_to_hn(c, lambda kc, tc: c.h[:, kc, tc * 512:(tc + 1) * 512], lambda kc, tc: [f"h{kc}.{tc}"], gcol, 2,
              lambda kc, tc: c.hn[:, kc, tc * 512:(tc + 1) * 512], lambda kc, tc: f"hn{kc}.{tc}", tmp_off)
    act = sb(96, 44 * 1024, BF16, "p (j t) -> p j t", j=22)
    sg = [sb(tmp_off + 4 + 2 * i, 2048, F32) for i in range(4)]
    gi = 0
    for half in range(2):
        for hb in range(22):
            sl_g, kg = wload(c, c.w_g[l, half * 22 + hb])
            sl_u, ku = wload(c, c.w_up[l, half * 22 + hb])
            for tc in range(2):
                bg = (gi % 2) * 4 + tc * 2
                bu = bg + 1
                rf = lambda kc, tc=tc: c.hn[:, kc, tc * 512:(tc + 1) * 512]
                rk = lambda kc, tc=tc: [f"hn{kc}.{tc}"]
                proj_fm(c, sl_g, kg, rf, rk, c.ps[bg], f"ps{bg}")
                proj_fm(c, sl_u, ku, rf, rk, c.ps[bu], f"ps{bu}")
                s = sg[(gi * 2 + tc) % 4]
                sk = f"sg{(gi * 2 + tc) % 4}"
                P.op("act", lambda e, s=s, bg=bg: e.activation(s, c.ps[bg], AF.Silu), reads=[f"ps{bg}"], writes=[sk])
                P.op("dve", lambda e, s=s, bu=bu, hb=hb, tc=tc: e.tensor_tensor(act[:, hb, tc * 512:(tc + 1) * 512], s, c.ps[bu], ALU.mult),
                     reads=[sk, f"ps{bu}"], writes=[f"act{hb}.{tc}"])
            gi += 1
        for db in range(16):
            wl = []
            for part in range(2):
                slab, sk_ = wload(c, c.w_dn[l, half, db, :, part * 11:(part + 1) * 11, :], nk=11)
                wl.append((slab, sk_))
            for tc in range(2):
                bank = (db * 2 + tc) % 4 if (gi % 2 == 0) else 4 + (db * 2 + tc) % 4
                ps, psk = c.ps[bank], f"ps{bank}"
                pairs = []
                reads = []
                for part in range(2):
                    slab, sk_ = wl[part]
                    reads += sk_
                    for jj in range(11):
                        j = part * 11 + jj
                        pairs.append((slab[:, jj, :], act[:, j, tc * 512:(tc + 1) * 512]))
                        reads.append(f"act{j}.{tc}")

                def fn(e, pairs=pairs, ps=ps):
                    ins = None
                    n = len(pairs)
                    for i, (lh, r) in enumerate(pairs):
                        ins = e.matmul(ps, lh, r, start=(i == 0), stop=(i == n - 1))
                    return ins
                P.op("pe", fn, reads=reads, writes=[psk])
                hv = c.h[:, db, tc * 512:(tc + 1) * 512]
                P.op("dve", lambda e, hv=hv, ps=ps: e.tensor_tensor(hv, hv, ps, ALU.add), reads=[psk, f"h{db}.{tc}"], writes=[f"h{db}.{tc}"])


def layer1_mixer(c):
    P, nc, sb = c.P, c.nc, c.sb
    P.barrier(lambda e: e.memset(c.lam[:, 2:3], 0.0))
    rms_to_hn(c, lambda kc, tc: c.h[:, kc, tc * 512:(tc + 1) * 512], lambda kc, tc: [f"h{kc}.{tc}"], V_MIXG1, 2,
              lambda kc, tc: c.hn[:, kc, tc * 512:(tc + 1) * 512], lambda kc, tc: f"hn{kc}.{tc}", 192)
    P.barrier(lambda e: e.memset(c.lam[:, 2:3], 0.0))
    y = sb(96, 32 * 1024, BF16, "p (k t) -> p k t", k=KC)
    A = [sb(128 + 4 * i, 4096, F32) for i in range(4)] + [sb(147 + 4 * i, 4096, F32) for i in range(3)]
    QB = sb(192, 2048, BF16)
    KB = sb(194, 2048, BF16)
    KBT = sb(196, 4096, BF16, "p (c k) -> p c k", c=16)
    VT = sb(200, 4096, BF16, "p (c k) -> p c k", c=16)
    SCT = sb(155, 2048, BF16, "p (c t) -> p c t", c=16)
    msk = sb(157, 1024, BF16)
    TB = [sb(158 + 0.5 * i, 512, F32) for i in range(4)]
    SBFR = [sb(153.5 + 0.25 * i, 256, BF16) for i in range(4)]
    S32 = sb(153, 512, F32)
    EBL = sb(146.5, 64, F32)
    LBV = sb(146.5625, 4 * 64, F32, "p (a h) -> p a h", a=4)
    IT = A[5]
    ITb = sb(147 + 4 * 1, 2048, BF16)
    P.op("dve", lambda e: e.memset(msk, 1.0), writes=["msk"])
    P.op("dve", lambda e: e.memset(msk.rearrange("p (c t) -> p c t", t=64)[:, :, 0:1], 0.0), reads=["msk"], writes=["msk"])
    for ld in range(2):
        r0c = V_LB + (ld * 2 + 0) * 16
        r1c = V_LB + (ld * 2 + 1) * 16
        P.op("dve", lambda e, ld=ld, r0c=r0c, r1c=r1c: e.tensor_tensor(LBV[:, 2 * ld, :], c.vecs[:, r1c:r1c + 16], c.vecs[:, r0c:r0c + 16], ALU.subtract),
             reads=["vecs", "LBV"], writes=["LBV"])
        P.op("act", lambda e, ld=ld: e.activation(LBV[:, 2 * ld, :], LBV[:, 2 * ld, :], AF.Sigmoid), reads=["LBV"], writes=["LBV"])
        P.op("dve", lambda e, ld=ld: e.tensor_scalar(LBV[:, 2 * ld + 1, :], LBV[:, 2 * ld, :], -1.0, 1.0, ALU.mult, ALU.add), reads=["LBV"], writes=["LBV"])

    def proj2(wslab, dst_fn, func, dkey, scale=None):
        slab, skeys = wload(c, wslab)
        for tc in range(2):
            bank = tc
            proj_fm(c, slab, skeys, lambda kc, tc=tc: c.hn[:, kc, tc * 512:(tc + 1) * 512], lambda kc, tc=tc: [f"hn{kc}.{tc}"], c.ps[bank], f"ps{bank}")
            P.op("act", lambda e, tc=tc, bank=bank: e.activation(dst_fn(tc), c.ps[bank], func), reads=[f"ps{bank}"], writes=[dkey])

    SQs = [A[0], sb(180, 4096, F32)]
    FVs = [A[1], sb(184, 4096, F32)]
    ITBs = [ITb, sb(188, 2048, BF16)]

    def head_P(hd, ld, bs):
        for wsel, dstT, func, key in ((0, SQs[bs], AF.Silu, f"A0.{bs}"), (1 + ld, FVs[bs], AF.Sigmoid, f"A1.{bs}"), (3, ITBs[bs], AF.Copy, f"A5.{bs}")):
            slab, skeys = wload(c, c.w_hin[wsel, hd])
            for tc in range(2):
                bank = tc
                proj_fm(c, slab, skeys, lambda kc, tc=tc: c.hn[:, kc, tc * 512:(tc + 1) * 512], lambda kc, tc=tc: [f"hn{kc}.{tc}"], c.ps[bank], f"ps{bank}")
                P.op("act", lambda e, tc=tc, bank=bank, dstT=dstT, func=func: e.activation(dstT[:, tc * 512:(tc + 1) * 512], c.ps[bank], func),
                     reads=[f"ps{bank}"], writes=[key])
                yield

    def head_pass(hd, ld, bs):
        lb = LBV[:, 2 * ld, hd:hd + 1]
        oml = LBV[:, 2 * ld + 1, hd:hd + 1]
        sq, fv, kk, lf, bb = SQs[bs], FVs[bs], A[2], A[3], A[4]
        ITb = ITBs[bs]
        kA0, kA1, kA5 = f"A0.{bs}", f"A1.{bs}", f"A5.{bs}"
        P.op("dve", lambda e: e.tensor_scalar(fv, fv, oml, lb, ALU.mult, ALU.add), reads=[kA1, "LBV"], writes=[kA1])
        P.op("dve", lambda e: e.tensor_scalar(kk, fv, -1.0, 1.0, ALU.mult, ALU.add), reads=[kA1], writes=["A2"])
        P.op("act", lambda e: e.activation(lf, fv, AF.Ln), reads=[kA1], writes=["A3"])
        for tc in range(2):
            P.op("dve", lambda e, tc=tc: e.tensor_tensor_scan(bb[:, tc * 512:(tc + 1) * 512], msk, lf[:, tc * 512:(tc + 1) * 512], 0.0, ALU.mult, ALU.add),
                 reads=["A3", "msk"], writes=["A4"])
        b3 = bb.rearrange("p (c t) -> p c t", t=64)
        if ld == 1:
            P.op("dve", lambda e: e.tensor_tensor(lf, lf, bb, ALU.subtract), reads=["A3", "A4"], writes=["A3"])
            P.op("dve", lambda e: e.tensor_tensor(b3, lf.rearrange("p (c t) -> p c t", t=64), b3[:, :, 63:64].to_broadcast([128, 16, 64]), ALU.add),
                 reads=["A3", "A4"], writes=["A4"])
        eb, enb = fv, A[3]
        P.op("act", lambda e: e.activation(eb, bb, AF.Exp), reads=["A4", kA1], writes=[kA1])
        P.op("act", lambda e: e.activation(enb, bb, AF.Exp, scale=-1.0), reads=["A4", "A3"], writes=["A3"])
        P.op("dve", lambda e: e.tensor_tensor(QB, sq, eb, ALU.mult), reads=[kA0, kA1], writes=["QB"])
        P.op("dve", lambda e: e.tensor_tensor(KB, kk, enb, ALU.mult), reads=["A2", "A3"], writes=["KB"])
        e3 = eb.rearrange("p (c t) -> p c t", t=64)
        edge = 63 if ld == 0 else 0
        P.op("dve", lambda e: e.tensor_copy(EBL, e3[:, :, edge]), reads=[kA1], writes=["EBL"])
        for src, skey, dst, dkey in ((ITb, kA5, VT, "VT"), (KB, "KB", KBT, "KBT")):
            for half in range(2):
                bank = 4 + half
                pst = c.ps[bank].bitcast(BF16).rearrange("p (c k) -> p c k", k=128)

                def fn(e, src=src, half=half, pst=pst):
                    ins = None
                    for cl in range(8):
                        ch = half * 8 + cl
                        ins = e.transpose(pst[0:64, cl, :], src[:, ch * 64:(ch + 1) * 64], c.ident)
                    return ins
                P.op("pe", fn, reads=[skey, "ident"], writes=[f"ps{bank}"])
                P.op("act" if half == 0 else "dve",
                     (lambda e, dst=dst, half=half, pst=pst: e.copy(dst[0:64, half * 8:(half + 1) * 8, :], pst[0:64, :, :])) if half == 0 else
                     (lambda e, dst=dst, half=half, pst=pst: e.tensor_copy(dst[0:64, half * 8:(half + 1) * 8, :], pst[0:64, :, :])),
                     reads=[f"ps{bank}"], writes=[dkey])
        msk2 = (c.tri if ld == 0 else c.triT)[0:64, 0:64]
        for half in range(2):
            bank = 6 + half
            psv = c.ps[bank].rearrange("p (c t) -> p c t", t=64)

            def fn(e, half=half, psv=psv):
                ins = None
                for cl in range(8):
                    ch = half * 8 + cl
                    ins = e.matmul(psv[0:64, cl, :], KB[:, ch * 64:(ch + 1) * 64], QB[:, ch * 64:(ch + 1) * 64], start=True, stop=True)
                return ins
            P.op("pe", fn, reads=["KB", "QB"], writes=[f"ps{bank}"])
            P.op("dve", lambda e, half=half, psv=psv: e.tensor_tensor(SCT[0:64, half * 8:(half + 1) * 8, :], psv[0:64, :, :],
                                                                     msk2.unsqueeze(1).to_broadcast([64, 8, 64]), ALU.mult),
                 reads=[f"ps{bank}", "tri", "triT"], writes=["SCT"])
        order = list(range(16)) if ld == 0 else list(range(15, -1, -1))
        for q4 in range(4):
            ubank = 4 + q4

            def fu(e, q4=q4, ubank=ubank):
                ins = None
                for i4 in range(4):
                    ch = order[q4 * 4 + i4]
                    ins = e.matmul(c.ps[ubank][:, i4 * 128:(i4 + 1) * 128], KBT[0:64, ch, :], VT[0:64, ch, :], start=True, stop=True)
                return ins
            P.op("pe", fu, reads=["KBT", "VT"], writes=[f"ps{ubank}"])
        if ld == 0:
            P.op("dve", lambda e: e.memset(TB[3], 0.0), writes=["TB3"])
        else:
            P.op("sp", lambda e: e.dma_start(out=TB[3], in_=c.cout[hd * 128:(hd + 1) * 128, :]), reads=["cout"], writes=["TB3"], dma_key="st0")
            P.op("sp", lambda e: e.dma_start(out=TB[2], in_=c.cout[2048 + hd * 128:2048 + (hd + 1) * 128, :]), reads=["cout"], writes=["TB2"], dma_key="st1")
            P.op("dve", lambda e: e.tensor_scalar(TB[3], TB[3], c.vecs[:, V_SEL0:V_SEL0 + 1], None, ALU.mult), reads=["TB3", "vecs"], writes=["TB3"])
            P.op("dve", lambda e: e.scalar_tensor_tensor(TB[3], TB[2], c.vecs[:, V_SEL1:V_SEL1 + 1], TB[3], ALU.mult, ALU.add), reads=["TB3", "TB2", "vecs"], writes=["TB3"])
        P.op("act", lambda e: e.copy(SBFR[3], TB[3]), reads=["TB3"], writes=["SBF3"])
        for n, ch in enumerate(order):
            un = c.ps[4 + n // 4][:, (n % 4) * 128:(n % 4 + 1) * 128]
            prev = (n - 1) % 4
            cur = n % 4
            if n == 0:
                P.op("dve", lambda e, un=un, prev=prev, cur=cur: e.tensor_tensor(TB[cur], TB[prev], un, ALU.add),
                     reads=[f"TB{prev}", f"ps{4 + n // 4}"], writes=[f"TB{cur}"])
            else:
                ep = EBL[:, order[n - 1]:order[n - 1] + 1]
                P.op("dve", lambda e, un=un, prev=prev, cur=cur, ep=ep: e.scalar_tensor_tensor(TB[cur], TB[prev], ep, un, ALU.mult, ALU.add),
                     reads=[f"TB{prev}", f"ps{4 + n // 4}", "EBL"], writes=[f"TB{cur}"])
            obank = 2 + (ch // 8)
            pso = c.ps[obank].rearrange("p (c t) -> p c t", t=64)[:, ch % 8, :]

            def fo(e, ch=ch, pso=pso, prev=prev):
                e.matmul(pso, VT[0:64, ch, :], SCT[0:64, ch, :], start=True, stop=False)
                return e.matmul(pso, SBFR[prev], QB[:, ch * 64:(ch + 1) * 64], start=False, stop=True)
            P.op("pe", fo, reads=["VT", "SCT", f"SBF{prev}", "QB"], writes=[f"ps{obank}"])
            ec = EBL[:, ch:ch + 1]
            if n < 15:
                P.op("act", lambda e, cur=cur, ec=ec: e.activation(SBFR[cur], TB[cur], AF.Copy, scale=ec), reads=[f"TB{cur}", "EBL"], writes=[f"SBF{cur}"])
            else:
                P.op("act", lambda e, cur=cur, ec=ec: e.activation(S32, TB[cur], AF.Copy, scale=ec), reads=[f"TB{cur}", "EBL"], writes=["S32"])
            if (n % 8) == 7:
                t0 = (ch // 8) * 512
                yv = y[:, hd, t0:t0 + 512]
                if ld == 0:
                    P.op("act", lambda e, yv=yv, obank=obank: e.copy(yv, c.ps[obank]), reads=[f"ps{obank}"], writes=[f"y{hd}.{ch // 8}"])
                else:
                    P.op("dve", lambda e, yv=yv, obank=obank: e.tensor_tensor(yv, yv, c.ps[obank], ALU.add), reads=[f"ps{obank}", f"y{hd}.{ch // 8}"], writes=[f"y{hd}.{ch // 8}"])
        if ld == 0:
            P.op("sp", lambda e: e.dma_start(out=c.cin[hd * 128:(hd + 1) * 128, :], in_=S32), reads=["S32"], writes=["cin"], dma_key=f"ci{hd % 4}")

    c.nring = 5
    seq = [(hd, 0) for hd in range(16)] + [(hd, 1) for hd in range(16)]
    gens = [head_P(hd, ld, i % 2) for i, (hd, ld) in enumerate(seq)]
    for _ in gens[0]:
        pass
    for i, (hd, ld) in enumerate(seq):
        g = gens[i + 1] if i + 1 < len(seq) else None
        cnt = [0]

        def hook(g=g, cnt=cnt):
            cnt[0] += 1
            if g is not None and cnt[0] % 2 == 0:
                P.in_hook = True
                try:
                    next(g)
                except StopIteration:
                    pass
                P.in_hook = False
        P.hook = hook
        head_pass(hd, ld, i % 2)
        P.hook = None
        if g is not None:
            for _ in g:
                pass
        if i == 15:
            P.op("pool", lambda e: e.collective_compute("AllGather", ALU.bypass, [[0, 1], [2, 3], [4, 5], [6, 7]], [c.cin.opt()], [c.cout.opt()]),
                 reads=["cin"], writes=["cout"], dma_key="cc", dma_inc=1)
    c.nring = 8

    P.barrier(lambda e: e.memset(c.lam[:, 2:3], 0.0))
    sqy = [sb(192 + i, 1024, BF16) for i in range(2)]
    rstd2 = A[0]
    for tc in range(2):
        ps, psk = c.ps[6 + tc], f"ps{6 + tc}"
        for hd in range(16):
            s_ = sqy[hd % 2]
            sk = f"sqy{hd % 2}"
            P.op("act", lambda e, s_=s_, hd=hd, tc=tc: e.activation(s_, y[:, hd, tc * 512:(tc + 1) * 512], AF.Square), reads=[f"y{hd}.{tc}"], writes=[sk])
            P.op("pe", lambda e, s_=s_, hd=hd, ps=ps: e.matmul(ps, c.ones, s_, start=(hd == 0), stop=(hd == 15)), reads=[sk, "ones"], writes=[psk])
        rv = rstd2[:, tc * 512:(tc + 1) * 512]
        P.op("act", lambda e, ps=ps, rv=rv: e.activation(rv, ps, AF.Sqrt, bias=c.epsb[:, 0:1], scale=1.0 / D), reads=[psk, "epsb"], writes=["A0"])
        P.op("dve", lambda e, rv=rv: e.reciprocal(rv, rv), reads=["A0"], writes=["A0"])
    sgt = [sb(128 + 4 + 2 * i, 2048, F32) for i in range(2)]
    ytmp = sb(128 + 8, 2048, F32)
    for hd in range(16):
        slab, skeys = wload(c, c.w_hin[4, hd])
        for tc in range(2):
            bank = tc
            proj_fm(c, slab, skeys, lambda kc, tc=tc: c.hn[:, kc, tc * 512:(tc + 1) * 512], lambda kc, tc=tc: [f"hn{kc}.{tc}"], c.ps[bank], f"ps{bank}")
            sg_ = sgt[tc]
            P.op("act", lambda e, sg_=sg_, bank=bank: e.activation(sg_, c.ps[bank], AF.Silu), reads=[f"ps{bank}"], writes=[f"sgt{tc}"])
            yv = y[:, hd, tc * 512:(tc + 1) * 512]
            P.op("dve", lambda e, yv=yv, hd=hd, tc=tc: e.scalar_tensor_tensor(ytmp, yv, c.vecs[:, V_GNORM + hd:V_GNORM + hd + 1], rstd2[:, tc * 512:(tc + 1) * 512], ALU.mult, ALU.mult),
                 reads=[f"y{hd}.{tc}", "A0", "vecs"], writes=["ytmp"])
            P.op("dve", lambda e, yv=yv, sg_=sg_: e.tensor_tensor(yv, ytmp, sg_, ALU.mult), reads=["ytmp", f"sgt{tc}"], writes=[f"y{hd}.{tc}"])
    for db in range(16):
        slab, skeys = wload(c, c.w_ho[db])
        for tc in range(2):
            bank = 2 + (db * 2 + tc) % 4
            proj_fm(c, slab, skeys, lambda kc, tc=tc: y[:, kc, tc * 512:(tc + 1) * 512], lambda kc, tc=tc: [f"y{kc}.{tc}"], c.ps[bank], f"ps{bank}")
            hv = c.h[:, db, tc * 512:(tc + 1) * 512]
            P.op("dve", lambda e, hv=hv, bank=bank: e.tensor_tensor(hv, hv, c.ps[bank], ALU.add), reads=[f"ps{bank}", f"h{db}.{tc}"], writes=[f"h{db}.{tc}"])


def final_norm(c):
    P, sb = c.P, c.sb
    P.barrier(lambda e: e.memset(c.lam[:, 2:3], 0.0))
    rms_to_hn(c, lambda kc, tc: c.h[:, kc, tc * 512:(tc + 1) * 512], lambda kc, tc: [f"h{kc}.{tc}"], V_FING, 2,
              lambda kc, tc: c.h[:, kc, tc * 512:(tc + 1) * 512], lambda kc, tc: f"h{kc}.{tc}", 192)
    for kc in range(KC):
        P.op("sp", lambda e, kc=kc: e.dma_start(out=c.yT[:, kc, :], in_=c.h[:, kc, :]),
             reads=[f"h{kc}.0", f"h{kc}.1"], writes=[f"yo{kc}"], dma_key=f"out{kc % 8}")


def host_prep(inp):
    f32 = np.float32
    x = inp["x"]
    shared = {}
    W = inp["even_w_in"][0]
    shared["w_q"] = slabify(W[:, 0:1024], 128)
    shared["w_k"] = slabify(W[:, 1024:2048], 128)
    shared["w_v"] = slabify(W[:, 2048:3072], 512)
    shared["w_u"] = slabify(W[:, 3072:4096], 128)
    shared["w_vb"] = slabify(W[:, 4096:5120], 512)
    shared["w_eo"] = slabify(inp["even_w_out"][0], 128)
    shared["w_g"] = np.stack([slabify(inp["ffn_w_gate"][l], 128) for l in range(2)])
    shared["w_up"] = np.stack([slabify(inp["ffn_w_up"][l], 128) for l in range(2)])
    wd = inp["ffn_w_down"]
    shared["w_dn"] = np.ascontiguousarray(wd.reshape(2, 2, 22, 128, 16, 128).transpose(0, 1, 4, 3, 2, 5))
    shared["w_ho"] = slabify(inp["hgrn_w_out"][0], 128)
    Wh = inp["hgrn_w_in"][0]
    hin = [slabify(Wh[:, i * 2048:(i + 1) * 2048], 128) for i in range(5)]
    hin_even = np.stack([hin[0], hin[1], hin[2], hin[3], hin[4]])
    hin_odd = np.stack([hin[0], hin[2], hin[1], hin[3], hin[4]])
    consts = np.zeros((128, 4, 128), f32)
    consts[:, 0, :] = np.eye(128, dtype=f32)
    consts[:, 1, :] = 1.0
    pm = np.zeros((128, 128), f32)
    for base in (0, 64):
        for d in range(8):
            pm[base + d + 8, base + d] = 1.0
            pm[base + d, base + d + 8] = 1.0
    consts[:, 2, :] = pm
    consts[:, 3, :] = np.triu(np.ones((128, 128), f32))
    half = 8
    invf_vals = (500000.0 ** (-np.arange(half, dtype=np.float64) / half)).astype(f32)
    invf = np.zeros(128, f32)
    sgn = np.zeros(128, f32)
    for base in (0, 64):
        invf[base:base + 8] = invf_vals
        invf[base + 8:base + 16] = invf_vals
        sgn[base:base + 8] = -1.0
        sgn[base + 8:base + 16] = 1.0
    lbraw = inp["hgrn_lower_bounds"]
    in_maps = []
    for core in range(NCORES):
        b, hf = core // 2, core % 2
        own = np.arange(T) if hf == 0 else (TA - 1 - np.arange(T))
        oth = (TA - 1 - np.arange(T)) if hf == 0 else np.arange(T)
        tok = np.concatenate([own, oth])
        xT = np.ascontiguousarray(x[b][tok, :].T.reshape(KC, 128, TA).transpose(1, 0, 2))
        vecs = np.zeros((128, NV), f32)
        vecs[:, V_MIXG0:V_MIXG0 + 16] = fm_vec(inp["mix_norm"][0])
        vecs[:, V_FFNG0:V_FFNG0 + 16] = fm_vec(inp["ffn_norm"][0])
        vecs[:, V_MIXG1:V_MIXG1 + 16] = fm_vec(inp["mix_norm"][1])
        vecs[:, V_FFNG1:V_FFNG1 + 16] = fm_vec(inp["ffn_norm"][1])
        vecs[:, V_FING:V_FING + 16] = fm_vec(inp["final_norm"])
        vecs[:, V_INVF] = invf
        vecs[:, V_SGN] = sgn
        vecs[:, V_SUBLN] = inp["diff_subln"][0]
        vecs[:, V_SEL0] = 1.0 if hf == 1 else 0.0
        vecs[:, V_SEL1] = 1.0 if hf == 0 else 0.0
        dirs = (0, 1) if hf == 0 else (1, 0)
        for ld in range(2):
            for layer in range(2):
                vecs[:, V_LB + (ld * 2 + layer) * 16:V_LB + (ld * 2 + layer + 1) * 16] = fm_vec(lbraw[dirs[ld], layer])
        vecs[:, V_GNORM:V_GNORM + 16] = fm_vec(inp["hgrn_g_norm"][0])
        rows = np.zeros((1, NR), f32)
        rows[0, R_LNG:R_LNG + 1024] = inp["gmlp_ln_g"][0]
        rows[0, R_LNB:R_LNB + 1024] = inp["gmlp_ln_b"][0]
        ws = inp["gmlp_w_s"][0]
        bs = inp["gmlp_b_s"][0]
        if hf == 1:
            ws = ws[:, ::-1, ::-1]
            bs = bs[:, ::-1]
        rows[0, R_BSB:R_BSB + 1024] = bs.reshape(-1)
        rows[0, R_LQ:R_LQ + 256] = np.concatenate([inp["diff_lq1"][0], inp["diff_lk1"][0], inp["diff_lq2"][0], inp["diff_lk2"][0]])
        m = dict(shared)
        m["xT"] = xT
        m["pos"] = tok.astype(f32).reshape(1, TA)
        m["vecs"] = vecs
        m["rows"] = rows
        m["consts"] = consts
        m["w_s"] = np.ascontiguousarray(ws.transpose(2, 0, 1))
        m["w_hin"] = hin_even if hf == 0 else hin_odd
        in_maps.append(m)
    return in_maps


def assemble(results):
    out = np.zeros((4, TA, D), np.float32)
    for core in range(NCORES):
        b, hf = core // 2, core % 2
        yT = results[core]["yT"]
        y = yT.transpose(2, 1, 0).reshape(T, D)
        if hf == 0:
            out[b, 0:T] = y
        else:
            out[b, T:TA] = y[::-1]
    return out


def kernel(**inputs):
    inp = {k: np.asarray(v) for k, v in inputs.items()}
    in_maps = host_prep(inp)
    nc = build("full")
    res = run_bass_kernel_spmd(nc, in_maps, core_ids=list(range(NCORES)))
    return assemble(res.results)
```

```python
import contextlib
import numpy as np
import concourse.bass as bass
import concourse.mybir as mybir
from concourse.bass_utils import run_bass_kernel_spmd

F32 = mybir.dt.float32
BF16 = mybir.dt.bfloat16
AF = mybir.ActivationFunctionType
ALU = mybir.AluOpType
AX = mybir.AxisListType

ENGS = ("pe", "act", "dve", "pool", "sp")
DEBUG_TAGS = False
INS_TAGS = {}
D = 2048
KC = 16
T = 1024
TA = 2048
FH = 5632
EPS = 1e-6
NCORES = 8


class Op:
    __slots__ = ("eng", "fn", "deps", "signal", "count", "dma_key", "dma_cum", "dma_inc", "idx", "tag")

    def __init__(self, eng, fn, dma_key, dma_inc):
        self.eng = eng
        self.fn = fn
        self.deps = []
        self.signal = False
        self.count = 0
        self.dma_key = dma_key
        self.dma_cum = 0
        self.dma_inc = dma_inc
        self.idx = 0


class Prog:
    def __init__(self):
        self.ops = []
        self.last_w = {}
        self.readers = {}
        self.dma_cnt = {}
        self.last_on = {}
        self.bar = None
        self.bar_seen = set()
        self.hook = None
        self.in_hook = False

    def op(self, eng, fn, reads=(), writes=(), dma_key=None, dma_inc=16):
        o = Op(eng, fn, dma_key, dma_inc)
        o.idx = len(self.ops)
        if DEBUG_TAGS:
            import sys as _s
            f = _s._getframe(1)
            o.tag = f"{f.f_lineno}<{f.f_back.f_lineno}<{f.f_back.f_back.f_lineno if f.f_back.f_back else 0} w={list(writes)[:3]}"
        deps = set()
        for r in reads:
            w = self.last_w.get(r)
            if w is not None:
                deps.add(w)
            if r.startswith("ps"):
                for rd in self.readers.get(r, ()):
                    if rd.eng != eng:
                        deps.add(rd)
        for wkey in writes:
            w = self.last_w.get(wkey)
            if w is not None:
                deps.add(w)
            for rd in self.readers.get(wkey, ()):
                deps.add(rd)
        if dma_key is not None:
            prev = self.last_on.get("dma:" + dma_key)
            if prev is not None:
                deps.add(prev)
        if self.bar is not None and eng not in self.bar_seen:
            deps.add(self.bar)
            self.bar_seen.add(eng)
        for d in deps:
            if d.dma_key is None and d.eng == "pe" and eng == "pe" and dma_key is None:
                continue
            o.deps.append(d)
            if d.dma_key is None:
                d.signal = True
        for r in reads:
            self.readers.setdefault(r, []).append(o)
        for wkey in writes:
            self.last_w[wkey] = o
            self.readers[wkey] = []
        if dma_key is not None:
            self.dma_cnt[dma_key] = self.dma_cnt.get(dma_key, 0) + dma_inc
            o.dma_cum = self.dma_cnt[dma_key]
            self.last_on["dma:" + dma_key] = o
        else:
            self.last_on[eng] = o
        self.ops.append(o)
        if self.hook is not None and not self.in_hook:
            self.hook()
        return o

    def barrier(self, nopfn):
        o = Op("dve", nopfn, None, 16)
        o.idx = len(self.ops)
        for k, d in self.last_on.items():
            if d is None:
                continue
            if d.dma_key is None and d.eng == "dve":
                continue
            o.deps.append(d)
            if d.dma_key is None:
                d.signal = True
        self.last_on["dve"] = o
        self.ops.append(o)
        self.bar = o
        self.bar_seen = {"dve"}
        self.last_w = {}
        self.readers = {}
        return o

    def emit(self, nc, final_dma_keys=()):
        cnt = {e: 0 for e in ENGS}
        for o in self.ops:
            if o.dma_key is None and o.signal:
                cnt[o.eng] += 1
                o.count = cnt[o.eng]
        per_eng = {e: [] for e in ENGS}
        for o in self.ops:
            per_eng[o.eng].append(o)
        dma_keys = sorted(self.dma_cnt.keys())
        with contextlib.ExitStack() as st:
            sems = {}
            for e in ENGS:
                sems[e] = st.enter_context(nc.semaphore("s_" + e))
            for k in dma_keys:
                sems["dma_" + k] = st.enter_context(nc.semaphore("d_" + k))
            block = st.enter_context(nc.Block())

            def run(engname, engobj):
                waited = {}
                for o in per_eng[engname]:
                    need = {}
                    for d in o.deps:
                        if d.dma_key is not None:
                            s, v = "dma_" + d.dma_key, d.dma_cum
                        else:
                            s, v = d.eng, d.count
                        if v > need.get(s, 0):
                            need[s] = v
                    for s, v in need.items():
                        if waited.get(s, 0) >= v:
                            continue
                        engobj.wait_ge(sems[s], v)
                        waited[s] = v
                    ins = o.fn(engobj)
                    if DEBUG_TAGS:
                        try:
                            INS_TAGS[ins.ins.name] = o.tag
                        except Exception:
                            pass
                    if o.dma_key is not None:
                        ins.then_inc(sems["dma_" + o.dma_key], o.dma_inc)
                    elif o.signal:
                        ins.then_inc(sems[o.eng], 1)
                if engname == "sp":
                    for k in final_dma_keys:
                        engobj.wait_ge(sems["dma_" + k], self.dma_cnt[k])

            @block.tensor
            def _(e):
                run("pe", e)

            @block.scalar
            def _(e):
                run("act", e)

            @block.vector
            def _(e):
                run("dve", e)

            @block.gpsimd
            def _(e):
                run("pool", e)

            @block.sync
            def _(e):
                run("sp", e)


def slabify(W, ncols):
    K, N = W.shape
    return np.ascontiguousarray(W.reshape(K // 128, 128, N // ncols, ncols).transpose(2, 1, 0, 3))


def fm_vec(v):
    return np.ascontiguousarray(v.reshape(-1, 128).T)


V_MIXG0, V_FFNG0, V_MIXG1, V_FFNG1, V_FING = 0, 16, 32, 48, 64
V_INVF, V_SGN, V_SUBLN, V_SEL0, V_SEL1 = 80, 81, 82, 83, 84
V_LB = 88
V_GNORM = 152
NV = 168
R_LNG, R_LNB, R_BSB, R_LQ = 0, 1024, 2048, 3072
NR = 3072 + 256


class Ctx:
    pass


def build(stage="full"):
    nc = bass.Bass("TRN2", target_bir_lowering=False)
    P = Prog()
    c = Ctx()
    c.nc, c.P = nc, P
    dt_in = lambda name, shape: nc.dram_tensor(name, list(shape), F32, kind="ExternalInput").ap()
    c.xT = dt_in("xT", [128, KC, TA])
    c.pos = dt_in("pos", [1, TA])
    c.vecs_d = dt_in("vecs", [128, NV])
    c.rows_d = dt_in("rows", [1, NR])
    c.consts_d = dt_in("consts", [128, 4, 128])
    c.w_q = dt_in("w_q", [8, 128, KC, 128])
    c.w_k = dt_in("w_k", [8, 128, KC, 128])
    c.w_v = dt_in("w_v", [2, 128, KC, 512])
    c.w_u = dt_in("w_u", [8, 128, KC, 128])
    c.w_vb = dt_in("w_vb", [2, 128, KC, 512])
    c.w_s = dt_in("w_s", [128, 8, 128])
    c.w_eo = dt_in("w_eo", [16, 128, KC, 128])
    if stage in ("full", "l0", "ffn0"):
        c.w_g = dt_in("w_g", [2, 44, 128, KC, 128])
        c.w_up = dt_in("w_up", [2, 44, 128, KC, 128])
        c.w_dn = dt_in("w_dn", [2, 2, 16, 128, 22, 128])
    if stage in ("full", "l1"):
        c.w_hin = dt_in("w_hin", [5, 16, 128, KC, 128])
        c.w_ho = dt_in("w_ho", [16, 128, KC, 128])
    c.yT = nc.dram_tensor("yT", [128, KC, T], F32, kind="ExternalOutput").ap()
    c.cin = nc.dram_tensor("cin", [16 * 128, 128], F32).ap()
    c.cout = nc.dram_tensor("cout", [2 * 16 * 128, 128], F32).ap()

    ARENA_KB = 204
    arena = nc.alloc_sbuf_tensor("arena", [128, ARENA_KB * 512], BF16).ap()

    def sb(off_kb, nbytes, dtype, pattern=None, **kw):
        a = int(round(off_kb * 512))
        n = nbytes // 2
        assert a + n <= ARENA_KB * 512, (off_kb, nbytes)
        v = arena[:, a:a + n]
        if dtype == F32:
            v = v.bitcast(F32)
        if pattern:
            v = v.rearrange(pattern, **kw)
        return v

    c.sb = sb
    c.ps = [nc.alloc_psum_tensor(f"ps{i}", [128, 512], F32).ap() for i in range(8)]
    c.h = sb(0, 64 * 1024, F32, "p (k t) -> p k t", k=KC)
    c.hn = sb(64, 32 * 1024, BF16, "p (k t) -> p k t", k=KC)
    c.vecs = sb(144, NV * 4, F32)
    c.ident = sb(144.75, 256, BF16)
    c.ones = sb(145.0, 256, BF16)
    c.pm = sb(145.25, 256, BF16)
    c.tri = sb(145.5, 256, BF16)
    c.triT = sb(145.75, 256, BF16)
    c.lam = sb(146.0, 16, F32)
    c.epsb = sb(146.0625, 16, F32)
    c.cst32 = sb(200, 4 * 512, F32, "p (a b) -> p a b", a=4)
    c.wslot = [sb(160 + 4 * i, 4096, BF16, "p (k n) -> p k n", k=KC) for i in range(8)]
    c.wbig = [sb(160 + 16 * i, 16384, BF16, "p (k n) -> p k n", k=KC) for i in range(2)]
    c.wcnt = 0
    c.bigcnt = 0
    c.nring = 8

    load_consts(c)
    c.stage = stage
    if stage == "l1":
        for kc in range(KC):
            P.op("sp", lambda e, kc=kc: e.dma_start(out=c.h[:, kc, :], in_=c.xT[:, kc, 0:T]), writes=[f"h{kc}.0", f"h{kc}.1"], dma_key=f"x{kc}")
        layer1_mixer(c)
    elif stage == "ffn0":
        for kc in range(KC):
            P.op("sp", lambda e, kc=kc: e.dma_start(out=c.h[:, kc, :], in_=c.xT[:, kc, 0:T]), writes=[f"h{kc}.0", f"h{kc}.1"], dma_key=f"x{kc}")
        ffn(c, 0)
    elif stage in ("full", "l0") or stage.startswith("l0"):
        layer0_mixer(c)
        if stage in ("full", "l0"):
            ffn(c, 0)
    if stage in ("full",):
        layer1_mixer(c)
        ffn(c, 1)
        final_norm(c)
    else:
        P.barrier(lambda e: e.memset(c.lam[:, 2:3], 0.0))
        for kc in range(KC):
            P.op("sp", lambda e, kc=kc: e.dma_start(out=c.yT[:, kc, :], in_=c.h[:, kc, :]),
                 reads=[f"h{kc}.0", f"h{kc}.1"], writes=[f"y{kc}"], dma_key=f"out{kc % 8}")
    P.emit(nc, final_dma_keys=[f"out{i}" for i in range(8)])
    return nc


def wload(c, dram_slab, big=False, nk=KC):
    P = c.P
    if big:
        i = c.bigcnt % 2
        c.bigcnt += 1
        ap = c.wbig[i]
        keys = [f"ws{4 * i + j}" for j in range(4)]
        dk = f"wb{i}"
    else:
        i = c.wcnt % c.nring
        c.wcnt += 1
        ap = c.wslot[i]
        keys = [f"ws{i}"]
        dk = f"w{i}"
    dst = ap if nk == KC else ap[:, 0:nk, :]
    P.op("pool", lambda e: e.dma_start(out=dst, in_=dram_slab), writes=keys, dma_key=dk)
    return ap, keys


def load_consts(c):
    P, nc = c.P, c.nc
    P.op("dve", lambda e: e.memset(c.epsb[:, 0:1], EPS), writes=["epsb"])
    P.op("dve", lambda e: e.memset(c.epsb[:, 1:2], EPS / 0.64), reads=["epsb"], writes=["epsb"])
    P.op("dve", lambda e: e.memset(c.epsb[:, 2:3], 1.0), reads=["epsb"], writes=["epsb"])
    P.op("sp", lambda e: e.dma_start(out=c.vecs, in_=c.vecs_d), writes=["vecs"], dma_key="c")
    P.op("sp", lambda e: e.dma_start(out=c.cst32, in_=c.consts_d), writes=["cst32"], dma_key="c")
    for i, (ap, nm) in enumerate([(c.ident, "ident"), (c.ones, "ones"), (c.pm, "pm"), (c.tri, "tri")]):
        P.op("dve", lambda e, ap=ap, i=i: e.tensor_copy(ap, c.cst32[:, i, :]), reads=["cst32"], writes=[nm])
    psb = c.ps[7].bitcast(BF16)
    P.op("pe", lambda e: e.transpose(psb[:, 0:128], c.tri, c.ident), reads=["tri", "ident"], writes=["ps7"])
    P.op("dve", lambda e: e.tensor_copy(c.triT, psb[:, 0:128]), reads=["ps7"], writes=["triT"])


def rms_to_hn(c, src_fn, src_keys_fn, gcol, ntc, dst, dst_key_fn, tmp_off):
    P = c.P
    sq = [c.sb(tmp_off + i, 1024, BF16) for i in range(2)]
    rstd = c.sb(tmp_off + 2, 2048, F32)
    for tc in range(ntc):
        ps = c.ps[6 + (tc % 2)]
        psk = f"ps{6 + (tc % 2)}"
        for kc in range(KC):
            s = sq[kc % 2]
            sk = f"sq{kc % 2}"
            P.op("act", lambda e, s=s, kc=kc, tc=tc: e.activation(s, src_fn(kc, tc), AF.Square),
                 reads=src_keys_fn(kc, tc), writes=[sk])
            P.op("pe", lambda e, s=s, kc=kc, ps=ps: e.matmul(ps, c.ones, s, start=(kc == 0), stop=(kc == KC - 1)),
                 reads=[sk, "ones"], writes=[psk])
        P.op("act", lambda e, ps=ps: e.activation(rstd, ps, AF.Ln, bias=c.epsb[:, 0:1], scale=1.0 / D), reads=[psk, "epsb"], writes=["rstd"])
        P.op("act", lambda e: e.activation(rstd, rstd, AF.Exp, scale=-0.5), reads=["rstd"], writes=["rstd"])
        for kc in range(KC):
            P.op("dve", lambda e, kc=kc, tc=tc: e.scalar_tensor_tensor(dst(kc, tc), src_fn(kc, tc), c.vecs[:, gcol + kc:gcol + kc + 1], rstd, ALU.mult, ALU.mult),
                 reads=src_keys_fn(kc, tc) + ["rstd", "vecs"], writes=[dst_key_fn(kc, tc)])


def proj_fm(c, slab, slab_keys, rhs_fn, rhs_keys_fn, ps, psk, nk=KC):
    pairs = [(slab[:, kc, :], rhs_fn(kc)) for kc in range(nk)]
    reads = list(slab_keys)
    for kc in range(nk):
        reads += rhs_keys_fn(kc)

    def fn(e):
        ins = None
        for i, (l, r) in enumerate(pairs):
            ins = e.matmul(ps, l, r, start=(i == 0), stop=(i == nk - 1))
        return ins
    c.P.op("pe", fn, reads=reads, writes=[psk])


def layer0_mixer(c):
    P, nc, sb = c.P, c.nc, c.sb
    hn_oth = sb(96, 32 * 1024, BF16, "p (k t) -> p k t", k=KC)
    cat = hn_oth
    K_all = sb(0, 32 * 1024, BF16, "p (h t) -> p h t", h=8)
    V_all = sb(32, 32 * 1024, BF16, "p (b n) -> p b n", b=16)
    Ctab = sb(128, 8192, F32)
    Stab = sb(136, 8192, F32)
    rows = sb(147, NR * 4, F32)
    tmp_off = 192
    P.op("sp", lambda e: e.dma_start(out=rows, in_=c.rows_d.partition_broadcast(128)), writes=["rows"], dma_key="c")
    posb = sb(160, 8192, F32)
    P.op("sp", lambda e: e.dma_start(out=posb, in_=c.pos.partition_broadcast(128)), writes=["ws0", "ws1"], dma_key="c")
    TWO_PI = 2.0 * np.pi
    C1 = 6.28125
    C2 = TWO_PI - C1
    invf = c.vecs[:, V_INVF:V_INVF + 1]
    ang = sb(168, 8192, F32)
    kf = sb(176, 8192, F32)
    ki = sb(184, 8192, F32).bitcast(mybir.dt.int32)
    PI_IN = 3.1415925

    def make_table(dst, shift, key):
        P.op("dve", lambda e: e.tensor_scalar(ang, posb, invf, shift, ALU.mult, ALU.add), reads=["ws0", "ws1", "vecs"], writes=["ang"])
        P.op("dve", lambda e: e.tensor_scalar(kf, ang, 1.0 / TWO_PI, None, ALU.mult), reads=["ang"], writes=["kf"])
        P.op("dve", lambda e: e.tensor_copy(ki, kf), reads=["kf"], writes=["ki"])
        P.op("dve", lambda e: e.tensor_copy(kf, ki), reads=["ki"], writes=["kf"])
        P.op("dve", lambda e: e.scalar_tensor_tensor(ang, kf, -C1, ang, ALU.mult, ALU.add), reads=["kf", "ang"], writes=["ang"])
        P.op("dve", lambda e: e.scalar_tensor_tensor(ang, kf, -C2, ang, ALU.mult, ALU.add), reads=["kf", "ang"], writes=["ang"])
        P.op("dve", lambda e: e.tensor_scalar(kf, ang, np.pi, -TWO_PI, ALU.is_gt, ALU.mult), reads=["ang"], writes=["kf"])
        P.op("dve", lambda e: e.tensor_tensor(ang, ang, kf, ALU.add), reads=["kf", "ang"], writes=["ang"])
        P.op("dve", lambda e: e.tensor_scalar(kf, ang, -np.pi, TWO_PI, ALU.is_lt, ALU.mult), reads=["ang"], writes=["kf"])
        P.op("dve", lambda e: e.tensor_tensor(ang, ang, kf, ALU.add), reads=["kf", "ang"], writes=["ang"])
        P.op("dve", lambda e: e.tensor_scalar(ang, ang, -PI_IN, PI_IN, ALU.max, ALU.min), reads=["ang"], writes=["ang"])
        P.op("act", lambda e: e.activation(dst, ang, AF.Sin), reads=["ang"], writes=[key])

    make_table(Stab, 0.0, "Stab")
    P.op("dve", lambda e: e.tensor_scalar(Stab, Stab, c.vecs[:, V_SGN:V_SGN + 1], None, ALU.mult), reads=["Stab", "vecs"], writes=["Stab"])
    make_table(Ctab, np.pi / 2, "Ctab")
    lq = rows[:, R_LQ:R_LQ + 256].rearrange("p (a b) -> p a b", a=4)
    lt = sb(tmp_off, 512, F32, "p (a b) -> p a b", a=2)
    P.op("dve", lambda e: e.tensor_tensor(lt[:, 0, :], lq[:, 0, :], lq[:, 1, :], ALU.mult), reads=["rows"], writes=["lt"])
    P.op("dve", lambda e: e.tensor_tensor(lt[:, 1, :], lq[:, 2, :], lq[:, 3, :], ALU.mult), reads=["rows", "lt"], writes=["lt"])
    P.op("dve", lambda e: e.reduce_sum(c.lam[:, 2:4], lt, AX.X), reads=["lt"], writes=["lam"])
    P.op("act", lambda e: e.activation(c.lam[:, 2:4], c.lam[:, 2:4], AF.Exp), reads=["lam"], writes=["lam"])
    P.op("dve", lambda e: e.tensor_tensor(c.lam[:, 0:1], c.lam[:, 2:3], c.lam[:, 3:4], ALU.subtract), reads=["lam"], writes=["lam"])
    P.op("dve", lambda e: e.tensor_scalar(c.lam[:, 1:2], c.lam[:, 0:1], 0.2, -1.0, ALU.add, ALU.mult), reads=["lam"], writes=["lam"])

    stage = c.h
    for half in range(2):
        for kc in range(KC):
            P.op("sp", lambda e, kc=kc, half=half: e.dma_start(out=stage[:, kc, :], in_=c.xT[:, kc, half * T:(half + 1) * T]),
                 writes=[f"h{kc}.0", f"h{kc}.1"], dma_key=f"x{kc}")
        dstT = c.hn if half == 0 else hn_oth
        dkey = "hn" if half == 0 else "ho"
        rms_to_hn(c, lambda kc, tc: stage[:, kc, tc * 512:(tc + 1) * 512], lambda kc, tc: [f"h{kc}.{tc}"], V_MIXG0, 2,
                  lambda kc, tc, dstT=dstT: dstT[:, kc, tc * 512:(tc + 1) * 512], lambda kc, tc, dkey=dkey: f"{dkey}{kc}.{tc}", tmp_off)

    def hn_all(kc, tcc):
        src = c.hn if tcc < 2 else hn_oth
        return src[:, kc, (tcc % 2) * 512:(tcc % 2 + 1) * 512]

    def hn_all_keys(kc, tcc):
        return [f"{'hn' if tcc < 2 else 'ho'}{kc}.{tcc % 2}"]

    if c.stage == "l0A":
        return
    if c.stage == "l0Aw":
        slab, skeys = wload(c, c.w_k[0])
        proj_fm(c, slab, skeys, lambda kc: c.hn[:, kc, 0:512], lambda kc: [f"hn{kc}.0"], c.ps[2], "ps2")
        P.op("dve", lambda e: e.tensor_copy(c.h[:, 0, 0:512], c.ps[2]), reads=["ps2"], writes=["h0.0"])
        return
    P.barrier(lambda e: e.memset(c.lam[:, 2:3], 0.0))
    if c.stage == "l0Abar":
        P.op("act", lambda e: e.copy(c.h[:, 0, 0:512], c.h[:, 1, 0:512]), reads=[], writes=["h0.0"])
        P.op("pe", lambda e: e.matmul(c.ps[2], c.ones, c.hn[:, 0, 0:512], start=True, stop=True), reads=[], writes=["ps2"])
        P.op("dve", lambda e: e.tensor_copy(c.h[:, 2, 0:512], c.ps[2]), reads=["ps2"], writes=["h2.0"])
        return
    evi = [0]
    for cb in range(2 if c.stage not in ("l0B1k", "l0B1kn", "l0B1r1", "l0B1r2") else 0):
        slab, skeys = wload(c, c.w_v[cb], big=True)
        for tb in range(16):
            src = c.hn if tb < 8 else hn_oth
            sk = "hn" if tb < 8 else "ho"
            tcl = (tb % 8) // 4
            bank = tb % 2
            ps, psk = c.ps[bank], f"ps{bank}"
            pairs = [(src[:, kc, (tb % 8) * 128:(tb % 8 + 1) * 128], slab[:, kc, :]) for kc in range(KC)]

            def fn(e, pairs=pairs, ps=ps):
                ins = None
                for i, (l, r) in enumerate(pairs):
                    ins = e.matmul(ps, l, r, start=(i == 0), stop=(i == KC - 1))
                return ins
            P.op("pe", fn, reads=skeys + [f"{sk}{kc}.{tcl}" for kc in range(KC)], writes=[psk])
            dst = V_all[:, tb, cb * 512:(cb + 1) * 512]
            if tb % 2 == 0:
                P.op("act", lambda e, dst=dst, ps=ps: e.copy(dst, ps), reads=[psk], writes=[f"V{tb}.{cb}"])
            else:
                P.op("dve", lambda e, dst=dst, ps=ps: e.tensor_copy(dst, ps), reads=[psk], writes=[f"V{tb}.{cb}"])

    if c.stage == "l0B1v":
        return
    t1 = sb(tmp_off + 4, 2048, F32)
    t2 = sb(tmp_off + 6, 2048, F32)
    q16_b1 = [sb(tmp_off + 8 + i, 1024, BF16) for i in range(2)]
    q16_c = [sb(154 + i, 1024, BF16) for i in range(2)]

    def rope_block(ps_a, psk_a, ps_b, psk_b, tcc, dst, dst_key, i, q16):
        qb = q16[i % 2]
        qk = f"q16{i % 2}"
        if c.stage == "l0B1r1":
            P.op("act", lambda e: e.copy(qb, ps_a), reads=[psk_a], writes=[qk])
            P.op("pe", lambda e: e.matmul(ps_b, c.pm, qb, start=True, stop=True), reads=[qk, "pm"], writes=[psk_b])
            P.op("dve", lambda e: e.tensor_copy(dst, ps_b), reads=[psk_b], writes=[dst_key])
            return
        if c.stage == "l0B1r2":
            P.op("dve", lambda e: e.tensor_tensor(t1, ps_a, Ctab[:, tcc * 512:(tcc + 1) * 512], ALU.mult), reads=[psk_a, "Ctab"], writes=["t1"])
            P.op("dve", lambda e: e.tensor_tensor(t2, ps_a, Stab[:, tcc * 512:(tcc + 1) * 512], ALU.mult), reads=[psk_a, "Stab"], writes=["t2"])
            P.op("dve", lambda e: e.tensor_tensor(dst, t1, t2, ALU.add), reads=["t1", "t2"], writes=[dst_key])
            return
        P.op("act", lambda e: e.copy(qb, ps_a), reads=[psk_a], writes=[qk])
        P.op("pe", lambda e: e.matmul(ps_b, c.pm, qb, start=True, stop=True), reads=[qk, "pm"], writes=[psk_b])
        P.op("dve", lambda e: e.tensor_tensor(t1, ps_a, Ctab[:, tcc * 512:(tcc + 1) * 512], ALU.mult), reads=[psk_a, "Ctab", qk], writes=["t1"])
        P.op("dve", lambda e: e.tensor_tensor(t2, ps_b, Stab[:, tcc * 512:(tcc + 1) * 512], ALU.mult), reads=[psk_b, "Stab"], writes=["t2"])
        P.op("dve", lambda e: e.tensor_tensor(dst, t1, t2, ALU.add), reads=["t1", "t2"], writes=[dst_key])

    ri = 0
    for hd in range(8):
        slab, skeys = wload(c, c.w_k[hd])
        for tcc in range(4):
            ba, bb = 2 + (ri % 2) * 2, 3 + (ri % 2) * 2
            proj_fm(c, slab, skeys, lambda kc, tcc=tcc: hn_all(kc, tcc), lambda kc, tcc=tcc: hn_all_keys(kc, tcc), c.ps[ba], f"ps{ba}")
            if c.stage == "l0B1kn":
                P.op("act", lambda e, ba=ba, hd=hd, tcc=tcc: e.copy(K_all[:, hd, tcc * 512:(tcc + 1) * 512], c.ps[ba]), reads=[f"ps{ba}"], writes=[f"K{hd}.{tcc}"])
            else:
                rope_block(c.ps[ba], f"ps{ba}", c.ps[bb], f"ps{bb}", tcc, K_all[:, hd, tcc * 512:(tcc + 1) * 512], f"K{hd}.{tcc}", ri, q16_b1)
            ri += 1

    if c.stage.startswith("l0B1"):
        return
    P.barrier(lambda e: e.memset(c.lam[:, 2:3], 0.0))
    wsT = sb(tmp_off + 10, 2048, BF16, "p (g n) -> p g n", g=8)
    P.op("pool", lambda e: e.dma_start(out=wsT, in_=c.w_s), writes=["wsT"], dma_key="c2")
    for g in range(8):
        slab, skeys = wload(c, c.w_u[g])
        for tc in range(2):
            bank = 2 + (g * 2 + tc) % 2
            proj_fm(c, slab, skeys, lambda kc, tc=tc: c.hn[:, kc, tc * 512:(tc + 1) * 512], lambda kc, tc=tc: [f"hn{kc}.{tc}"], c.ps[bank], f"ps{bank}")
            P.op("act", lambda e, g=g, tc=tc, bank=bank: e.activation(cat[:, 8 + g, tc * 512:(tc + 1) * 512], c.ps[bank], AF.Gelu),
                 reads=[f"ps{bank}"], writes=[f"cat{8 + g}.{tc}"])
    vbg = sb(96, 16 * 1024, F32, "p (b n) -> p b n", b=4)
    vbn = sb(tmp_off + 1, 2048, BF16)
    sqj = sb(tmp_off + 4, 4096, F32)
    stats = sb(tmp_off, 64, F32)
    lng = rows[:, R_LNG:R_LNG + 1024]
    lnb = rows[:, R_LNB:R_LNB + 1024]
    bsb = rows[:, R_BSB:R_BSB + 1024].rearrange("p (g n) -> p g n", g=8)
    for th in range(2):
        for cb in range(2):
            slab, skeys = wload(c, c.w_vb[cb], big=True)
            for tbl in range(4):
                tb = th * 4 + tbl
                bank = tb % 2
                ps, psk = c.ps[bank], f"ps{bank}"
                pairs = [(c.hn[:, kc, tb * 128:(tb + 1) * 128], slab[:, kc, :]) for kc in range(KC)]

                def fn(e, pairs=pairs, ps=ps):
                    ins = None
                    for i, (l, r) in enumerate(pairs):
                        ins = e.matmul(ps, l, r, start=(i == 0), stop=(i == KC - 1))
                    return ins
                P.op("pe", fn, reads=skeys + [f"hn{kc}.{tb // 4}" for kc in range(KC)], writes=[psk])
                P.op("act", lambda e, tbl=tbl, cb=cb, ps=ps: e.activation(vbg[:, tbl, cb * 512:(cb + 1) * 512], ps, AF.Gelu),
                     reads=[psk], writes=[f"vbg{tbl}.{cb}"])
        for tbl in range(4):
            tb = th * 4 + tbl
            xv = vbg[:, tbl, :]
            rk = [f"vbg{tbl}.0", f"vbg{tbl}.1"]
            P.op("dve", lambda e, xv=xv: e.reduce_sum(stats[:, 0:1], xv, AX.X), reads=rk, writes=["stats"])
            P.op("dve", lambda e: e.tensor_scalar(stats[:, 1:2], stats[:, 0:1], -1.0 / 1024, None, ALU.mult), reads=["stats"], writes=["stats"])
            P.op("dve", lambda e, xv=xv: e.tensor_scalar(xv, xv, stats[:, 1:2], None, ALU.add), reads=rk + ["stats"], writes=rk)
            P.op("dve", lambda e, xv=xv: e.tensor_tensor(sqj, xv, xv, ALU.mult), reads=rk, writes=["sqj"])
            P.op("dve", lambda e: e.reduce_sum(stats[:, 2:3], sqj, AX.X), reads=["sqj"], writes=["stats"])
            P.op("act", lambda e: e.activation(stats[:, 3:4], stats[:, 2:3], AF.Sqrt, bias=c.epsb[:, 0:1], scale=1.0 / 1024), reads=["stats", "epsb"], writes=["stats"])
            P.op("dve", lambda e: e.reciprocal(stats[:, 3:4], stats[:, 3:4]), reads=["stats"], writes=["stats"])
            P.op("dve", lambda e, xv=xv: e.scalar_tensor_tensor(xv, xv, stats[:, 3:4], lng, ALU.mult, ALU.mult), reads=rk + ["stats", "rows"], writes=rk)
            P.op("dve", lambda e, xv=xv: e.tensor_tensor(vbn, xv, lnb, ALU.add), reads=rk + ["rows"], writes=["vbn"])
            for gh in range(2):
                bank = 2 + gh
                ps, psk = c.ps[bank], f"ps{bank}"

                def fn(e, gh=gh, ps=ps):
                    ins = None
                    for gl in range(4):
                        g = gh * 4 + gl
                        ins = e.matmul(ps[:, gl * 128:(gl + 1) * 128], vbn[:, g * 128:(g + 1) * 128], wsT[:, g, :], start=True, stop=True)
                    return ins
                P.op("pe", fn, reads=["vbn", "wsT"], writes=[psk])
                svt = sb(tmp_off + 8, 2048, F32, "p (g n) -> p g n", g=4)
                P.op("dve", lambda e, ps=ps, gh=gh: e.tensor_tensor(svt, ps.rearrange("p (g n) -> p g n", g=4), bsb[:, gh * 4:(gh + 1) * 4, :], ALU.add),
                     reads=[psk, "rows"], writes=["svt"])
                cv = cat[:, 8 + gh * 4:8 + gh * 4 + 4, tb * 128:(tb + 1) * 128]
                ck = [f"cat{8 + gh * 4 + gl}.{tb // 4}" for gl in range(4)]
                P.op("dve", lambda e, cv=cv: e.tensor_tensor(cv, cv, svt, ALU.mult), reads=["svt"] + ck, writes=ck)

    if c.stage == "l0B2":
        return
    P.barrier(lambda e: e.memset(c.lam[:, 2:3], 0.0))
    Et = [[sb(tmp_off + 8 + 2 * m + b, 1024, BF16) for b in range(2)] for m in range(2)]
    qrot = sb(tmp_off + 1, 2048, BF16)
    r0 = sb(156, 2048, F32)
    r1 = sb(158, 2048, F32)
    oT = sb(147, 2048, F32)
    a0 = sb(149, 2048, F32)
    sqb = sb(151, 1024, BF16)
    rs2 = sb(152, 2048, F32)
    scale = 64 ** -0.5
    ri = 0
    for hd in range(8):
        slab, skeys = wload(c, c.w_q[hd])
        for tc in range(2):
            proj_fm(c, slab, skeys, lambda kc, tc=tc: c.hn[:, kc, tc * 512:(tc + 1) * 512], lambda kc, tc=tc: [f"hn{kc}.{tc}"], c.ps[6], "ps6")
            rope_block(c.ps[6], "ps6", c.ps[7], "ps7", tc, qrot[:, tc * 512:(tc + 1) * 512], f"qrot{tc}", ri, q16_c)
            ri += 1
        for tc in range(2):
            SB = [0, 1, 6, 7]

            def emit_scores(j, tc=tc, hd=hd):
                for m in range(2):
                    b = SB[(j % 2) * 2 + m]
                    P.op("pe", lambda e, m=m, j=j, tc=tc, b=b, hd=hd: e.matmul(c.ps[b], K_all[64 * m:64 * m + 64, hd, j * 128:(j + 1) * 128],
                                                                            qrot[64 * m:64 * m + 64, tc * 512:(tc + 1) * 512], start=True, stop=True),
                         reads=[f"K{hd}.{j // 4}", f"qrot{tc}"], writes=[f"ps{b}"])

            emit_scores(0)
            for j in range(16):
                for m in range(2):
                    b = SB[(j % 2) * 2 + m]
                    E = Et[m][j % 2]
                    ek = f"E{m}.{j % 2}"
                    P.op("act", lambda e, E=E, b=b: e.activation(E, c.ps[b], AF.Exp, scale=scale), reads=[f"ps{b}"], writes=[ek])
                if j < 15:
                    emit_scores(j + 1)
                for m in range(2):
                    E = Et[m][j % 2]
                    ek = f"E{m}.{j % 2}"
                    P.op("pe", lambda e, m=m, j=j, E=E, hd=hd: e.matmul(c.ps[2 + 2 * m], V_all[:, j, hd * 128:(hd + 1) * 128], E, start=(j == 0), stop=(j == 15)),
                         reads=[ek, f"V{j}.{hd // 4}"], writes=[f"ps{2 + 2 * m}"])
                    P.op("pe", lambda e, m=m, j=j, E=E: e.matmul(c.ps[3 + 2 * m], c.ones, E, start=(j == 0), stop=(j == 15)),
                         reads=[ek, "ones"], writes=[f"ps{3 + 2 * m}"])
            B0, B1, B2, B3 = r0, r1, a0, oT
            P.op("dve", lambda e: e.tensor_copy(B2, c.ps[2]), reads=["ps2"], writes=["a0"])
            P.op("dve", lambda e: e.tensor_copy(B0, c.ps[3]), reads=["ps3"], writes=["cb0"])
            P.op("dve", lambda e: e.tensor_copy(B3, c.ps[4]), reads=["ps4"], writes=["oT"])
            P.op("dve", lambda e: e.tensor_copy(B1, c.ps[5]), reads=["ps5"], writes=["cb1"])
            P.op("dve", lambda e: e.reciprocal(B0, B0), reads=["cb0"], writes=["cb0"])
            P.op("dve", lambda e: e.reciprocal(B1, B1), reads=["cb1"], writes=["cb1"])
            P.op("dve", lambda e: e.tensor_tensor(B2, B2, B0, ALU.mult), reads=["a0", "cb0"], writes=["a0"])
            P.op("dve", lambda e: e.tensor_tensor(B3, B3, B1, ALU.mult), reads=["oT", "cb1"], writes=["oT"])
            P.op("dve", lambda e: e.scalar_tensor_tensor(B0, B3, c.lam[:, 1:2], B2, ALU.mult, ALU.add), reads=["oT", "a0", "lam", "cb0"], writes=["cb0"])
            oTf = B0
            P.op("act", lambda e: e.activation(sqb, oTf, AF.Square), reads=["cb0"], writes=["sqb"])
            P.op("pe", lambda e: e.matmul(c.ps[6], c.ones, sqb, start=True, stop=True), reads=["sqb", "ones"], writes=["ps6"])
            P.op("act", lambda e: e.activation(rs2, c.ps[6], AF.Ln, bias=c.epsb[:, 1:2], scale=1.0 / (128 * 0.64)), reads=["ps6", "epsb"], writes=["rs2"])
            P.op("act", lambda e: e.activation(rs2, rs2, AF.Exp, scale=-0.5), reads=["rs2"], writes=["rs2"])
            P.op("dve", lambda e, tc=tc, hd=hd: e.scalar_tensor_tensor(cat[:, hd, tc * 512:(tc + 1) * 512], oTf, c.vecs[:, V_SUBLN:V_SUBLN + 1], rs2, ALU.mult, ALU.mult),
                 reads=["cb0", "rs2", "vecs"], writes=[f"cat{hd}.{tc}"])

    if c.stage == "l0C":
        P.barrier(lambda e: e.memset(c.lam[:, 2:3], 0.0))
        for kc in range(KC):
            P.op("dve", lambda e, kc=kc: e.tensor_copy(c.h[:, kc, :], cat[:, kc, :]), writes=[f"h{kc}.0", f"h{kc}.1"])
        return
    P.barrier(lambda e: e.memset(c.lam[:, 2:3], 0.0))
    for db in range(16):
        slab, skeys = wload(c, c.w_eo[db])
        P.op("sp", lambda e, db=db: e.dma_start(out=c.h[:, db, :], in_=c.xT[:, db, 0:T]), writes=[f"h{db}.0", f"h{db}.1"], dma_key=f"x{db}")
        for tc in range(2):
            bank = (db * 2 + tc) % 4
            proj_fm(c, slab, skeys, lambda kc, tc=tc: cat[:, kc, tc * 512:(tc + 1) * 512], lambda kc, tc=tc: [f"cat{kc}.{tc}"], c.ps[bank], f"ps{bank}")
            hv = c.h[:, db, tc * 512:(tc + 1) * 512]
            P.op("dve", lambda e, hv=hv, bank=bank: e.tensor_tensor(hv, hv, c.ps[bank], ALU.add), reads=[f"ps{bank}", f"h{db}.{tc}"], writes=[f"h{db}.{tc}"])


def ffn(c, l):
    P, sb = c.P, c.sb
    P.barrier(lambda e: e.memset(c.lam[:, 2:3], 0.0))
    tmp_off = 192
    gcol = V_FFNG0 if l == 0 else V_FFNG1
    rms_to_hn(c, lambda kc, tc: c.h[:, kc, tc * 512:(tc + 1) * 512], lambda kc, tc: [f"h{kc}.{tc}"], gcol, 2,
              lambda kc, tc: c.hn[:, kc, tc * 512:(tc + 1) * 512], lambda kc, tc: f"hn{kc}.{tc}", tmp_off)
    act = sb(96, 44 * 1024, BF16, "p (j t) -> p j t", j=22)
    sg = [sb(tmp_off + 4 + 2 * i, 2048, F32) for i in range(4)]
    gi = 0
    for half in range(2):
        for hb in range(22):
            sl_g, kg = wload(c, c.w_g[l, half * 22 + hb])
            sl_u, ku = wload(c, c.w_up[l, half * 22 + hb])
            for tc in range(2):
                bg = (gi % 2) * 4 + tc * 2
                bu = bg + 1
                rf = lambda kc, tc=tc: c.hn[:, kc, tc * 512:(tc + 1) * 512]
                rk = lambda kc, tc=tc: [f"hn{kc}.{tc}"]
                proj_fm(c, sl_g, kg, rf, rk, c.ps[bg], f"ps{bg}")
                proj_fm(c, sl_u, ku, rf, rk, c.ps[bu], f"ps{bu}")
                s = sg[(gi * 2 + tc) % 4]
                sk = f"sg{(gi * 2 + tc) % 4}"
                P.op("act", lambda e, s=s, bg=bg: e.activation(s, c.ps[bg], AF.Silu), reads=[f"ps{bg}"], writes=[sk])
                P.op("dve", lambda e, s=s, bu=bu, hb=hb, tc=tc: e.tensor_tensor(act[:, hb, tc * 512:(tc + 1) * 512], s, c.ps[bu], ALU.mult),
                     reads=[sk, f"ps{bu}"], writes=[f"act{hb}.{tc}"])
            gi += 1
        for db in range(16):
            wl = []
            for part in range(2):
                slab, sk_ = wload(c, c.w_dn[l, half, db, :, part * 11:(part + 1) * 11, :], nk=11)
                wl.append((slab, sk_))
            for tc in range(2):
                bank = (db * 2 + tc) % 4 if (gi % 2 == 0) else 4 + (db * 2 + tc) % 4
                ps, psk = c.ps[bank], f"ps{bank}"
                pairs = []
                reads = []
                for part in range(2):
                    slab, sk_ = wl[part]
                    reads += sk_
                    for jj in range(11):
                        j = part * 11 + jj
                        pairs.append((slab[:, jj, :], act[:, j, tc * 512:(tc + 1) * 512]))
                        reads.append(f"act{j}.{tc}")

                def fn(e, pairs=pairs, ps=ps):
                    ins = None
                    n = len(pairs)
                    for i, (lh, r) in enumerate(pairs):
                        ins = e.matmul(ps, lh, r, start=(i == 0), stop=(i == n - 1))
                    return ins
                P.op("pe", fn, reads=reads, writes=[psk])
                hv = c.h[:, db, tc * 512:(tc + 1) * 512]
                P.op("dve", lambda e, hv=hv, ps=ps: e.tensor_tensor(hv, hv, ps, ALU.add), reads=[psk, f"h{db}.{tc}"], writes=[f"h{db}.{tc}"])


def layer1_mixer(c):
    P, nc, sb = c.P, c.nc, c.sb
    P.barrier(lambda e: e.memset(c.lam[:, 2:3], 0.0))
    rms_to_hn(c, lambda kc, tc: c.h[:, kc, tc * 512:(tc + 1) * 512], lambda kc, tc: [f"h{kc}.{tc}"], V_MIXG1, 2,
              lambda kc, tc: c.hn[:, kc, tc * 512:(tc + 1) * 512], lambda kc, tc: f"hn{kc}.{tc}", 192)
    P.barrier(lambda e: e.memset(c.lam[:, 2:3], 0.0))
    y = sb(96, 32 * 1024, BF16, "p (k t) -> p k t", k=KC)
    A = [sb(128 + 4 * i, 4096, F32) for i in range(4)] + [sb(147 + 4 * i, 4096, F32) for i in range(3)]
    QB = sb(192, 2048, BF16)
    KB = sb(194, 2048, BF16)
    KBT = sb(196, 4096, BF16, "p (c k) -> p c k", c=16)
    VT = sb(200, 4096, BF16, "p (c k) -> p c k", c=16)
    SCT = sb(155, 2048, BF16, "p (c t) -> p c t", c=16)
    msk = sb(157, 1024, BF16)
    TB = [sb(158 + 0.5 * i, 512, F32) for i in range(4)]
    SBFR = [sb(153.5 + 0.25 * i, 256, BF16) for i in range(4)]
    S32 = sb(153, 512, F32)
    EBL = sb(146.5, 64, F32)
    LBV = sb(146.5625, 4 * 64, F32, "p (a h) -> p a h", a=4)
    IT = A[5]
    ITb = sb(147 + 4 * 1, 2048, BF16)
    P.op("dve", lambda e: e.memset(msk, 1.0), writes=["msk"])
    P.op("dve", lambda e: e.memset(msk.rearrange("p (c t) -> p c t", t=64)[:, :, 0:1], 0.0), reads=["msk"], writes=["msk"])
    for ld in range(2):
        r0c = V_LB + (ld * 2 + 0) * 16
        r1c = V_LB + (ld * 2 + 1) * 16
        P.op("dve", lambda e, ld=ld, r0c=r0c, r1c=r1c: e.tensor_tensor(LBV[:, 2 * ld, :], c.vecs[:, r1c:r1c + 16], c.vecs[:, r0c:r0c + 16], ALU.subtract),
             reads=["vecs", "LBV"], writes=["LBV"])
        P.op("act", lambda e, ld=ld: e.activation(LBV[:, 2 * ld, :], LBV[:, 2 * ld, :], AF.Sigmoid), reads=["LBV"], writes=["LBV"])
        P.op("dve", lambda e, ld=ld: e.tensor_scalar(LBV[:, 2 * ld + 1, :], LBV[:, 2 * ld, :], -1.0, 1.0, ALU.mult, ALU.add), reads=["LBV"], writes=["LBV"])

    def proj2(wslab, dst_fn, func, dkey, scale=None):
        slab, skeys = wload(c, wslab)
        for tc in range(2):
            bank = tc
            proj_fm(c, slab, skeys, lambda kc, tc=tc: c.hn[:, kc, tc * 512:(tc + 1) * 512], lambda kc, tc=tc: [f"hn{kc}.{tc}"], c.ps[bank], f"ps{bank}")
            P.op("act", lambda e, tc=tc, bank=bank: e.activation(dst_fn(tc), c.ps[bank], func), reads=[f"ps{bank}"], writes=[dkey])

    SQs = [A[0], sb(180, 4096, F32)]
    FVs = [A[1], sb(184, 4096, F32)]
    ITBs = [ITb, sb(188, 2048, BF16)]

    def head_P(hd, ld, bs):
        for wsel, dstT, func, key in ((0, SQs[bs], AF.Silu, f"A0.{bs}"), (1 + ld, FVs[bs], None, f"A1.{bs}"), (3, ITBs[bs], AF.Copy, f"A5.{bs}")):
            slab, skeys = wload(c, c.w_hin[wsel, hd])
            for tc in range(2):
                bank = tc
                proj_fm(c, slab, skeys, lambda kc, tc=tc: c.hn[:, kc, tc * 512:(tc + 1) * 512], lambda kc, tc=tc: [f"hn{kc}.{tc}"], c.ps[bank], f"ps{bank}")
                dv = dstT[:, tc * 512:(tc + 1) * 512]
                if func is None:
                    P.op("act", lambda e, dv=dv, bank=bank: e.activation(dv, c.ps[bank], AF.Exp, scale=-1.0), reads=[f"ps{bank}"], writes=[key])
                else:
                    P.op("act", lambda e, dv=dv, bank=bank, func=func: e.activation(dv, c.ps[bank], func), reads=[f"ps{bank}"], writes=[key])
            if func is None:
                P.op("act", lambda e, dstT=dstT: e.activation(dstT, dstT, AF.Ln, bias=c.epsb[:, 2:3], scale=1.0), reads=[key, "epsb"], writes=[key])
                P.op("act", lambda e, dstT=dstT: e.activation(dstT, dstT, AF.Exp, scale=-1.0), reads=[key], writes=[key])
            yield

    def head_pass(hd, ld, bs):
        lb = LBV[:, 2 * ld, hd:hd + 1]
        oml = LBV[:, 2 * ld + 1, hd:hd + 1]
        sq, fv, kk, lf, bb = SQs[bs], FVs[bs], A[2], A[3], A[4]
        ITb = ITBs[bs]
        kA0, kA1, kA5 = f"A0.{bs}", f"A1.{bs}", f"A5.{bs}"
        P.op("dve", lambda e: e.tensor_scalar(fv, fv, oml, lb, ALU.mult, ALU.add), reads=[kA1, "LBV"], writes=[kA1])
        P.op("dve", lambda e: e.tensor_scalar(kk, fv, -1.0, 1.0, ALU.mult, ALU.add), reads=[kA1], writes=["A2"])
        P.op("act", lambda e: e.activation(lf, fv, AF.Ln), reads=[kA1], writes=["A3"])
        for tc in range(2):
            P.op("dve", lambda e, tc=tc: e.tensor_tensor_scan(bb[:, tc * 512:(tc + 1) * 512], msk, lf[:, tc * 512:(tc + 1) * 512], 0.0, ALU.mult, ALU.add),
                 reads=["A3", "msk"], writes=["A4"])
        b3 = bb.rearrange("p (c t) -> p c t", t=64)
        if ld == 1:
            P.op("dve", lambda e: e.tensor_tensor(lf, lf, bb, ALU.subtract), reads=["A3", "A4"], writes=["A3"])
            P.op("dve", lambda e: e.tensor_tensor(b3, lf.rearrange("p (c t) -> p c t", t=64), b3[:, :, 63:64].to_broadcast([128, 16, 64]), ALU.add),
                 reads=["A3", "A4"], writes=["A4"])
        eb, enb = fv, A[3]
        P.op("act", lambda e: e.activation(eb, bb, AF.Exp), reads=["A4", kA1], writes=[kA1])
        P.op("act", lambda e: e.activation(enb, bb, AF.Exp, scale=-1.0), reads=["A4", "A3"], writes=["A3"])
        P.op("dve", lambda e: e.tensor_tensor(QB, sq, eb, ALU.mult), reads=[kA0, kA1], writes=["QB"])
        P.op("dve", lambda e: e.tensor_tensor(KB, kk, enb, ALU.mult), reads=["A2", "A3"], writes=["KB"])
        e3 = eb.rearrange("p (c t) -> p c t", t=64)
        edge = 63 if ld == 0 else 0
        P.op("dve", lambda e: e.tensor_copy(EBL, e3[:, :, edge]), reads=[kA1], writes=["EBL"])
        for src, skey, dst, dkey in ((ITb, kA5, VT, "VT"), (KB, "KB", KBT, "KBT")):
            for half in range(2):
                bank = 4 + half
                pst = c.ps[bank].bitcast(BF16).rearrange("p (c k) -> p c k", k=128)

                def fn(e, src=src, half=half, pst=pst):
                    ins = None
                    for cl in range(8):
                        ch = half * 8 + cl
                        ins = e.transpose(pst[0:64, cl, :], src[:, ch * 64:(ch + 1) * 64], c.ident)
                    return ins
                P.op("pe", fn, reads=[skey, "ident"], writes=[f"ps{bank}"])
                P.op("act" if half == 0 else "dve",
                     (lambda e, dst=dst, half=half, pst=pst: e.copy(dst[0:64, half * 8:(half + 1) * 8, :], pst[0:64, :, :])) if half == 0 else
                     (lambda e, dst=dst, half=half, pst=pst: e.tensor_copy(dst[0:64, half * 8:(half + 1) * 8, :], pst[0:64, :, :])),
                     reads=[f"ps{bank}"], writes=[dkey])
        msk2 = (c.tri if ld == 0 else c.triT)[0:64, 0:64]
        for half in range(2):
            bank = 6 + half
            psv = c.ps[bank].rearrange("p (c t) -> p c t", t=64)

            def fn(e, half=half, psv=psv):
                ins = None
                for cl in range(8):
                    ch = half * 8 + cl
                    ins = e.matmul(psv[0:64, cl, :], KB[:, ch * 64:(ch + 1) * 64], QB[:, ch * 64:(ch + 1) * 64], start=True, stop=True)
                return ins
            P.op("pe", fn, reads=["KB", "QB"], writes=[f"ps{bank}"])
            P.op("dve", lambda e, half=half, psv=psv: e.tensor_tensor(SCT[0:64, half * 8:(half + 1) * 8, :], psv[0:64, :, :],
                                                                     msk2.unsqueeze(1).to_broadcast([64, 8, 64]), ALU.mult),
                 reads=[f"ps{bank}", "tri", "triT"], writes=["SCT"])
        order = list(range(16)) if ld == 0 else list(range(15, -1, -1))
        for q4 in range(4):
            ubank = 4 + q4

            def fu(e, q4=q4, ubank=ubank):
                ins = None
                for i4 in range(4):
                    ch = order[q4 * 4 + i4]
                    ins = e.matmul(c.ps[ubank][:, i4 * 128:(i4 + 1) * 128], KBT[0:64, ch, :], VT[0:64, ch, :], start=True, stop=True)
                return ins
            P.op("pe", fu, reads=["KBT", "VT"], writes=[f"ps{ubank}"])
        if ld == 0:
            P.op("dve", lambda e: e.memset(TB[3], 0.0), writes=["TB3"])
        else:
            P.op("sp", lambda e: e.dma_start(out=TB[3], in_=c.cout[hd * 128:(hd + 1) * 128, :]), reads=["cout"], writes=["TB3"], dma_key="st0")
            P.op("sp", lambda e: e.dma_start(out=TB[2], in_=c.cout[2048 + hd * 128:2048 + (hd + 1) * 128, :]), reads=["cout"], writes=["TB2"], dma_key="st1")
            P.op("dve", lambda e: e.tensor_scalar(TB[3], TB[3], c.vecs[:, V_SEL0:V_SEL0 + 1], None, ALU.mult), reads=["TB3", "vecs"], writes=["TB3"])
            P.op("dve", lambda e: e.scalar_tensor_tensor(TB[3], TB[2], c.vecs[:, V_SEL1:V_SEL1 + 1], TB[3], ALU.mult, ALU.add), reads=["TB3", "TB2", "vecs"], writes=["TB3"])
        P.op("act", lambda e: e.copy(SBFR[3], TB[3]), reads=["TB3"], writes=["SBF3"])
        for n, ch in enumerate(order):
            un = c.ps[4 + n // 4][:, (n % 4) * 128:(n % 4 + 1) * 128]
            prev = (n - 1) % 4
            cur = n % 4
            if n == 0:
                P.op("dve", lambda e, un=un, prev=prev, cur=cur: e.tensor_tensor(TB[cur], TB[prev], un, ALU.add),
                     reads=[f"TB{prev}", f"ps{4 + n // 4}"], writes=[f"TB{cur}"])
            else:
                ep = EBL[:, order[n - 1]:order[n - 1] + 1]
                P.op("dve", lambda e, un=un, prev=prev, cur=cur, ep=ep: e.scalar_tensor_tensor(TB[cur], TB[prev], ep, un, ALU.mult, ALU.add),
                     reads=[f"TB{prev}", f"ps{4 + n // 4}", "EBL"], writes=[f"TB{cur}"])
            obank = 2 + (ch // 8)
            pso = c.ps[obank].rearrange("p (c t) -> p c t", t=64)[:, ch % 8, :]

            def fo(e, ch=ch, pso=pso, prev=prev):
                e.matmul(pso, VT[0:64, ch, :], SCT[0:64, ch, :], start=True, stop=False)
                return e.matmul(pso, SBFR[prev], QB[:, ch * 64:(ch + 1) * 64], start=False, stop=True)
            P.op("pe", fo, reads=["VT", "SCT", f"SBF{prev}", "QB"], writes=[f"ps{obank}"])
            ec = EBL[:, ch:ch + 1]
            if n < 15:
                P.op("act", lambda e, cur=cur, ec=ec: e.activation(SBFR[cur], TB[cur], AF.Copy, scale=ec), reads=[f"TB{cur}", "EBL"], writes=[f"SBF{cur}"])
            else:
                P.op("act", lambda e, cur=cur, ec=ec: e.activation(S32, TB[cur], AF.Copy, scale=ec), reads=[f"TB{cur}", "EBL"], writes=["S32"])
            if (n % 8) == 7:
                t0 = (ch // 8) * 512
                yv = y[:, hd, t0:t0 + 512]
                if ld == 0:
                    P.op("act", lambda e, yv=yv, obank=obank: e.copy(yv, c.ps[obank]), reads=[f"ps{obank}"], writes=[f"y{hd}.{ch // 8}"])
                else:
                    P.op("dve", lambda e, yv=yv, obank=obank: e.tensor_tensor(yv, yv, c.ps[obank], ALU.add), reads=[f"ps{obank}", f"y{hd}.{ch // 8}"], writes=[f"y{hd}.{ch // 8}"])
        if ld == 0:
            P.op("sp", lambda e: e.dma_start(out=c.cin[hd * 128:(hd + 1) * 128, :], in_=S32), reads=["S32"], writes=["cin"], dma_key=f"ci{hd % 4}")

    c.nring = 5
    seq = [(hd, 0) for hd in range(16)] + [(hd, 1) for hd in range(16)]
    gens = [head_P(hd, ld, i % 2) for i, (hd, ld) in enumerate(seq)]
    for _ in gens[0]:
        pass
    for i, (hd, ld) in enumerate(seq):
        g = gens[i + 1] if i + 1 < len(seq) else None
        cnt = [0]

        def hook(g=g, cnt=cnt):
            cnt[0] += 1
            if g is not None and cnt[0] % 4 == 0:
                P.in_hook = True
                try:
                    next(g)
                except StopIteration:
                    pass
                P.in_hook = False
        P.hook = hook
        head_pass(hd, ld, i % 2)
        P.hook = None
        if g is not None:
            for _ in g:
                pass
        if i == 15:
            P.op("pool", lambda e: e.collective_compute("AllGather", ALU.bypass, [[0, 1], [2, 3], [4, 5], [6, 7]], [c.cin.opt()], [c.cout.opt()]),
                 reads=["cin"], writes=["cout"], dma_key="cc", dma_inc=1)
    c.nring = 8

    P.barrier(lambda e: e.memset(c.lam[:, 2:3], 0.0))
    sqy = [sb(192 + i, 1024, BF16) for i in range(2)]
    rstd2 = A[0]
    for tc in range(2):
        ps, psk = c.ps[6 + tc], f"ps{6 + tc}"
        for hd in range(16):
            s_ = sqy[hd % 2]
            sk = f"sqy{hd % 2}"
            P.op("act", lambda e, s_=s_, hd=hd, tc=tc: e.activation(s_, y[:, hd, tc * 512:(tc + 1) * 512], AF.Square), reads=[f"y{hd}.{tc}"], writes=[sk])
            P.op("pe", lambda e, s_=s_, hd=hd, ps=ps: e.matmul(ps, c.ones, s_, start=(hd == 0), stop=(hd == 15)), reads=[sk, "ones"], writes=[psk])
        rv = rstd2[:, tc * 512:(tc + 1) * 512]
        P.op("act", lambda e, ps=ps, rv=rv: e.activation(rv, ps, AF.Ln, bias=c.epsb[:, 0:1], scale=1.0 / D), reads=[psk, "epsb"], writes=["A0"])
        P.op("act", lambda e, rv=rv: e.activation(rv, rv, AF.Exp, scale=-0.5), reads=["A0"], writes=["A0"])
    sgt = [sb(128 + 4 + 2 * i, 2048, F32) for i in range(2)]
    ytmp = sb(128 + 8, 2048, F32)
    for hd in range(16):
        slab, skeys = wload(c, c.w_hin[4, hd])
        for tc in range(2):
            bank = tc
            proj_fm(c, slab, skeys, lambda kc, tc=tc: c.hn[:, kc, tc * 512:(tc + 1) * 512], lambda kc, tc=tc: [f"hn{kc}.{tc}"], c.ps[bank], f"ps{bank}")
            sg_ = sgt[tc]
            P.op("act", lambda e, sg_=sg_, bank=bank: e.activation(sg_, c.ps[bank], AF.Silu), reads=[f"ps{bank}"], writes=[f"sgt{tc}"])
            yv = y[:, hd, tc * 512:(tc + 1) * 512]
            P.op("dve", lambda e, yv=yv, hd=hd, tc=tc: e.scalar_tensor_tensor(ytmp, yv, c.vecs[:, V_GNORM + hd:V_GNORM + hd + 1], rstd2[:, tc * 512:(tc + 1) * 512], ALU.mult, ALU.mult),
                 reads=[f"y{hd}.{tc}", "A0", "vecs"], writes=["ytmp"])
            P.op("dve", lambda e, yv=yv, sg_=sg_: e.tensor_tensor(yv, ytmp, sg_, ALU.mult), reads=["ytmp", f"sgt{tc}"], writes=[f"y{hd}.{tc}"])
    for db in range(16):
        slab, skeys = wload(c, c.w_ho[db])
        for tc in range(2):
            bank = 2 + (db * 2 + tc) % 4
            proj_fm(c, slab, skeys, lambda kc, tc=tc: y[:, kc, tc * 512:(tc + 1) * 512], lambda kc, tc=tc: [f"y{kc}.{tc}"], c.ps[bank], f"ps{bank}")
            hv = c.h[:, db, tc * 512:(tc + 1) * 512]
            P.op("dve", lambda e, hv=hv, bank=bank: e.tensor_tensor(hv, hv, c.ps[bank], ALU.add), reads=[f"ps{bank}", f"h{db}.{tc}"], writes=[f"h{db}.{tc}"])


def final_norm(c):
    P, sb = c.P, c.sb
    P.barrier(lambda e: e.memset(c.lam[:, 2:3], 0.0))
    rms_to_hn(c, lambda kc, tc: c.h[:, kc, tc * 512:(tc + 1) * 512], lambda kc, tc: [f"h{kc}.{tc}"], V_FING, 2,
              lambda kc, tc: c.h[:, kc, tc * 512:(tc + 1) * 512], lambda kc, tc: f"h{kc}.{tc}", 192)
    for kc in range(KC):
        P.op("sp", lambda e, kc=kc: e.dma_start(out=c.yT[:, kc, :], in_=c.h[:, kc, :]),
             reads=[f"h{kc}.0", f"h{kc}.1"], writes=[f"yo{kc}"], dma_key=f"out{kc % 8}")


def host_prep(inp):
    f32 = np.float32
    x = inp["x"]
    shared = {}
    W = inp["even_w_in"][0]
    shared["w_q"] = slabify(W[:, 0:1024], 128)
    shared["w_k"] = slabify(W[:, 1024:2048], 128)
    shared["w_v"] = slabify(W[:, 2048:3072], 512)
    shared["w_u"] = slabify(W[:, 3072:4096], 128)
    shared["w_vb"] = slabify(W[:, 4096:5120], 512)
    shared["w_eo"] = slabify(inp["even_w_out"][0], 128)
    shared["w_g"] = np.stack([slabify(inp["ffn_w_gate"][l], 128) for l in range(2)])
    shared["w_up"] = np.stack([slabify(inp["ffn_w_up"][l], 128) for l in range(2)])
    wd = inp["ffn_w_down"]
    shared["w_dn"] = np.ascontiguousarray(wd.reshape(2, 2, 22, 128, 16, 128).transpose(0, 1, 4, 3, 2, 5))
    shared["w_ho"] = slabify(inp["hgrn_w_out"][0], 128)
    Wh = inp["hgrn_w_in"][0]
    hin = [slabify(Wh[:, i * 2048:(i + 1) * 2048], 128) for i in range(5)]
    hin_even = np.stack([hin[0], hin[1], hin[2], hin[3], hin[4]])
    hin_odd = np.stack([hin[0], hin[2], hin[1], hin[3], hin[4]])
    consts = np.zeros((128, 4, 128), f32)
    consts[:, 0, :] = np.eye(128, dtype=f32)
    consts[:, 1, :] = 1.0
    pm = np.zeros((128, 128), f32)
    for base in (0, 64):
        for d in range(8):
            pm[base + d + 8, base + d] = 1.0
            pm[base + d, base + d + 8] = 1.0
    consts[:, 2, :] = pm
    consts[:, 3, :] = np.triu(np.ones((128, 128), f32))
    half = 8
    invf_vals = (500000.0 ** (-np.arange(half, dtype=np.float64) / half)).astype(f32)
    invf = np.zeros(128, f32)
    sgn = np.zeros(128, f32)
    for base in (0, 64):
        invf[base:base + 8] = invf_vals
        invf[base + 8:base + 16] = invf_vals
        sgn[base:base + 8] = -1.0
        sgn[base + 8:base + 16] = 1.0
    lbraw = inp["hgrn_lower_bounds"]
    in_maps = []
    for core in range(NCORES):
        b, hf = core // 2, core % 2
        own = np.arange(T) if hf == 0 else (TA - 1 - np.arange(T))
        oth = (TA - 1 - np.arange(T)) if hf == 0 else np.arange(T)
        tok = np.concatenate([own, oth])
        xT = np.ascontiguousarray(x[b][tok, :].T.reshape(KC, 128, TA).transpose(1, 0, 2))
        vecs = np.zeros((128, NV), f32)
        vecs[:, V_MIXG0:V_MIXG0 + 16] = fm_vec(inp["mix_norm"][0])
        vecs[:, V_FFNG0:V_FFNG0 + 16] = fm_vec(inp["ffn_norm"][0])
        vecs[:, V_MIXG1:V_MIXG1 + 16] = fm_vec(inp["mix_norm"][1])
        vecs[:, V_FFNG1:V_FFNG1 + 16] = fm_vec(inp["ffn_norm"][1])
        vecs[:, V_FING:V_FING + 16] = fm_vec(inp["final_norm"])
        vecs[:, V_INVF] = invf
        vecs[:, V_SGN] = sgn
        vecs[:, V_SUBLN] = inp["diff_subln"][0]
        vecs[:, V_SEL0] = 1.0 if hf == 1 else 0.0
        vecs[:, V_SEL1] = 1.0 if hf == 0 else 0.0
        dirs = (0, 1) if hf == 0 else (1, 0)
        for ld in range(2):
            for layer in range(2):
                vecs[:, V_LB + (ld * 2 + layer) * 16:V_LB + (ld * 2 + layer + 1) * 16] = fm_vec(lbraw[dirs[ld], layer])
        vecs[:, V_GNORM:V_GNORM + 16] = fm_vec(inp["hgrn_g_norm"][0])
        rows = np.zeros((1, NR), f32)
        rows[0, R_LNG:R_LNG + 1024] = inp["gmlp_ln_g"][0]
        rows[0, R_LNB:R_LNB + 1024] = inp["gmlp_ln_b"][0]
        ws = inp["gmlp_w_s"][0]
        bs = inp["gmlp_b_s"][0]
        if hf == 1:
            ws = ws[:, ::-1, ::-1]
            bs = bs[:, ::-1]
        rows[0, R_BSB:R_BSB + 1024] = bs.reshape(-1)
        rows[0, R_LQ:R_LQ + 256] = np.concatenate([inp["diff_lq1"][0], inp["diff_lk1"][0], inp["diff_lq2"][0], inp["diff_lk2"][0]])
        m = dict(shared)
        m["xT"] = xT
        m["pos"] = tok.astype(f32).reshape(1, TA)
        m["vecs"] = vecs
        m["rows"] = rows
        m["consts"] = consts
        m["w_s"] = np.ascontiguousarray(ws.transpose(2, 0, 1))
        m["w_hin"] = hin_even if hf == 0 else hin_odd
        in_maps.append(m)
    return in_maps


def assemble(results):
    out = np.zeros((4, TA, D), np.float32)
    for core in range(NCORES):
        b, hf = core // 2, core % 2
        yT = results[core]["yT"]
        y = yT.transpose(2, 1, 0).reshape(T, D)
        if hf == 0:
            out[b, 0:T] = y
        else:
            out[b, T:TA] = y[::-1]
    return out


def kernel(**inputs):
    inp = {k: np.asarray(v) for k, v in inputs.items()}
    in_maps = host_prep(inp)
    nc = build("full")
    res = run_bass_kernel_spmd(nc, in_maps, core_ids=list(range(NCORES)))
    return assemble(res.results)
```

```python
import contextlib
import numpy as np
import concourse.bass as bass
import concourse.mybir as mybir
from concourse.bass_utils import run_bass_kernel_spmd

F32 = mybir.dt.float32
BF16 = mybir.dt.bfloat16
AF = mybir.ActivationFunctionType
ALU = mybir.AluOpType
AX = mybir.AxisListType

ENGS = ("pe", "act", "dve", "pool", "sp")
DEBUG_TAGS = False
INS_TAGS = {}
D = 2048
KC = 16
T = 1024
TA = 2048
FH = 5632
EPS = 1e-6
NCORES = 8


class Op:
    __slots__ = ("eng", "fn", "deps", "signal", "count", "dma_key", "dma_cum", "dma_inc", "idx", "tag")

    def __init__(self, eng, fn, dma_key, dma_inc):
        self.eng = eng
        self.fn = fn
        self.deps = []
        self.signal = False
        self.count = 0
        self.dma_key = dma_key
        self.dma_cum = 0
        self.dma_inc = dma_inc
        self.idx = 0


class Prog:
    def __init__(self):
        self.ops = []
        self.last_w = {}
        self.readers = {}
        self.dma_cnt = {}
        self.last_on = {}
        self.bar = None
        self.bar_seen = set()
        self.hook = None
        self.in_hook = False

    def op(self, eng, fn, reads=(), writes=(), dma_key=None, dma_inc=16):
        o = Op(eng, fn, dma_key, dma_inc)
        o.idx = len(self.ops)
        if DEBUG_TAGS:
            import sys as _s
            f = _s._getframe(1)
            o.tag = f"{f.f_lineno}<{f.f_back.f_lineno}<{f.f_back.f_back.f_lineno if f.f_back.f_back else 0} w={list(writes)[:3]}"
        deps = set()
        for r in reads:
            w = self.last_w.get(r)
            if w is not None:
                deps.add(w)
            if r.startswith("ps"):
                for rd in self.readers.get(r, ()):
                    if rd.eng != eng:
                        deps.add(rd)
        for wkey in writes:
            w = self.last_w.get(wkey)
            if w is not None:
                deps.add(w)
            for rd in self.readers.get(wkey, ()):
                deps.add(rd)
        if dma_key is not None:
            prev = self.last_on.get("dma:" + dma_key)
            if prev is not None:
                deps.add(prev)
        if self.bar is not None and eng not in self.bar_seen:
            deps.add(self.bar)
            self.bar_seen.add(eng)
        for d in deps:
            if d.dma_key is None and d.eng == "pe" and eng == "pe" and dma_key is None:
                continue
            o.deps.append(d)
            if d.dma_key is None:
                d.signal = True
        for r in reads:
            self.readers.setdefault(r, []).append(o)
        for wkey in writes:
            self.last_w[wkey] = o
            self.readers[wkey] = []
        if dma_key is not None:
            self.dma_cnt[dma_key] = self.dma_cnt.get(dma_key, 0) + dma_inc
            o.dma_cum = self.dma_cnt[dma_key]
            self.last_on["dma:" + dma_key] = o
        else:
            self.last_on[eng] = o
        self.ops.append(o)
        if self.hook is not None and not self.in_hook:
            self.hook()
        return o

    def barrier(self, nopfn):
        o = Op("dve", nopfn, None, 16)
        o.idx = len(self.ops)
        for k, d in self.last_on.items():
            if d is None:
                continue
            if d.dma_key is None and d.eng == "dve":
                continue
            o.deps.append(d)
            if d.dma_key is None:
                d.signal = True
        self.last_on["dve"] = o
        self.ops.append(o)
        self.bar = o
        self.bar_seen = {"dve"}
        self.last_w = {}
        self.readers = {}
        return o

    def emit(self, nc, final_dma_keys=()):
        cnt = {e: 0 for e in ENGS}
        for o in self.ops:
            if o.dma_key is None and o.signal:
                cnt[o.eng] += 1
                o.count = cnt[o.eng]
        per_eng = {e: [] for e in ENGS}
        for o in self.ops:
            per_eng[o.eng].append(o)
        dma_keys = sorted(self.dma_cnt.keys())
        with contextlib.ExitStack() as st:
            sems = {}
            for e in ENGS:
                sems[e] = st.enter_context(nc.semaphore("s_" + e))
            for k in dma_keys:
                sems["dma_" + k] = st.enter_context(nc.semaphore("d_" + k))
            block = st.enter_context(nc.Block())

            def run(engname, engobj):
                waited = {}
                for o in per_eng[engname]:
                    need = {}
                    for d in o.deps:
                        if d.dma_key is not None:
                            s, v = "dma_" + d.dma_key, d.dma_cum
                        else:
                            s, v = d.eng, d.count
                        if v > need.get(s, 0):
                            need[s] = v
                    for s, v in need.items():
                        if waited.get(s, 0) >= v:
                            continue
                        engobj.wait_ge(sems[s], v)
                        waited[s] = v
                    ins = o.fn(engobj)
                    if DEBUG_TAGS:
                        try:
                            INS_TAGS[ins.ins.name] = o.tag
                        except Exception:
                            pass
                    if o.dma_key is not None:
                        ins.then_inc(sems["dma_" + o.dma_key], o.dma_inc)
                    elif o.signal:
                        ins.then_inc(sems[o.eng], 1)
                if engname == "sp":
                    for k in final_dma_keys:
                        engobj.wait_ge(sems["dma_" + k], self.dma_cnt[k])

            @block.tensor
            def _(e):
                run("pe", e)

            @block.scalar
            def _(e):
                run("act", e)

            @block.vector
            def _(e):
                run("dve", e)

            @block.gpsimd
            def _(e):
                run("pool", e)

            @block.sync
            def _(e):
                run("sp", e)


def slabify(W, ncols):
    K, N = W.shape
    return np.ascontiguousarray(W.reshape(K // 128, 128, N // ncols, ncols).transpose(2, 1, 0, 3))


def fm_vec(v):
    return np.ascontiguousarray(v.reshape(-1, 128).T)


V_MIXG0, V_FFNG0, V_MIXG1, V_FFNG1, V_FING = 0, 16, 32, 48, 64
V_INVF, V_SGN, V_SUBLN, V_SEL0, V_SEL1 = 80, 81, 82, 83, 84
V_LB = 88
V_GNORM = 152
NV = 168
R_LNG, R_LNB, R_BSB, R_LQ = 0, 1024, 2048, 3072
NR = 3072 + 256


class Ctx:
    pass


def build(stage="full"):
    nc = bass.Bass("TRN2", target_bir_lowering=False)
    P = Prog()
    c = Ctx()
    c.nc, c.P = nc, P
    dt_in = lambda name, shape: nc.dram_tensor(name, list(shape), F32, kind="ExternalInput").ap()
    c.xT = dt_in("xT", [128, KC, TA])
    c.pos = dt_in("pos", [1, TA])
    c.vecs_d = dt_in("vecs", [128, NV])
    c.rows_d = dt_in("rows", [1, NR])
    c.consts_d = dt_in("consts", [128, 4, 128])
    c.w_q = dt_in("w_q", [8, 128, KC, 128])
    c.w_k = dt_in("w_k", [8, 128, KC, 128])
    c.w_v = dt_in("w_v", [2, 128, KC, 512])
    c.w_u = dt_in("w_u", [8, 128, KC, 128])
    c.w_vb = dt_in("w_vb", [2, 128, KC, 512])
    c.w_s = dt_in("w_s", [128, 8, 128])
    c.w_eo = dt_in("w_eo", [16, 128, KC, 128])
    if stage in ("full", "l0", "ffn0"):
        c.w_g = dt_in("w_g", [2, 44, 128, KC, 128])
        c.w_up = dt_in("w_up", [2, 44, 128, KC, 128])
        c.w_dn = dt_in("w_dn", [2, 2, 16, 128, 22, 128])
    if stage in ("full", "l1"):
        c.w_hin = dt_in("w_hin", [5, 16, 128, KC, 128])
        c.w_ho = dt_in("w_ho", [16, 128, KC, 128])
    c.yT = nc.dram_tensor("yT", [128, KC, T], F32, kind="ExternalOutput").ap()
    c.cin = nc.dram_tensor("cin", [16 * 128, 128], F32).ap()
    c.cout = nc.dram_tensor("cout", [2 * 16 * 128, 128], F32).ap()

    ARENA_KB = 204
    arena = nc.alloc_sbuf_tensor("arena", [128, ARENA_KB * 512], BF16).ap()

    def sb(off_kb, nbytes, dtype, pattern=None, **kw):
        a = int(round(off_kb * 512))
        n = nbytes // 2
        assert a + n <= ARENA_KB * 512, (off_kb, nbytes)
        v = arena[:, a:a + n]
        if dtype == F32:
            v = v.bitcast(F32)
        if pattern:
            v = v.rearrange(pattern, **kw)
        return v

    c.sb = sb
    c.ps = [nc.alloc_psum_tensor(f"ps{i}", [128, 512], F32).ap() for i in range(8)]
    c.h = sb(0, 64 * 1024, F32, "p (k t) -> p k t", k=KC)
    c.hn = sb(64, 32 * 1024, BF16, "p (k t) -> p k t", k=KC)
    c.vecs = sb(144, NV * 4, F32)
    c.ident = sb(144.75, 256, BF16)
    c.ones = sb(145.0, 256, BF16)
    c.pm = sb(145.25, 256, BF16)
    c.tri = sb(145.5, 256, BF16)
    c.triT = sb(145.75, 256, BF16)
    c.lam = sb(146.0, 16, F32)
    c.epsb = sb(146.0625, 16, F32)
    c.cst32 = sb(200, 4 * 512, F32, "p (a b) -> p a b", a=4)
    c.wslot = [sb(160 + 4 * i, 4096, BF16, "p (k n) -> p k n", k=KC) for i in range(8)]
    c.wbig = [sb(160 + 16 * i, 16384, BF16, "p (k n) -> p k n", k=KC) for i in range(2)]
    c.wcnt = 0
    c.bigcnt = 0
    c.nring = 8

    load_consts(c)
    c.stage = stage
    if stage == "l1":
        for kc in range(KC):
            P.op("sp", lambda e, kc=kc: e.dma_start(out=c.h[:, kc, :], in_=c.xT[:, kc, 0:T]), writes=[f"h{kc}.0", f"h{kc}.1"], dma_key=f"x{kc}")
        layer1_mixer(c)
    elif stage == "ffn0":
        for kc in range(KC):
            P.op("sp", lambda e, kc=kc: e.dma_start(out=c.h[:, kc, :], in_=c.xT[:, kc, 0:T]), writes=[f"h{kc}.0", f"h{kc}.1"], dma_key=f"x{kc}")
        ffn(c, 0)
    elif stage in ("full", "l0") or stage.startswith("l0"):
        layer0_mixer(c)
        if stage in ("full", "l0"):
            ffn(c, 0)
    if stage in ("full",):
        layer1_mixer(c)
        ffn(c, 1)
        final_norm(c)
    else:
        P.barrier(lambda e: e.memset(c.lam[:, 2:3], 0.0))
        for kc in range(KC):
            P.op("sp", lambda e, kc=kc: e.dma_start(out=c.yT[:, kc, :], in_=c.h[:, kc, :]),
                 reads=[f"h{kc}.0", f"h{kc}.1"], writes=[f"y{kc}"], dma_key=f"out{kc % 8}")
    P.emit(nc, final_dma_keys=[f"out{i}" for i in range(8)])
    return nc


def wload(c, dram_slab, big=False, nk=KC):
    P = c.P
    if big:
        i = c.bigcnt % 2
        c.bigcnt += 1
        ap = c.wbig[i]
        keys = [f"ws{4 * i + j}" for j in range(4)]
        dk = f"wb{i}"
    else:
        i = c.wcnt % c.nring
        c.wcnt += 1
        ap = c.wslot[i]
        keys = [f"ws{i}"]
        dk = f"w{i}"
    dst = ap if nk == KC else ap[:, 0:nk, :]
    P.op("pool", lambda e: e.dma_start(out=dst, in_=dram_slab), writes=keys, dma_key=dk)
    return ap, keys


def load_consts(c):
    P, nc = c.P, c.nc
    P.op("dve", lambda e: e.memset(c.epsb[:, 0:1], EPS), writes=["epsb"])
    P.op("dve", lambda e: e.memset(c.epsb[:, 1:2], EPS / 0.64), reads=["epsb"], writes=["epsb"])
    P.op("dve", lambda e: e.memset(c.epsb[:, 2:3], 1.0), reads=["epsb"], writes=["epsb"])
    P.op("sp", lambda e: e.dma_start(out=c.vecs, in_=c.vecs_d), writes=["vecs"], dma_key="c")
    P.op("sp", lambda e: e.dma_start(out=c.cst32, in_=c.consts_d), writes=["cst32"], dma_key="c")
    for i, (ap, nm) in enumerate([(c.ident, "ident"), (c.ones, "ones"), (c.pm, "pm"), (c.tri, "tri")]):
        P.op("dve", lambda e, ap=ap, i=i: e.tensor_copy(ap, c.cst32[:, i, :]), reads=["cst32"], writes=[nm])
    psb = c.ps[7].bitcast(BF16)
    P.op("pe", lambda e: e.transpose(psb[:, 0:128], c.tri, c.ident), reads=["tri", "ident"], writes=["ps7"])
    P.op("dve", lambda e: e.tensor_copy(c.triT, psb[:, 0:128]), reads=["ps7"], writes=["triT"])


def rms_to_hn(c, src_fn, src_keys_fn, gcol, ntc, dst, dst_key_fn, tmp_off):
    P = c.P
    sq = [c.sb(tmp_off + i, 1024, BF16) for i in range(2)]
    rstd = c.sb(tmp_off + 2, 2048, F32)
    for tc in range(ntc):
        ps = c.ps[6 + (tc % 2)]
        psk = f"ps{6 + (tc % 2)}"
        for kc in range(KC):
            s = sq[kc % 2]
            sk = f"sq{kc % 2}"
            P.op("act", lambda e, s=s, kc=kc, tc=tc: e.activation(s, src_fn(kc, tc), AF.Square),
                 reads=src_keys_fn(kc, tc), writes=[sk])
            P.op("pe", lambda e, s=s, kc=kc, ps=ps: e.matmul(ps, c.ones, s, start=(kc == 0), stop=(kc == KC - 1)),
                 reads=[sk, "ones"], writes=[psk])
        P.op("act", lambda e, ps=ps: e.activation(rstd, ps, AF.Ln, bias=c.epsb[:, 0:1], scale=1.0 / D), reads=[psk, "epsb"], writes=["rstd"])
        P.op("act", lambda e: e.activation(rstd, rstd, AF.Exp, scale=-0.5), reads=["rstd"], writes=["rstd"])
        for kc in range(KC):
            P.op("dve", lambda e, kc=kc, tc=tc: e.scalar_tensor_tensor(dst(kc, tc), src_fn(kc, tc), c.vecs[:, gcol + kc:gcol + kc + 1], rstd, ALU.mult, ALU.mult),
                 reads=src_keys_fn(kc, tc) + ["rstd", "vecs"], writes=[dst_key_fn(kc, tc)])


def proj_fm(c, slab, slab_keys, rhs_fn, rhs_keys_fn, ps, psk, nk=KC):
    pairs = [(slab[:, kc, :], rhs_fn(kc)) for kc in range(nk)]
    reads = list(slab_keys)
    for kc in range(nk):
        reads += rhs_keys_fn(kc)

    def fn(e):
        ins = None
        for i, (l, r) in enumerate(pairs):
            ins = e.matmul(ps, l, r, start=(i == 0), stop=(i == nk - 1))
        return ins
    c.P.op("pe", fn, reads=reads, writes=[psk])


def layer0_mixer(c):
    P, nc, sb = c.P, c.nc, c.sb
    hn_oth = sb(96, 32 * 1024, BF16, "p (k t) -> p k t", k=KC)
    cat = hn_oth
    K_all = sb(0, 32 * 1024, BF16, "p (h t) -> p h t", h=8)
    V_all = sb(32, 32 * 1024, BF16, "p (b n) -> p b n", b=16)
    Ctab = sb(128, 8192, F32)
    Stab = sb(136, 8192, F32)
    rows = sb(147, NR * 4, F32)
    tmp_off = 192
    P.op("sp", lambda e: e.dma_start(out=rows, in_=c.rows_d.partition_broadcast(128)), writes=["rows"], dma_key="c")
    posb = sb(160, 8192, F32)
    P.op("sp", lambda e: e.dma_start(out=posb, in_=c.pos.partition_broadcast(128)), writes=["ws0", "ws1"], dma_key="c")
    TWO_PI = 2.0 * np.pi
    C1 = 6.28125
    C2 = TWO_PI - C1
    invf = c.vecs[:, V_INVF:V_INVF + 1]
    ang = sb(168, 8192, F32)
    kf = sb(176, 8192, F32)
    ki = sb(184, 8192, F32).bitcast(mybir.dt.int32)
    PI_IN = 3.1415925

    def make_table(dst, shift, key):
        P.op("dve", lambda e: e.tensor_scalar(ang, posb, invf, shift, ALU.mult, ALU.add), reads=["ws0", "ws1", "vecs"], writes=["ang"])
        P.op("dve", lambda e: e.tensor_scalar(kf, ang, 1.0 / TWO_PI, None, ALU.mult), reads=["ang"], writes=["kf"])
        P.op("dve", lambda e: e.tensor_copy(ki, kf), reads=["kf"], writes=["ki"])
        P.op("dve", lambda e: e.tensor_copy(kf, ki), reads=["ki"], writes=["kf"])
        P.op("dve", lambda e: e.scalar_tensor_tensor(ang, kf, -C1, ang, ALU.mult, ALU.add), reads=["kf", "ang"], writes=["ang"])
        P.op("dve", lambda e: e.scalar_tensor_tensor(ang, kf, -C2, ang, ALU.mult, ALU.add), reads=["kf", "ang"], writes=["ang"])
        P.op("dve", lambda e: e.tensor_scalar(kf, ang, np.pi, -TWO_PI, ALU.is_gt, ALU.mult), reads=["ang"], writes=["kf"])
        P.op("dve", lambda e: e.tensor_tensor(ang, ang, kf, ALU.add), reads=["kf", "ang"], writes=["ang"])
        P.op("dve", lambda e: e.tensor_scalar(kf, ang, -np.pi, TWO_PI, ALU.is_lt, ALU.mult), reads=["ang"], writes=["kf"])
        P.op("dve", lambda e: e.tensor_tensor(ang, ang, kf, ALU.add), reads=["kf", "ang"], writes=["ang"])
        P.op("dve", lambda e: e.tensor_scalar(ang, ang, -PI_IN, PI_IN, ALU.max, ALU.min), reads=["ang"], writes=["ang"])
        P.op("act", lambda e: e.activation(dst, ang, AF.Sin), reads=["ang"], writes=[key])

    make_table(Stab, 0.0, "Stab")
    P.op("dve", lambda e: e.tensor_scalar(Stab, Stab, c.vecs[:, V_SGN:V_SGN + 1], None, ALU.mult), reads=["Stab", "vecs"], writes=["Stab"])
    make_table(Ctab, np.pi / 2, "Ctab")
    lq = rows[:, R_LQ:R_LQ + 256].rearrange("p (a b) -> p a b", a=4)
    lt = sb(tmp_off, 512, F32, "p (a b) -> p a b", a=2)
    P.op("dve", lambda e: e.tensor_tensor(lt[:, 0, :], lq[:, 0, :], lq[:, 1, :], ALU.mult), reads=["rows"], writes=["lt"])
    P.op("dve", lambda e: e.tensor_tensor(lt[:, 1, :], lq[:, 2, :], lq[:, 3, :], ALU.mult), reads=["rows", "lt"], writes=["lt"])
    P.op("dve", lambda e: e.reduce_sum(c.lam[:, 2:4], lt, AX.X), reads=["lt"], writes=["lam"])
    P.op("act", lambda e: e.activation(c.lam[:, 2:4], c.lam[:, 2:4], AF.Exp), reads=["lam"], writes=["lam"])
    P.op("dve", lambda e: e.tensor_tensor(c.lam[:, 0:1], c.lam[:, 2:3], c.lam[:, 3:4], ALU.subtract), reads=["lam"], writes=["lam"])
    P.op("dve", lambda e: e.tensor_scalar(c.lam[:, 1:2], c.lam[:, 0:1], 0.2, -1.0, ALU.add, ALU.mult), reads=["lam"], writes=["lam"])

    stage = c.h
    for half in range(2):
        for kc in range(KC):
            P.op("sp", lambda e, kc=kc, half=half: e.dma_start(out=stage[:, kc, :], in_=c.xT[:, kc, half * T:(half + 1) * T]),
                 writes=[f"h{kc}.0", f"h{kc}.1"], dma_key=f"x{kc}")
        dstT = c.hn if half == 0 else hn_oth
        dkey = "hn" if half == 0 else "ho"
        rms_to_hn(c, lambda kc, tc: stage[:, kc, tc * 512:(tc + 1) * 512], lambda kc, tc: [f"h{kc}.{tc}"], V_MIXG0, 2,
                  lambda kc, tc, dstT=dstT: dstT[:, kc, tc * 512:(tc + 1) * 512], lambda kc, tc, dkey=dkey: f"{dkey}{kc}.{tc}", tmp_off)

    def hn_all(kc, tcc):
        src = c.hn if tcc < 2 else hn_oth
        return src[:, kc, (tcc % 2) * 512:(tcc % 2 + 1) * 512]

    def hn_all_keys(kc, tcc):
        return [f"{'hn' if tcc < 2 else 'ho'}{kc}.{tcc % 2}"]

    if c.stage == "l0A":
        return
    if c.stage == "l0Aw":
        slab, skeys = wload(c, c.w_k[0])
        proj_fm(c, slab, skeys, lambda kc: c.hn[:, kc, 0:512], lambda kc: [f"hn{kc}.0"], c.ps[2], "ps2")
        P.op("dve", lambda e: e.tensor_copy(c.h[:, 0, 0:512], c.ps[2]), reads=["ps2"], writes=["h0.0"])
        return
    P.barrier(lambda e: e.memset(c.lam[:, 2:3], 0.0))
    if c.stage == "l0Abar":
        P.op("act", lambda e: e.copy(c.h[:, 0, 0:512], c.h[:, 1, 0:512]), reads=[], writes=["h0.0"])
        P.op("pe", lambda e: e.matmul(c.ps[2], c.ones, c.hn[:, 0, 0:512], start=True, stop=True), reads=[], writes=["ps2"])
        P.op("dve", lambda e: e.tensor_copy(c.h[:, 2, 0:512], c.ps[2]), reads=["ps2"], writes=["h2.0"])
        return
    evi = [0]
    for cb in range(2 if c.stage not in ("l0B1k", "l0B1kn", "l0B1r1", "l0B1r2") else 0):
        slab, skeys = wload(c, c.w_v[cb], big=True)
        for tb in range(16):
            src = c.hn if tb < 8 else hn_oth
            sk = "hn" if tb < 8 else "ho"
            tcl = (tb % 8) // 4
            bank = tb % 2
            ps, psk = c.ps[bank], f"ps{bank}"
            pairs = [(src[:, kc, (tb % 8) * 128:(tb % 8 + 1) * 128], slab[:, kc, :]) for kc in range(KC)]

            def fn(e, pairs=pairs, ps=ps):
                ins = None
                for i, (l, r) in enumerate(pairs):
                    ins = e.matmul(ps, l, r, start=(i == 0), stop=(i == KC - 1))
                return ins
            P.op("pe", fn, reads=skeys + [f"{sk}{kc}.{tcl}" for kc in range(KC)], writes=[psk])
            dst = V_all[:, tb, cb * 512:(cb + 1) * 512]
            if tb % 2 == 0:
                P.op("act", lambda e, dst=dst, ps=ps: e.copy(dst, ps), reads=[psk], writes=[f"V{tb}.{cb}"])
            else:
                P.op("dve", lambda e, dst=dst, ps=ps: e.tensor_copy(dst, ps), reads=[psk], writes=[f"V{tb}.{cb}"])

    if c.stage == "l0B1v":
        return
    t1 = sb(tmp_off + 4, 2048, F32)
    t2 = sb(tmp_off + 6, 2048, F32)
    q16_b1 = [sb(tmp_off + 8 + i, 1024, BF16) for i in range(2)]
    q16_c = [sb(154 + i, 1024, BF16) for i in range(2)]

    def rope_block(ps_a, psk_a, ps_b, psk_b, tcc, dst, dst_key, i, q16):
        qb = q16[i % 2]
        qk = f"q16{i % 2}"
        if c.stage == "l0B1r1":
            P.op("act", lambda e: e.copy(qb, ps_a), reads=[psk_a], writes=[qk])
            P.op("pe", lambda e: e.matmul(ps_b, c.pm, qb, start=True, stop=True), reads=[qk, "pm"], writes=[psk_b])
            P.op("dve", lambda e: e.tensor_copy(dst, ps_b), reads=[psk_b], writes=[dst_key])
            return
        if c.stage == "l0B1r2":
            P.op("dve", lambda e: e.tensor_tensor(t1, ps_a, Ctab[:, tcc * 512:(tcc + 1) * 512], ALU.mult), reads=[psk_a, "Ctab"], writes=["t1"])
            P.op("dve", lambda e: e.tensor_tensor(t2, ps_a, Stab[:, tcc * 512:(tcc + 1) * 512], ALU.mult), reads=[psk_a, "Stab"], writes=["t2"])
            P.op("dve", lambda e: e.tensor_tensor(dst, t1, t2, ALU.add), reads=["t1", "t2"], writes=[dst_key])
            return
        P.op("act", lambda e: e.copy(qb, ps_a), reads=[psk_a], writes=[qk])
        P.op("pe", lambda e: e.matmul(ps_b, c.pm, qb, start=True, stop=True), reads=[qk, "pm"], writes=[psk_b])
        P.op("dve", lambda e: e.tensor_tensor(t1, ps_a, Ctab[:, tcc * 512:(tcc + 1) * 512], ALU.mult), reads=[psk_a, "Ctab", qk], writes=["t1"])
        P.op("dve", lambda e: e.tensor_tensor(t2, ps_b, Stab[:, tcc * 512:(tcc + 1) * 512], ALU.mult), reads=[psk_b, "Stab"], writes=["t2"])
        P.op("dve", lambda e: e.tensor_tensor(dst, t1, t2, ALU.add), reads=["t1", "t2"], writes=[dst_key])

    ri = 0
    for hd in range(8):
        slab, skeys = wload(c, c.w_k[hd])
        for tcc in range(4):
            ba, bb = 2 + (ri % 2) * 2, 3 + (ri % 2) * 2
            proj_fm(c, slab, skeys, lambda kc, tcc=tcc: hn_all(kc, tcc), lambda kc, tcc=tcc: hn_all_keys(kc, tcc), c.ps[ba], f"ps{ba}")
            if c.stage == "l0B1kn":
                P.op("act", lambda e, ba=ba, hd=hd, tcc=tcc: e.copy(K_all[:, hd, tcc * 512:(tcc + 1) * 512], c.ps[ba]), reads=[f"ps{ba}"], writes=[f"K{hd}.{tcc}"])
            else:
                rope_block(c.ps[ba], f"ps{ba}", c.ps[bb], f"ps{bb}", tcc, K_all[:, hd, tcc * 512:(tcc + 1) * 512], f"K{hd}.{tcc}", ri, q16_b1)
            ri += 1

    if c.stage.startswith("l0B1"):
        return
    P.barrier(lambda e: e.memset(c.lam[:, 2:3], 0.0))
    wsT = sb(tmp_off + 10, 2048, BF16, "p (g n) -> p g n", g=8)
    P.op("pool", lambda e: e.dma_start(out=wsT, in_=c.w_s), writes=["wsT"], dma_key="c2")
    for g in range(8):
        slab, skeys = wload(c, c.w_u[g])
        for tc in range(2):
            bank = 2 + (g * 2 + tc) % 2
            proj_fm(c, slab, skeys, lambda kc, tc=tc: c.hn[:, kc, tc * 512:(tc + 1) * 512], lambda kc, tc=tc: [f"hn{kc}.{tc}"], c.ps[bank], f"ps{bank}")
            P.op("act", lambda e, g=g, tc=tc, bank=bank: e.activation(cat[:, 8 + g, tc * 512:(tc + 1) * 512], c.ps[bank], AF.Gelu),
                 reads=[f"ps{bank}"], writes=[f"cat{8 + g}.{tc}"])
    vbg = sb(96, 16 * 1024, F32, "p (b n) -> p b n", b=4)
    vbn = sb(tmp_off + 1, 2048, BF16)
    sqj = sb(tmp_off + 4, 4096, F32)
    stats = sb(tmp_off, 64, F32)
    lng = rows[:, R_LNG:R_LNG + 1024]
    lnb = rows[:, R_LNB:R_LNB + 1024]
    bsb = rows[:, R_BSB:R_BSB + 1024].rearrange("p (g n) -> p g n", g=8)
    for th in range(2):
        for cb in range(2):
            slab, skeys = wload(c, c.w_vb[cb], big=True)
            for tbl in range(4):
                tb = th * 4 + tbl
                bank = tb % 2
                ps, psk = c.ps[bank], f"ps{bank}"
                pairs = [(c.hn[:, kc, tb * 128:(tb + 1) * 128], slab[:, kc, :]) for kc in range(KC)]

                def fn(e, pairs=pairs, ps=ps):
                    ins = None
                    for i, (l, r) in enumerate(pairs):
                        ins = e.matmul(ps, l, r, start=(i == 0), stop=(i == KC - 1))
                    return ins
                P.op("pe", fn, reads=skeys + [f"hn{kc}.{tb // 4}" for kc in range(KC)], writes=[psk])
                P.op("act", lambda e, tbl=tbl, cb=cb, ps=ps: e.activation(vbg[:, tbl, cb * 512:(cb + 1) * 512], ps, AF.Gelu),
                     reads=[psk], writes=[f"vbg{tbl}.{cb}"])
        for tbl in range(4):
            tb = th * 4 + tbl
            xv = vbg[:, tbl, :]
            rk = [f"vbg{tbl}.0", f"vbg{tbl}.1"]
            P.op("dve", lambda e, xv=xv: e.reduce_sum(stats[:, 0:1], xv, AX.X), reads=rk, writes=["stats"])
            P.op("dve", lambda e: e.tensor_scalar(stats[:, 1:2], stats[:, 0:1], -1.0 / 1024, None, ALU.mult), reads=["stats"], writes=["stats"])
            P.op("dve", lambda e, xv=xv: e.tensor_scalar(xv, xv, stats[:, 1:2], None, ALU.add), reads=rk + ["stats"], writes=rk)
            P.op("dve", lambda e, xv=xv: e.tensor_tensor(sqj, xv, xv, ALU.mult), reads=rk, writes=["sqj"])
            P.op("dve", lambda e: e.reduce_sum(stats[:, 2:3], sqj, AX.X), reads=["sqj"], writes=["stats"])
            P.op("act", lambda e: e.activation(stats[:, 3:4], stats[:, 2:3], AF.Sqrt, bias=c.epsb[:, 0:1], scale=1.0 / 1024), reads=["stats", "epsb"], writes=["stats"])
            P.op("dve", lambda e: e.reciprocal(stats[:, 3:4], stats[:, 3:4]), reads=["stats"], writes=["stats"])
            P.op("dve", lambda e, xv=xv: e.scalar_tensor_tensor(xv, xv, stats[:, 3:4], lng, ALU.mult, ALU.mult), reads=rk + ["stats", "rows"], writes=rk)
            P.op("dve", lambda e, xv=xv: e.tensor_tensor(vbn, xv, lnb, ALU.add), reads=rk + ["rows"], writes=["vbn"])
            for gh in range(2):
                bank = 2 + gh
                ps, psk = c.ps[bank], f"ps{bank}"

                def fn(e, gh=gh, ps=ps):
                    ins = None
                    for gl in range(4):
                        g = gh * 4 + gl
                        ins = e.matmul(ps[:, gl * 128:(gl + 1) * 128], vbn[:, g * 128:(g + 1) * 128], wsT[:, g, :], start=True, stop=True)
                    return ins
                P.op("pe", fn, reads=["vbn", "wsT"], writes=[psk])
                svt = sb(tmp_off + 8, 2048, F32, "p (g n) -> p g n", g=4)
                P.op("dve", lambda e, ps=ps, gh=gh: e.tensor_tensor(svt, ps.rearrange("p (g n) -> p g n", g=4), bsb[:, gh * 4:(gh + 1) * 4, :], ALU.add),
                     reads=[psk, "rows"], writes=["svt"])
                cv = cat[:, 8 + gh * 4:8 + gh * 4 + 4, tb * 128:(tb + 1) * 128]
                ck = [f"cat{8 + gh * 4 + gl}.{tb // 4}" for gl in range(4)]
                P.op("dve", lambda e, cv=cv: e.tensor_tensor(cv, cv, svt, ALU.mult), reads=["svt"] + ck, writes=ck)

    if c.stage == "l0B2":
        return
    P.barrier(lambda e: e.memset(c.lam[:, 2:3], 0.0))
    Et = [[sb(tmp_off + 8 + 2 * m + b, 1024, BF16) for b in range(2)] for m in range(2)]
    qrot = sb(tmp_off + 1, 2048, BF16)
    r0 = sb(156, 2048, F32)
    r1 = sb(158, 2048, F32)
    oT = sb(147, 2048, F32)
    a0 = sb(149, 2048, F32)
    sqb = sb(151, 1024, BF16)
    rs2 = sb(152, 2048, F32)
    scale = 64 ** -0.5
    ri = 0
    pending = []
    pending2 = []
    for hd in range(8):
        slab, skeys = wload(c, c.w_q[hd])
        for tc in range(2):
            proj_fm(c, slab, skeys, lambda kc, tc=tc: c.hn[:, kc, tc * 512:(tc + 1) * 512], lambda kc, tc=tc: [f"hn{kc}.{tc}"], c.ps[6], "ps6")
            rope_block(c.ps[6], "ps6", c.ps[7], "ps7", tc, qrot[:, tc * 512:(tc + 1) * 512], f"qrot{tc}", ri, q16_c)
            ri += 1
        for tc in range(2):
            SB = [0, 1, 6, 7]

            def emit_scores(j, tc=tc, hd=hd):
                for m in range(2):
                    b = SB[(j % 2) * 2 + m]
                    P.op("pe", lambda e, m=m, j=j, tc=tc, b=b, hd=hd: e.matmul(c.ps[b], K_all[64 * m:64 * m + 64, hd, j * 128:(j + 1) * 128],
                                                                            qrot[64 * m:64 * m + 64, tc * 512:(tc + 1) * 512], start=True, stop=True),
                         reads=[f"K{hd}.{j // 4}", f"qrot{tc}"], writes=[f"ps{b}"])

            emit_scores(0)
            for j in range(16):
                for m in range(2):
                    b = SB[(j % 2) * 2 + m]
                    E = Et[m][j % 2]
                    ek = f"E{m}.{j % 2}"
                    P.op("act", lambda e, E=E, b=b: e.activation(E, c.ps[b], AF.Exp, scale=scale), reads=[f"ps{b}"], writes=[ek])
                if j < 15:
                    emit_scores(j + 1)
                if j == 2 and pending:
                    pending.pop(0)()
                if j == 13 and pending2 and len(pending2) > len(pending):
                    pending2.pop(0)()
                for m in range(2):
                    E = Et[m][j % 2]
                    ek = f"E{m}.{j % 2}"
                    P.op("pe", lambda e, m=m, j=j, E=E, hd=hd: e.matmul(c.ps[2 + 2 * m], V_all[:, j, hd * 128:(hd + 1) * 128], E, start=(j == 0), stop=(j == 15)),
                         reads=[ek, f"V{j}.{hd // 4}"], writes=[f"ps{2 + 2 * m}"])
                    P.op("pe", lambda e, m=m, j=j, E=E: e.matmul(c.ps[3 + 2 * m], c.ones, E, start=(j == 0), stop=(j == 15)),
                         reads=[ek, "ones"], writes=[f"ps{3 + 2 * m}"])
            B0, B1, B2, B3 = r0, r1, a0, oT
            P.op("dve", lambda e: e.tensor_copy(B2, c.ps[2]), reads=["ps2"], writes=["a0"])
            P.op("dve", lambda e: e.tensor_copy(B0, c.ps[3]), reads=["ps3"], writes=["cb0"])
            P.op("dve", lambda e: e.tensor_copy(B3, c.ps[4]), reads=["ps4"], writes=["oT"])
            P.op("dve", lambda e: e.tensor_copy(B1, c.ps[5]), reads=["ps5"], writes=["cb1"])
            def finish(tc=tc, hd=hd):
                P.op("dve", lambda e: e.reciprocal(B0, B0), reads=["cb0"], writes=["cb0"])
                P.op("dve", lambda e: e.reciprocal(B1, B1), reads=["cb1"], writes=["cb1"])
                P.op("dve", lambda e: e.tensor_tensor(B2, B2, B0, ALU.mult), reads=["a0", "cb0"], writes=["a0"])
                P.op("dve", lambda e: e.tensor_tensor(B3, B3, B1, ALU.mult), reads=["oT", "cb1"], writes=["oT"])
                P.op("dve", lambda e: e.scalar_tensor_tensor(B0, B3, c.lam[:, 1:2], B2, ALU.mult, ALU.add), reads=["oT", "a0", "lam", "cb0"], writes=["cb0"])
                oTf = B0
                P.op("act", lambda e: e.activation(sqb, oTf, AF.Square), reads=["cb0"], writes=["sqb"])

            def finish2(tc=tc, hd=hd):
                oTf = B0
                P.op("pe", lambda e: e.matmul(c.ps[6], c.ones, sqb, start=True, stop=True), reads=["sqb", "ones"], writes=["ps6"])
                P.op("act", lambda e: e.activation(rs2, c.ps[6], AF.Ln, bias=c.epsb[:, 1:2], scale=1.0 / (128 * 0.64)), reads=["ps6", "epsb"], writes=["rs2"])
                P.op("act", lambda e: e.activation(rs2, rs2, AF.Exp, scale=-0.5), reads=["rs2"], writes=["rs2"])
                P.op("dve", lambda e, tc=tc, hd=hd: e.scalar_tensor_tensor(cat[:, hd, tc * 512:(tc + 1) * 512], oTf, c.vecs[:, V_SUBLN:V_SUBLN + 1], rs2, ALU.mult, ALU.mult),
                     reads=["cb0", "rs2", "vecs"], writes=[f"cat{hd}.{tc}"])

            pending.append(finish)
            pending2.append(finish2)

    while pending:
        pending.pop(0)()
    while pending2:
        pending2.pop(0)()
    if c.stage == "l0C":
        P.barrier(lambda e: e.memset(c.lam[:, 2:3], 0.0))
        for kc in range(KC):
            P.op("dve", lambda e, kc=kc: e.tensor_copy(c.h[:, kc, :], cat[:, kc, :]), writes=[f"h{kc}.0", f"h{kc}.1"])
        return
    P.barrier(lambda e: e.memset(c.lam[:, 2:3], 0.0))
    for db in range(16):
        slab, skeys = wload(c, c.w_eo[db])
        P.op("sp", lambda e, db=db: e.dma_start(out=c.h[:, db, :], in_=c.xT[:, db, 0:T]), writes=[f"h{db}.0", f"h{db}.1"], dma_key=f"x{db}")
        for tc in range(2):
            bank = (db * 2 + tc) % 4
            proj_fm(c, slab, skeys, lambda kc, tc=tc: cat[:, kc, tc * 512:(tc + 1) * 512], lambda kc, tc=tc: [f"cat{kc}.{tc}"], c.ps[bank], f"ps{bank}")
            hv = c.h[:, db, tc * 512:(tc + 1) * 512]
            P.op("dve", lambda e, hv=hv, bank=bank: e.tensor_tensor(hv, hv, c.ps[bank], ALU.add), reads=[f"ps{bank}", f"h{db}.{tc}"], writes=[f"h{db}.{tc}"])


def ffn(c, l):
    P, sb = c.P, c.sb
    P.barrier(lambda e: e.memset(c.lam[:, 2:3], 0.0))
    tmp_off = 192
    gcol = V_FFNG0 if l == 0 else V_FFNG1
    rms_to_hn(c, lambda kc, tc: c.h[:, kc, tc * 512:(tc + 1) * 512], lambda kc, tc: [f"h{kc}.{tc}"], gcol, 2,
              lambda kc, tc: c.hn[:, kc, tc * 512:(tc + 1) * 512], lambda kc, tc: f"hn{kc}.{tc}", tmp_off)
    act = sb(96, 44 * 1024, BF16, "p (j t) -> p j t", j=22)
    sg = [sb(tmp_off + 4 + 2 * i, 2048, F32) for i in range(4)]
    gi = 0
    for half in range(2):
        for hb in range(22):
            sl_g, kg = wload(c, c.w_g[l, half * 22 + hb])
            sl_u, ku = wload(c, c.w_up[l, half * 22 + hb])
            for tc in range(2):
                bg = (gi % 2) * 4 + tc * 2
                bu = bg + 1
                rf = lambda kc, tc=tc: c.hn[:, kc, tc * 512:(tc + 1) * 512]
                rk = lambda kc, tc=tc: [f"hn{kc}.{tc}"]
                proj_fm(c, sl_g, kg, rf, rk, c.ps[bg], f"ps{bg}")
                proj_fm(c, sl_u, ku, rf, rk, c.ps[bu], f"ps{bu}")
                s = sg[(gi * 2 + tc) % 4]
                sk = f"sg{(gi * 2 + tc) % 4}"
                P.op("act", lambda e, s=s, bg=bg: e.activation(s, c.ps[bg], AF.Silu), reads=[f"ps{bg}"], writes=[sk])
                P.op("dve", lambda e, s=s, bu=bu, hb=hb, tc=tc: e.tensor_tensor(act[:, hb, tc * 512:(tc + 1) * 512], s, c.ps[bu], ALU.mult),
                     reads=[sk, f"ps{bu}"], writes=[f"act{hb}.{tc}"])
            gi += 1
        for db in range(16):
            wl = []
            for part in range(2):
                slab, sk_ = wload(c, c.w_dn[l, half, db, :, part * 11:(part + 1) * 11, :], nk=11)
                wl.append((slab, sk_))
            for tc in range(2):
                bank = (db * 2 + tc) % 4 if (gi % 2 == 0) else 4 + (db * 2 + tc) % 4
                ps, psk = c.ps[bank], f"ps{bank}"
                pairs = []
                reads = []
                for part in range(2):
                    slab, sk_ = wl[part]
                    reads += sk_
                    for jj in range(11):
                        j = part * 11 + jj
                        pairs.append((slab[:, jj, :], act[:, j, tc * 512:(tc + 1) * 512]))
                        reads.append(f"act{j}.{tc}")

                def fn(e, pairs=pairs, ps=ps):
                    ins = None
                    n = len(pairs)
                    for i, (lh, r) in enumerate(pairs):
                        ins = e.matmul(ps, lh, r, start=(i == 0), stop=(i == n - 1))
                    return ins
                P.op("pe", fn, reads=reads, writes=[psk])
                hv = c.h[:, db, tc * 512:(tc + 1) * 512]
                P.op("dve", lambda e, hv=hv, ps=ps: e.tensor_tensor(hv, hv, ps, ALU.add), reads=[psk, f"h{db}.{tc}"], writes=[f"h{db}.{tc}"])


def layer1_mixer(c):
    P, nc, sb = c.P, c.nc, c.sb
    P.barrier(lambda e: e.memset(c.lam[:, 2:3], 0.0))
    rms_to_hn(c, lambda kc, tc: c.h[:, kc, tc * 512:(tc + 1) * 512], lambda kc, tc: [f"h{kc}.{tc}"], V_MIXG1, 2,
              lambda kc, tc: c.hn[:, kc, tc * 512:(tc + 1) * 512], lambda kc, tc: f"hn{kc}.{tc}", 192)
    P.barrier(lambda e: e.memset(c.lam[:, 2:3], 0.0))
    y = sb(96, 32 * 1024, BF16, "p (k t) -> p k t", k=KC)
    A = [sb(128 + 4 * i, 4096, F32) for i in range(4)] + [sb(147 + 4 * i, 4096, F32) for i in range(3)]
    QB = sb(192, 2048, BF16)
    KB = sb(194, 2048, BF16)
    KBT = sb(196, 4096, BF16, "p (c k) -> p c k", c=16)
    VT = sb(200, 4096, BF16, "p (c k) -> p c k", c=16)
    SCT = sb(155, 2048, BF16, "p (c t) -> p c t", c=16)
    msk = sb(157, 1024, BF16)
    TB = [sb(158 + 0.5 * i, 512, F32) for i in range(4)]
    SBFR = [sb(153.5 + 0.25 * i, 256, BF16) for i in range(4)]
    S32 = sb(153, 512, F32)
    EBL = sb(146.5, 64, F32)
    LBV = sb(146.5625, 4 * 64, F32, "p (a h) -> p a h", a=4)
    IT = A[5]
    ITb = sb(147 + 4 * 1, 2048, BF16)
    P.op("dve", lambda e: e.memset(msk, 1.0), writes=["msk"])
    P.op("dve", lambda e: e.memset(msk.rearrange("p (c t) -> p c t", t=64)[:, :, 0:1], 0.0), reads=["msk"], writes=["msk"])
    for ld in range(2):
        r0c = V_LB + (ld * 2 + 0) * 16
        r1c = V_LB + (ld * 2 + 1) * 16
        P.op("dve", lambda e, ld=ld, r0c=r0c, r1c=r1c: e.tensor_tensor(LBV[:, 2 * ld, :], c.vecs[:, r1c:r1c + 16], c.vecs[:, r0c:r0c + 16], ALU.subtract),
             reads=["vecs", "LBV"], writes=["LBV"])
        P.op("act", lambda e, ld=ld: e.activation(LBV[:, 2 * ld, :], LBV[:, 2 * ld, :], AF.Sigmoid), reads=["LBV"], writes=["LBV"])
        P.op("dve", lambda e, ld=ld: e.tensor_scalar(LBV[:, 2 * ld + 1, :], LBV[:, 2 * ld, :], -1.0, 1.0, ALU.mult, ALU.add), reads=["LBV"], writes=["LBV"])

    def proj2(wslab, dst_fn, func, dkey, scale=None):
        slab, skeys = wload(c, wslab)
        for tc in range(2):
            bank = tc
            proj_fm(c, slab, skeys, lambda kc, tc=tc: c.hn[:, kc, tc * 512:(tc + 1) * 512], lambda kc, tc=tc: [f"hn{kc}.{tc}"], c.ps[bank], f"ps{bank}")
            P.op("act", lambda e, tc=tc, bank=bank: e.activation(dst_fn(tc), c.ps[bank], func), reads=[f"ps{bank}"], writes=[dkey])

    SQs = [A[0], sb(180, 4096, F32)]
    FVs = [A[1], sb(184, 4096, F32)]
    ITBs = [ITb, sb(188, 2048, BF16)]

    def head_P(hd, ld, bs):
        for wsel, dstT, func, key in ((0, SQs[bs], AF.Silu, f"A0.{bs}"), (1 + ld, FVs[bs], None, f"A1.{bs}"), (3, ITBs[bs], AF.Copy, f"A5.{bs}")):
            slab, skeys = wload(c, c.w_hin[wsel, hd])
            for tc in range(2):
                bank = tc
                proj_fm(c, slab, skeys, lambda kc, tc=tc: c.hn[:, kc, tc * 512:(tc + 1) * 512], lambda kc, tc=tc: [f"hn{kc}.{tc}"], c.ps[bank], f"ps{bank}")
                dv = dstT[:, tc * 512:(tc + 1) * 512]
                if func is None:
                    P.op("act", lambda e, dv=dv, bank=bank: e.activation(dv, c.ps[bank], AF.Exp, scale=-1.0), reads=[f"ps{bank}"], writes=[key])
                else:
                    P.op("act", lambda e, dv=dv, bank=bank, func=func: e.activation(dv, c.ps[bank], func), reads=[f"ps{bank}"], writes=[key])
            if func is None:
                P.op("act", lambda e, dstT=dstT: e.activation(dstT, dstT, AF.Ln, bias=c.epsb[:, 2:3], scale=1.0), reads=[key, "epsb"], writes=[key])
                P.op("act", lambda e, dstT=dstT: e.activation(dstT, dstT, AF.Exp, scale=-1.0), reads=[key], writes=[key])
            yield

    def head_pass(hd, ld, bs):
        lb = LBV[:, 2 * ld, hd:hd + 1]
        oml = LBV[:, 2 * ld + 1, hd:hd + 1]
        sq, fv, kk, lf, bb = SQs[bs], FVs[bs], A[2], A[3], A[4]
        ITb = ITBs[bs]
        kA0, kA1, kA5 = f"A0.{bs}", f"A1.{bs}", f"A5.{bs}"
        P.op("dve", lambda e: e.tensor_scalar(fv, fv, oml, lb, ALU.mult, ALU.add), reads=[kA1, "LBV"], writes=[kA1])
        P.op("dve", lambda e: e.tensor_scalar(kk, fv, -1.0, 1.0, ALU.mult, ALU.add), reads=[kA1], writes=["A2"])
        P.op("act", lambda e: e.activation(lf, fv, AF.Ln), reads=[kA1], writes=["A3"])
        for tc in range(2):
            P.op("dve", lambda e, tc=tc: e.tensor_tensor_scan(bb[:, tc * 512:(tc + 1) * 512], msk, lf[:, tc * 512:(tc + 1) * 512], 0.0, ALU.mult, ALU.add),
                 reads=["A3", "msk"], writes=["A4"])
        b3 = bb.rearrange("p (c t) -> p c t", t=64)
        if ld == 1:
            P.op("dve", lambda e: e.tensor_tensor(lf, lf, bb, ALU.subtract), reads=["A3", "A4"], writes=["A3"])
            P.op("dve", lambda e: e.tensor_tensor(b3, lf.rearrange("p (c t) -> p c t", t=64), b3[:, :, 63:64].to_broadcast([128, 16, 64]), ALU.add),
                 reads=["A3", "A4"], writes=["A4"])
        eb, enb = fv, A[3]
        P.op("act", lambda e: e.activation(eb, bb, AF.Exp), reads=["A4", kA1], writes=[kA1])
        P.op("act", lambda e: e.activation(enb, bb, AF.Exp, scale=-1.0), reads=["A4", "A3"], writes=["A3"])
        P.op("dve", lambda e: e.tensor_tensor(QB, sq, eb, ALU.mult), reads=[kA0, kA1], writes=["QB"])
        P.op("dve", lambda e: e.tensor_tensor(KB, kk, enb, ALU.mult), reads=["A2", "A3"], writes=["KB"])
        e3 = eb.rearrange("p (c t) -> p c t", t=64)
        edge = 63 if ld == 0 else 0
        P.op("dve", lambda e: e.tensor_copy(EBL, e3[:, :, edge]), reads=[kA1], writes=["EBL"])
        for src, skey, dst, dkey in ((ITb, kA5, VT, "VT"), (KB, "KB", KBT, "KBT")):
            for half in range(2):
                bank = 4 + half
                pst = c.ps[bank].bitcast(BF16).rearrange("p (c k) -> p c k", k=128)

                def fn(e, src=src, half=half, pst=pst):
                    ins = None
                    for cl in range(8):
                        ch = half * 8 + cl
                        ins = e.transpose(pst[0:64, cl, :], src[:, ch * 64:(ch + 1) * 64], c.ident)
                    return ins
                P.op("pe", fn, reads=[skey, "ident"], writes=[f"ps{bank}"])
                P.op("act" if half == 0 else "dve",
                     (lambda e, dst=dst, half=half, pst=pst: e.copy(dst[0:64, half * 8:(half + 1) * 8, :], pst[0:64, :, :])) if half == 0 else
                     (lambda e, dst=dst, half=half, pst=pst: e.tensor_copy(dst[0:64, half * 8:(half + 1) * 8, :], pst[0:64, :, :])),
                     reads=[f"ps{bank}"], writes=[dkey])
        msk2 = (c.tri if ld == 0 else c.triT)[0:64, 0:64]
        for half in range(2):
            bank = 6 + half
            psv = c.ps[bank].rearrange("p (c t) -> p c t", t=64)

            def fn(e, half=half, psv=psv):
                ins = None
                for cl in range(8):
                    ch = half * 8 + cl
                    ins = e.matmul(psv[0:64, cl, :], KB[:, ch * 64:(ch + 1) * 64], QB[:, ch * 64:(ch + 1) * 64], start=True, stop=True)
                return ins
            P.op("pe", fn, reads=["KB", "QB"], writes=[f"ps{bank}"])
            P.op("dve", lambda e, half=half, psv=psv: e.tensor_tensor(SCT[0:64, half * 8:(half + 1) * 8, :], psv[0:64, :, :],
                                                                     msk2.unsqueeze(1).to_broadcast([64, 8, 64]), ALU.mult),
                 reads=[f"ps{bank}", "tri", "triT"], writes=["SCT"])
        order = list(range(16)) if ld == 0 else list(range(15, -1, -1))
        for q4 in range(4):
            ubank = 4 + q4

            def fu(e, q4=q4, ubank=ubank):
                ins = None
                for i4 in range(4):
                    ch = order[q4 * 4 + i4]
                    ins = e.matmul(c.ps[ubank][:, i4 * 128:(i4 + 1) * 128], KBT[0:64, ch, :], VT[0:64, ch, :], start=True, stop=True)
                return ins
            P.op("pe", fu, reads=["KBT", "VT"], writes=[f"ps{ubank}"])
        if ld == 0:
            P.op("dve", lambda e: e.memset(TB[3], 0.0), writes=["TB3"])
        else:
            P.op("sp", lambda e: e.dma_start(out=TB[3], in_=c.cout[hd * 128:(hd + 1) * 128, :]), reads=["cout"], writes=["TB3"], dma_key="st0")
            P.op("sp", lambda e: e.dma_start(out=TB[2], in_=c.cout[2048 + hd * 128:2048 + (hd + 1) * 128, :]), reads=["cout"], writes=["TB2"], dma_key="st1")
            P.op("dve", lambda e: e.tensor_scalar(TB[3], TB[3], c.vecs[:, V_SEL0:V_SEL0 + 1], None, ALU.mult), reads=["TB3", "vecs"], writes=["TB3"])
            P.op("dve", lambda e: e.scalar_tensor_tensor(TB[3], TB[2], c.vecs[:, V_SEL1:V_SEL1 + 1], TB[3], ALU.mult, ALU.add), reads=["TB3", "TB2", "vecs"], writes=["TB3"])
        P.op("act", lambda e: e.copy(SBFR[3], TB[3]), reads=["TB3"], writes=["SBF3"])
        for n, ch in enumerate(order):
            un = c.ps[4 + n // 4][:, (n % 4) * 128:(n % 4 + 1) * 128]
            prev = (n - 1) % 4
            cur = n % 4
            if n == 0:
                P.op("dve", lambda e, un=un, prev=prev, cur=cur: e.tensor_tensor(TB[cur], TB[prev], un, ALU.add),
                     reads=[f"TB{prev}", f"ps{4 + n // 4}"], writes=[f"TB{cur}"])
            else:
                ep = EBL[:, order[n - 1]:order[n - 1] + 1]
                P.op("dve", lambda e, un=un, prev=prev, cur=cur, ep=ep: e.scalar_tensor_tensor(TB[cur], TB[prev], ep, un, ALU.mult, ALU.add),
                     reads=[f"TB{prev}", f"ps{4 + n // 4}", "EBL"], writes=[f"TB{cur}"])
            obank = 2 + (ch // 8)
            pso = c.ps[obank].rearrange("p (c t) -> p c t", t=64)[:, ch % 8, :]

            def fo(e, ch=ch, pso=pso, prev=prev):
                e.matmul(pso, VT[0:64, ch, :], SCT[0:64, ch, :], start=True, stop=False)
                return e.matmul(pso, SBFR[prev], QB[:, ch * 64:(ch + 1) * 64], start=False, stop=True)
            P.op("pe", fo, reads=["VT", "SCT", f"SBF{prev}", "QB"], writes=[f"ps{obank}"])
            ec = EBL[:, ch:ch + 1]
            if n < 15:
                P.op("act", lambda e, cur=cur, ec=ec: e.activation(SBFR[cur], TB[cur], AF.Copy, scale=ec), reads=[f"TB{cur}", "EBL"], writes=[f"SBF{cur}"])
            else:
                P.op("act", lambda e, cur=cur, ec=ec: e.activation(S32, TB[cur], AF.Copy, scale=ec), reads=[f"TB{cur}", "EBL"], writes=["S32"])
            if (n % 8) == 7:
                t0 = (ch // 8) * 512
                yv = y[:, hd, t0:t0 + 512]
                if ld == 0:
                    P.op("act", lambda e, yv=yv, obank=obank: e.copy(yv, c.ps[obank]), reads=[f"ps{obank}"], writes=[f"y{hd}.{ch // 8}"])
                else:
                    P.op("dve", lambda e, yv=yv, obank=obank: e.tensor_tensor(yv, yv, c.ps[obank], ALU.add), reads=[f"ps{obank}", f"y{hd}.{ch // 8}"], writes=[f"y{hd}.{ch // 8}"])
        if ld == 0:
            P.op("sp", lambda e: e.dma_start(out=c.cin[hd * 128:(hd + 1) * 128, :], in_=S32), reads=["S32"], writes=["cin"], dma_key=f"ci{hd % 4}")

    c.nring = 5
    seq = [(hd, 0) for hd in range(16)] + [(hd, 1) for hd in range(16)]
    gens = [head_P(hd, ld, i % 2) for i, (hd, ld) in enumerate(seq)]
    for _ in gens[0]:
        pass
    for i, (hd, ld) in enumerate(seq):
        g = gens[i + 1] if i + 1 < len(seq) else None
        cnt = [0]

        def hook(g=g, cnt=cnt):
            cnt[0] += 1
            if g is not None and cnt[0] % 4 == 0:
                P.in_hook = True
                try:
                    next(g)
                except StopIteration:
                    pass
                P.in_hook = False
        P.hook = hook
        head_pass(hd, ld, i % 2)
        P.hook = None
        if g is not None:
            for _ in g:
                pass
        if i == 15:
            P.op("pool", lambda e: e.collective_compute("AllGather", ALU.bypass, [[0, 1], [2, 3], [4, 5], [6, 7]], [c.cin.opt()], [c.cout.opt()]),
                 reads=["cin"], writes=["cout"], dma_key="cc", dma_inc=1)
    c.nring = 8

    P.barrier(lambda e: e.memset(c.lam[:, 2:3], 0.0))
    sqy = [sb(192 + i, 1024, BF16) for i in range(2)]
    rstd2 = A[0]
    for tc in range(2):
        ps, psk = c.ps[6 + tc], f"ps{6 + tc}"
        for hd in range(16):
            s_ = sqy[hd % 2]
            sk = f"sqy{hd % 2}"
            P.op("act", lambda e, s_=s_, hd=hd, tc=tc: e.activation(s_, y[:, hd, tc * 512:(tc + 1) * 512], AF.Square), reads=[f"y{hd}.{tc}"], writes=[sk])
            P.op("pe", lambda e, s_=s_, hd=hd, ps=ps: e.matmul(ps, c.ones, s_, start=(hd == 0), stop=(hd == 15)), reads=[sk, "ones"], writes=[psk])
        rv = rstd2[:, tc * 512:(tc + 1) * 512]
        P.op("act", lambda e, ps=ps, rv=rv: e.activation(rv, ps, AF.Ln, bias=c.epsb[:, 0:1], scale=1.0 / D), reads=[psk, "epsb"], writes=["A0"])
        P.op("act", lambda e, rv=rv: e.activation(rv, rv, AF.Exp, scale=-0.5), reads=["A0"], writes=["A0"])
    sgt = [sb(128 + 4 + 2 * i, 2048, F32) for i in range(2)]
    ytmp = sb(128 + 8, 2048, F32)
    for hd in range(16):
        slab, skeys = wload(c, c.w_hin[4, hd])
        for tc in range(2):
            bank = tc
            proj_fm(c, slab, skeys, lambda kc, tc=tc: c.hn[:, kc, tc * 512:(tc + 1) * 512], lambda kc, tc=tc: [f"hn{kc}.{tc}"], c.ps[bank], f"ps{bank}")
            sg_ = sgt[tc]
            P.op("act", lambda e, sg_=sg_, bank=bank: e.activation(sg_, c.ps[bank], AF.Silu), reads=[f"ps{bank}"], writes=[f"sgt{tc}"])
            yv = y[:, hd, tc * 512:(tc + 1) * 512]
            P.op("dve", lambda e, yv=yv, hd=hd, tc=tc: e.scalar_tensor_tensor(ytmp, yv, c.vecs[:, V_GNORM + hd:V_GNORM + hd + 1], rstd2[:, tc * 512:(tc + 1) * 512], ALU.mult, ALU.mult),
                 reads=[f"y{hd}.{tc}", "A0", "vecs"], writes=["ytmp"])
            P.op("dve", lambda e, yv=yv, sg_=sg_: e.tensor_tensor(yv, ytmp, sg_, ALU.mult), reads=["ytmp", f"sgt{tc}"], writes=[f"y{hd}.{tc}"])
    for db in range(16):
        slab, skeys = wload(c, c.w_ho[db])
        for tc in range(2):
            bank = 2 + (db * 2 + tc) % 4
            proj_fm(c, slab, skeys, lambda kc, tc=tc: y[:, kc, tc * 512:(tc + 1) * 512], lambda kc, tc=tc: [f"y{kc}.{tc}"], c.ps[bank], f"ps{bank}")
            hv = c.h[:, db, tc * 512:(tc + 1) * 512]
            P.op("dve", lambda e, hv=hv, bank=bank: e.tensor_tensor(hv, hv, c.ps[bank], ALU.add), reads=[f"ps{bank}", f"h{db}.{tc}"], writes=[f"h{db}.{tc}"])


def final_norm(c):
    P, sb = c.P, c.sb
    P.barrier(lambda e: e.memset(c.lam[:, 2:3], 0.0))
    rms_to_hn(c, lambda kc, tc: c.h[:, kc, tc * 512:(tc + 1) * 512], lambda kc, tc: [f"h{kc}.{tc}"], V_FING, 2,
              lambda kc, tc: c.h[:, kc, tc * 512:(tc + 1) * 512], lambda kc, tc: f"h{kc}.{tc}", 192)
    for kc in range(KC):
        P.op("sp", lambda e, kc=kc: e.dma_start(out=c.yT[:, kc, :], in_=c.h[:, kc, :]),
             reads=[f"h{kc}.0", f"h{kc}.1"], writes=[f"yo{kc}"], dma_key=f"out{kc % 8}")


def host_prep(inp):
    f32 = np.float32
    x = inp["x"]
    shared = {}
    W = inp["even_w_in"][0]
    shared["w_q"] = slabify(W[:, 0:1024], 128)
    shared["w_k"] = slabify(W[:, 1024:2048], 128)
    shared["w_v"] = slabify(W[:, 2048:3072], 512)
    shared["w_u"] = slabify(W[:, 3072:4096], 128)
    shared["w_vb"] = slabify(W[:, 4096:5120], 512)
    shared["w_eo"] = slabify(inp["even_w_out"][0], 128)
    shared["w_g"] = np.stack([slabify(inp["ffn_w_gate"][l], 128) for l in range(2)])
    shared["w_up"] = np.stack([slabify(inp["ffn_w_up"][l], 128) for l in range(2)])
    wd = inp["ffn_w_down"]
    shared["w_dn"] = np.ascontiguousarray(wd.reshape(2, 2, 22, 128, 16, 128).transpose(0, 1, 4, 3, 2, 5))
    shared["w_ho"] = slabify(inp["hgrn_w_out"][0], 128)
    Wh = inp["hgrn_w_in"][0]
    hin = [slabify(Wh[:, i * 2048:(i + 1) * 2048], 128) for i in range(5)]
    hin_even = np.stack([hin[0], hin[1], hin[2], hin[3], hin[4]])
    hin_odd = np.stack([hin[0], hin[2], hin[1], hin[3], hin[4]])
    consts = np.zeros((128, 4, 128), f32)
    consts[:, 0, :] = np.eye(128, dtype=f32)
    consts[:, 1, :] = 1.0
    pm = np.zeros((128, 128), f32)
    for base in (0, 64):
        for d in range(8):
            pm[base + d + 8, base + d] = 1.0
            pm[base + d, base + d + 8] = 1.0
    consts[:, 2, :] = pm
    consts[:, 3, :] = np.triu(np.ones((128, 128), f32))
    half = 8
    invf_vals = (500000.0 ** (-np.arange(half, dtype=np.float64) / half)).astype(f32)
    invf = np.zeros(128, f32)
    sgn = np.zeros(128, f32)
    for base in (0, 64):
        invf[base:base + 8] = invf_vals
        invf[base + 8:base + 16] = invf_vals
        sgn[base:base + 8] = -1.0
        sgn[base + 8:base + 16] = 1.0
    lbraw = inp["hgrn_lower_bounds"]
    in_maps = []
    for core in range(NCORES):
        b, hf = core // 2, core % 2
        own = np.arange(T) if hf == 0 else (TA - 1 - np.arange(T))
        oth = (TA - 1 - np.arange(T)) if hf == 0 else np.arange(T)
        tok = np.concatenate([own, oth])
        xT = np.ascontiguousarray(x[b][tok, :].T.reshape(KC, 128, TA).transpose(1, 0, 2))
        vecs = np.zeros((128, NV), f32)
        vecs[:, V_MIXG0:V_MIXG0 + 16] = fm_vec(inp["mix_norm"][0])
        vecs[:, V_FFNG0:V_FFNG0 + 16] = fm_vec(inp["ffn_norm"][0])
        vecs[:, V_MIXG1:V_MIXG1 + 16] = fm_vec(inp["mix_norm"][1])
        vecs[:, V_FFNG1:V_FFNG1 + 16] = fm_vec(inp["ffn_norm"][1])
        vecs[:, V_FING:V_FING + 16] = fm_vec(inp["final_norm"])
        vecs[:, V_INVF] = invf
        vecs[:, V_SGN] = sgn
        vecs[:, V_SUBLN] = inp["diff_subln"][0]
        vecs[:, V_SEL0] = 1.0 if hf == 1 else 0.0
        vecs[:, V_SEL1] = 1.0 if hf == 0 else 0.0
        dirs = (0, 1) if hf == 0 else (1, 0)
        for ld in range(2):
            for layer in range(2):
                vecs[:, V_LB + (ld * 2 + layer) * 16:V_LB + (ld * 2 + layer + 1) * 16] = fm_vec(lbraw[dirs[ld], layer])
        vecs[:, V_GNORM:V_GNORM + 16] = fm_vec(inp["hgrn_g_norm"][0])
        rows = np.zeros((1, NR), f32)
        rows[0, R_LNG:R_LNG + 1024] = inp["gmlp_ln_g"][0]
        rows[0, R_LNB:R_LNB + 1024] = inp["gmlp_ln_b"][0]
        ws = inp["gmlp_w_s"][0]
        bs = inp["gmlp_b_s"][0]
        if hf == 1:
            ws = ws[:, ::-1, ::-1]
            bs = bs[:, ::-1]
        rows[0, R_BSB:R_BSB + 1024] = bs.reshape(-1)
        rows[0, R_LQ:R_LQ + 256] = np.concatenate([inp["diff_lq1"][0], inp["diff_lk1"][0], inp["diff_lq2"][0], inp["diff_lk2"][0]])
        m = dict(shared)
        m["xT"] = xT
        m["pos"] = tok.astype(f32).reshape(1, TA)
        m["vecs"] = vecs
        m["rows"] = rows
        m["consts"] = consts
        m["w_s"] = np.ascontiguousarray(ws.transpose(2, 0, 1))
        m["w_hin"] = hin_even if hf == 0 else hin_odd
        in_maps.append(m)
    return in_maps


def assemble(results):
    out = np.zeros((4, TA, D), np.float32)
    for core in range(NCORES):
        b, hf = core // 2, core % 2
        yT = results[core]["yT"]
        y = yT.transpose(2, 1, 0).reshape(T, D)
        if hf == 0:
            out[b, 0:T] = y
        else:
            out[b, T:TA] = y[::-1]
    return out


def kernel(**inputs):
    inp = {k: np.asarray(v) for k, v in inputs.items()}
    in_maps = host_prep(inp)
    nc = build("full")
    res = run_bass_kernel_spmd(nc, in_maps, core_ids=list(range(NCORES)))
    return assemble(res.results)
```
